# Optimizing a Trainium2 kernel written in Bass

```python
import jax, jax.numpy as jnp
from jax import lax
import numpy as np

D_MODEL = 2048
BATCH = 4
SEQ = 2048
DEPTH = 2
DEC_BATCH = 8
DEC_SEQ = 32
PAST_LEN = 4096

CHUNK = 64
Q_BLOCK = 128
MLA_HEADS = 16
QK_NOPE = 128
QK_ROPE = 64
QK_HEAD = QK_NOPE + QK_ROPE
V_HEAD = 128
Q_LORA = 512
KV_LORA = 512
ROPE_THETA = 10000.0
ATTN_SCALE = QK_HEAD ** -0.5
HG_HEADS = 16
HG_DK = 128
HG_DV = D_MODEL // HG_HEADS
HG_WIDTH_K = HG_HEADS * HG_DK
HG_WIDTH_V = HG_HEADS * HG_DV
D_FF = 4 * D_MODEL
EPS = 1e-6
SPLIT_SIZES = (Q_LORA, KV_LORA, QK_ROPE, HG_WIDTH_K, HG_WIDTH_K, HG_WIDTH_V, HG_WIDTH_V, D_MODEL, D_MODEL)
D_IN = Q_LORA + KV_LORA + QK_ROPE + 2 * HG_WIDTH_K + 2 * HG_WIDTH_V + 2 * D_MODEL

kernel_name = "mla_hgrn2_gated_streaming_encoder"


def rmsnorm(x, g):
    xf = x.astype(jnp.float32)
    y = xf * lax.rsqrt(jnp.mean(xf * xf, axis=-1, keepdims=True) + EPS)
    return (y * g.astype(jnp.float32)).astype(x.dtype)


def rope(x, pos):
    half = QK_ROPE // 2
    inv = ROPE_THETA ** (-jnp.arange(half, dtype=jnp.float32) / half)
    ang = pos.astype(jnp.float32)[:, None] * inv[None, :]
    cos = jnp.cos(ang)[None, :, None, :]
    sin = jnp.sin(ang)[None, :, None, :]
    xf = x.astype(jnp.float32)
    x1, x2 = xf[..., :half], xf[..., half:]
    return jnp.concatenate([x1 * cos - x2 * sin, x1 * sin + x2 * cos], axis=-1).astype(x.dtype)


def split_in(z):
    idx = np.cumsum(np.array(SPLIT_SIZES))[:-1].tolist()
    return jnp.split(z, idx, axis=-1)


def mla_core(q_abs, q_pe, ckv, kpe, mask):
    s = (jnp.einsum('bshc,btc->bhst', q_abs, ckv, preferred_element_type=jnp.float32)
         + jnp.einsum('bshr,btr->bhst', q_pe, kpe, preferred_element_type=jnp.float32)) * ATTN_SCALE
    if mask is not None:
        s = jnp.where(mask[None, None], s, -jnp.inf)
    p = jax.nn.softmax(s, axis=-1).astype(ckv.dtype)
    return jnp.einsum('bhst,btc->bshc', p, ckv)


def mla_prompt(q_abs, q_pe, ckv, kpe):
    B, S, H, C = q_abs.shape
    nb = S // Q_BLOCK
    key_chunk = jnp.arange(S, dtype=jnp.int32) // CHUNK

    def block(args):
        j, qa, qp = args
        q_chunk = (j * Q_BLOCK + jnp.arange(Q_BLOCK, dtype=jnp.int32)) // CHUNK
        mask = key_chunk[None, :] <= q_chunk[:, None]
        return mla_core(qa, qp, ckv, kpe, mask)

    qa = q_abs.reshape(B, nb, Q_BLOCK, H, C).transpose(1, 0, 2, 3, 4)
    qp = q_pe.reshape(B, nb, Q_BLOCK, H, QK_ROPE).transpose(1, 0, 2, 3, 4)
    o = lax.map(block, (jnp.arange(nb, dtype=jnp.int32), qa, qp))
    return o.transpose(1, 0, 2, 3, 4).reshape(B, S, H, C)


def hgrn_inputs(hq, hf, hi, lb):
    B, S, _ = hq.shape
    z = hf.astype(jnp.float32)
    logf = jnp.logaddexp(jnp.log(lb), jnp.log1p(-lb) + jax.nn.log_sigmoid(z))
    k = (1.0 - lb) * jax.nn.sigmoid(-z)
    q = jax.nn.silu(hq.astype(jnp.float32))
    v = hi.astype(jnp.float32)

    def heads(t, d):
        return t.reshape(B, S, HG_HEADS, d).transpose(0, 2, 1, 3)
    return heads(q, HG_DK), heads(k, HG_DK), heads(v, HG_DV), heads(logf, HG_DK)


def hgrn_chunk(S0, q, k, v, logf):
    L = q.shape[2]
    b = jnp.cumsum(logf, axis=2)
    causal = jnp.tril(jnp.ones((L, L), dtype=bool))
    diff = b[:, :, :, None, :] - b[:, :, None, :, :]
    decay = jnp.exp(jnp.where(causal[None, None, :, :, None], diff, -jnp.inf))
    attn = jnp.einsum('bhtk,bhsk,bhtsk->bhts', q, k, decay)
    o = (jnp.einsum('bhts,bhsv->bhtv', attn, v)
         + jnp.einsum('bhtk,bhkv->bhtv', q * jnp.exp(b), S0))
    b_last = b[:, :, -1:, :]
    S_new = (jnp.exp(b_last[:, :, 0, :])[..., None] * S0
             + jnp.einsum('bhsk,bhsv->bhkv', k * jnp.exp(b_last - b), v))
    return o, S_new


def hgrn_prompt(q, k, v, logf):
    B, H, S, _ = q.shape
    nc = S // CHUNK

    def to_chunks(t):
        return t.reshape(B, H, nc, CHUNK, t.shape[-1]).transpose(2, 0, 1, 3, 4)

    def step(S_c, xs):
        o, S_n = hgrn_chunk(S_c, *xs)
        return S_n, o

    S0 = jnp.zeros((B, H, HG_DK, HG_DV), jnp.float32)
    S_fin, o = lax.scan(step, S0, (to_chunks(q), to_chunks(k), to_chunks(v), to_chunks(logf)))
    return o.transpose(1, 2, 0, 3, 4).reshape(B, H, S, HG_DV), S_fin


def layer_forward(x, pos, p, lb, cache_ckv, cache_kpe, state):
    B, S, _ = x.shape
    h = rmsnorm(x, p['pre_mix_g'])
    q_lat, kv_lat, k_pe, hq, hf, hi, hg, ga, gb = split_in(h @ p['w_in'])
    q = (rmsnorm(q_lat, p['q_norm_g']) @ p['w_uq']).reshape(B, S, MLA_HEADS, QK_HEAD)
    q_pe = rope(q[..., QK_NOPE:], pos)
    q_abs = jnp.einsum('bshn,chn->bshc', q[..., :QK_NOPE], p['w_uk'])
    c_kv = rmsnorm(kv_lat, p['kv_norm_g'])
    k_rot = rope(k_pe[:, :, None, :], pos)[:, :, 0, :]
    if cache_ckv is None:
        o_lat = mla_prompt(q_abs, q_pe, c_kv, k_rot)
    else:
        o_lat = mla_core(q_abs, q_pe, jnp.concatenate([cache_ckv, c_kv], axis=1),
                         jnp.concatenate([cache_kpe, k_rot], axis=1), None)
    o_a = jnp.einsum('bshc,chv->bshv', o_lat, p['w_uv']).reshape(B, S, MLA_HEADS * V_HEAD)
    branch_a = o_a @ p['w_oa']
    qh, kh, vh, lfh = hgrn_inputs(hq, hf, hi, lb)
    if state is None:
        o_h, S_fin = hgrn_prompt(qh, kh, vh, lfh)
    else:
        o_h, S_fin = hgrn_chunk(state.astype(jnp.float32), qh, kh, vh, lfh)
    o_h = rmsnorm(o_h.transpose(0, 2, 1, 3).astype(x.dtype), p['hg_norm_g'])
    o_h = o_h * jax.nn.silu(hg.reshape(B, S, HG_HEADS, HG_DV))
    branch_b = o_h.reshape(B, S, HG_WIDTH_V) @ p['w_ob']
    mix = (jax.nn.sigmoid(ga) * branch_a + jax.nn.sigmoid(gb) * branch_b) @ p['w_out']
    x = x + rmsnorm(mix, p['post_mix_g'])
    h2 = rmsnorm(x, p['pre_mlp_g'])
    m = jnp.square(jax.nn.relu(h2 @ p['w_up'])) @ p['w_down']
    x = x + rmsnorm(m, p['post_mlp_g'])
    return x, c_kv, k_rot, S_fin.astype(x.dtype)


def setup_inputs(seed: int = 0) -> dict:
    key = jax.random.key(seed)
    ks = jax.random.split(key, 24)
    f32 = jnp.float32

    def nrm(k, shape, scale):
        return jax.random.normal(k, shape, f32) * scale

    def gain(k, n):
        return 1.0 + 0.01 * jax.random.normal(k, (DEPTH, n), f32)

    return {
        "x_prompt": nrm(ks[0], (BATCH, SEQ, D_MODEL), 1.0),
        "x_sample": nrm(ks[1], (DEC_BATCH, DEC_SEQ, D_MODEL), 1.0),
        "cache_ckv": nrm(ks[2], (DEPTH, DEC_BATCH, PAST_LEN, KV_LORA), 1.0),
        "cache_kpe": nrm(ks[3], (DEPTH, DEC_BATCH, PAST_LEN, QK_ROPE), 1.0),
        "state_hgrn": nrm(ks[4], (DEPTH, DEC_BATCH, HG_HEADS, HG_DK, HG_DV), 0.5),
        "pre_mix_g": gain(ks[5], D_MODEL),
        "w_in": nrm(ks[6], (DEPTH, D_MODEL, D_IN), D_MODEL ** -0.5),
        "q_norm_g": gain(ks[7], Q_LORA),
        "w_uq": nrm(ks[8], (DEPTH, Q_LORA, MLA_HEADS * QK_HEAD), Q_LORA ** -0.5),
        "kv_norm_g": gain(ks[9], KV_LORA),
        "w_uk": nrm(ks[10], (DEPTH, KV_LORA, MLA_HEADS, QK_NOPE), KV_LORA ** -0.5),
        "w_uv": nrm(ks[11], (DEPTH, KV_LORA, MLA_HEADS, V_HEAD), KV_LORA ** -0.5),
        "w_oa": nrm(ks[12], (DEPTH, MLA_HEADS * V_HEAD, D_MODEL), (MLA_HEADS * V_HEAD) ** -0.5),
        "hg_lb": nrm(ks[13], (DEPTH, HG_WIDTH_K), 1.0),
        "hg_norm_g": gain(ks[14], HG_DV),
        "w_ob": nrm(ks[15], (DEPTH, HG_WIDTH_V, D_MODEL), HG_WIDTH_V ** -0.5),
        "w_out": nrm(ks[16], (DEPTH, D_MODEL, D_MODEL), D_MODEL ** -0.5),
        "post_mix_g": gain(ks[17], D_MODEL),
        "pre_mlp_g": gain(ks[18], D_MODEL),
        "w_up": nrm(ks[19], (DEPTH, D_MODEL, D_FF), D_MODEL ** -0.5),
        "w_down": nrm(ks[20], (DEPTH, D_FF, D_MODEL), D_FF ** -0.5),
        "post_mlp_g": gain(ks[21], D_MODEL),
    }


def reference(x_prompt, x_sample, cache_ckv, cache_kpe, state_hgrn,
              pre_mix_g, w_in, q_norm_g, w_uq, kv_norm_g, w_uk, w_uv, w_oa,
              hg_lb, hg_norm_g, w_ob, w_out, post_mix_g, pre_mlp_g, w_up, w_down, post_mlp_g):
    cs = jnp.cumsum(jax.nn.softmax(hg_lb.astype(jnp.float32), axis=0), axis=0)
    lbs = cs - cs[0:1]
    past = cache_ckv.shape[2]
    pos_p = jnp.arange(x_prompt.shape[1], dtype=jnp.int32)
    pos_s = past + jnp.arange(x_sample.shape[1], dtype=jnp.int32)
    yp, ys = x_prompt, x_sample
    ckv_p, kpe_p, st_p, ckv_s, kpe_s, st_s = [], [], [], [], [], []
    for l in range(DEPTH):
        p = dict(pre_mix_g=pre_mix_g[l], w_in=w_in[l], q_norm_g=q_norm_g[l], w_uq=w_uq[l],
                 kv_norm_g=kv_norm_g[l], w_uk=w_uk[l], w_uv=w_uv[l], w_oa=w_oa[l],
                 hg_norm_g=hg_norm_g[l], w_ob=w_ob[l], w_out=w_out[l], post_mix_g=post_mix_g[l],
                 pre_mlp_g=pre_mlp_g[l], w_up=w_up[l], w_down=w_down[l], post_mlp_g=post_mlp_g[l])
        yp, c1, k1, s1 = layer_forward(yp, pos_p, p, lbs[l], None, None, None)
        ys, c2, k2, s2 = layer_forward(ys, pos_s, p, lbs[l], cache_ckv[l], cache_kpe[l], state_hgrn[l])
        ckv_p.append(c1); kpe_p.append(k1); st_p.append(s1)
        ckv_s.append(c2); kpe_s.append(k2); st_s.append(s2)
    return (yp, ys,
            jnp.stack(ckv_p), jnp.stack(kpe_p), jnp.stack(st_p),
            jnp.stack(ckv_s), jnp.stack(kpe_s), jnp.stack(st_s))
```

```python
import contextlib
import numpy as np
import concourse.bass as bass
import concourse.mybir as mybir
from concourse.bass_utils import run_bass_kernel_spmd

F32 = mybir.dt.float32
BF16 = mybir.dt.bfloat16
ALU = mybir.AluOpType
AF = mybir.ActivationFunctionType

PE, ACT, DVE, POOL, SP = "tensor", "scalar", "vector", "gpsimd", "sync"
ENGINES = (PE, ACT, DVE, POOL, SP)
STRICT_SAME_ENGINE = True

D_MODEL = 2048
SEQ = 2048
DEPTH = 2
DEC_SEQ = 32
PAST = 4096
NH = 16
QK_NOPE = 128
QK_ROPE = 64
KV_LORA = 512
Q_LORA = 512
D_FF = 8192
EPS = 1e-6
ATTN_SCALE = float((QK_NOPE + QK_ROPE) ** -0.5)
D_IN = 13376
C_Q, C_KV, C_KPE, C_HQ, C_HF, C_HI, C_HG, C_GA, C_GB = 0, 512, 1024, 1088, 3136, 5184, 7232, 9280, 11328
C_KPESW = D_IN
D_IN_EXT = D_IN + 64
TT = 512
NKC = D_MODEL // 128
G_PRE, G_POST, G_PREM, G_POSTM, G_QN, G_KVN, G_HGN, NG = 0, 16, 32, 48, 64, 68, 72, 73


class Res:
    __slots__ = ("name", "last_w", "readers", "aliases")

    def __init__(self, name):
        self.name = name
        self.last_w = None
        self.readers = []
        self.aliases = ()


class DmaSem:
    def __init__(self, handle):
        self.handle = handle
        self.total = 0


class Instr:
    __slots__ = ("eng", "fn", "deps", "signal", "count", "dsem", "dval", "is_dma")

    def __init__(self, eng, fn):
        self.eng = eng
        self.fn = fn
        self.deps = []
        self.signal = False
        self.count = 0
        self.dsem = None
        self.dval = 0
        self.is_dma = False


class Prog:
    def __init__(self, nc):
        self.nc = nc
        self.q = {e: [] for e in ENGINES}
        self.n_instr = 0

    def _track(self, ins, reads, writes):
        eng = ins.eng
        deps = ins.deps
        for r in reads:
            w = r.last_w
            if w is not None and not (w.eng == PE and eng == PE and not w.is_dma and not ins.is_dma):
                deps.append(w)
            r.readers.append(ins)
        same_ok = STRICT_SAME_ENGINE and eng != PE
        for r0 in writes:
            for r in (r0,) + tuple(r0.aliases):
                w = r.last_w
                if w is not None and w is not ins and (w.is_dma or ins.is_dma or w.eng != eng or same_ok):
                    deps.append(w)
                for rd in r.readers:
                    if rd is ins:
                        continue
                    if rd.is_dma or ins.is_dma or rd.eng != eng or same_ok:
                        deps.append(rd)
            r0.last_w = ins
            r0.readers = []

    def op(self, eng, fn, reads=(), writes=()):
        ins = Instr(eng, fn)
        self._track(ins, reads, writes)
        self.q[eng].append(ins)
        self.n_instr += 1
        return ins

    def dma(self, eng, out, in_, dsem, reads=(), writes=(), **kw):
        def fn(e, out=out, in_=in_, kw=kw):
            return e.dma_start(out=out, in_=in_, **kw)
        ins = Instr(eng, fn)
        ins.is_dma = True
        dsem.total += 16
        ins.dsem = dsem
        ins.dval = dsem.total
        self._track(ins, reads, writes)
        self.q[eng].append(ins)
        self.n_instr += 1
        return ins

    def wait_all(self, eng, toks):
        ins = Instr(eng, None)
        ins.deps = list(toks)
        self.q[eng].append(ins)
        return ins

    def emit(self, esems):
        for e in ENGINES:
            for ins in self.q[e]:
                for d in ins.deps:
                    if not d.is_dma:
                        d.signal = True
        for e in ENGINES:
            c = 0
            for ins in self.q[e]:
                if ins.signal and not ins.is_dma:
                    c += 1
                    ins.count = c
        stats = {}
        nc = self.nc
        with nc.Block() as block:
            for e in ENGINES:
                lst = self.q[e]
                if not lst:
                    continue

                def body(engine, lst=lst, e=e):
                    seen = {}
                    nw = 0
                    for ins in lst:
                        need = {}
                        for d in ins.deps:
                            if d.is_dma:
                                key, val, h = id(d.dsem), d.dval, d.dsem.handle
                            else:
                                key, val, h = d.eng, d.count, esems[d.eng]
                            if seen.get(key, 0) >= val:
                                continue
                            if key not in need or need[key][1] < val:
                                need[key] = (h, val)
                        for key, (h, val) in need.items():
                            engine.wait_ge(h, val)
                            seen[key] = val
                            nw += 1
                        if ins.fn is None:
                            continue
                        bi = ins.fn(engine)
                        if ins.is_dma:
                            bi.then_inc(ins.dsem.handle, 16)
                        elif ins.signal:
                            bi.then_inc(esems[e], 1)
                    stats[e] = (len(lst), nw)

                getattr(block, e)(body)
        return stats


class Buf:
    __slots__ = ("t", "r", "rc")

    def __init__(self, t, r):
        self.t = t
        self.r = r
        self.rc = None

    def split(self, n):
        self.rc = [Res(f"{self.r.name}.{i}") for i in range(n)]
        return self

    def ck(self, k):
        return self.rc[k] if self.rc is not None else self.r

    def all(self):
        return list(self.rc) if self.rc is not None else [self.r]


def build_program(n_tiles=4, n_layers=DEPTH, nslot=2, with_sample=True, debug=False):
    nc = bass.Bass("TRN2", target_bir_lowering=False)
    P = Prog(nc)

    def din(name, shape, dt=F32):
        return nc.dram_tensor(name, list(shape), dt, kind="ExternalInput").ap()

    def dout(name, shape, dt=F32):
        return nc.dram_tensor(name, list(shape), dt, kind="ExternalOutput").ap()

    xT_p = din("xT_p", [D_MODEL, SEQ])
    rope_p = din("rope_p", [2, 64, SEQ])
    cm1 = din("cm1", [128, 132])
    cm2 = din("cm2", [128, 128])
    cmask = din("cmask", [128, 128])
    cident = din("cident", [128, 128])
    gains_d = din("gains", [DEPTH, 128, NG])
    hg_lb = din("hg_lb", [DEPTH, D_MODEL])
    w_in = din("w_in", [DEPTH, D_MODEL, D_IN_EXT])
    w_uq = din("w_uq", [DEPTH, Q_LORA, 4096])
    w_uk = din("w_uk", [DEPTH, KV_LORA, 2048])
    w_uv = din("w_uv", [DEPTH, KV_LORA, 2048])
    w_oa = din("w_oa", [DEPTH, D_MODEL, D_MODEL])
    w_ob = din("w_ob", [DEPTH, D_MODEL, D_MODEL])
    w_out = din("w_out", [DEPTH, D_MODEL, D_MODEL])
    w_up = din("w_up", [DEPTH, D_MODEL, D_FF])
    w_down = din("w_down", [DEPTH, D_FF, D_MODEL])
    xT_s = din("xT_s", [D_MODEL, DEC_SEQ])
    rope_s = din("rope_s", [2, 64, DEC_SEQ])
    cache_ckvT = din("cache_ckvT", [DEPTH, KV_LORA, PAST])
    cache_kpeT = din("cache_kpeT", [DEPTH, 64, PAST])
    state_s = din("state_s", [DEPTH, NH, 128, 128])
    cm1s = din("cm1s", [32, 34])
    cm2s = din("cm2s", [32, 32])
    cmasks = din("cmasks", [32, 32])

    yT_p = dout("yT_p", [D_MODEL, SEQ])
    ckvT_p = dout("ckvT_p", [DEPTH, KV_LORA, SEQ])
    kpeT_p = dout("kpeT_p", [DEPTH, 64, SEQ])
    st_p = dout("st_p", [DEPTH, NH, 128, 128])
    yT_s = dout("yT_s", [D_MODEL, DEC_SEQ])
    ckvT_s = dout("ckvT_s", [DEPTH, KV_LORA, DEC_SEQ])
    kpeT_s = dout("kpeT_s", [DEPTH, 64, DEC_SEQ])
    st_s = dout("st_s", [DEPTH, NH, 128, 128])
    dbg_t = {}
    if debug:
        for nm, dt_ in (("d_oa", BF16), ("d_oh", BF16), ("d_gated", BF16), ("d_xm", F32), ("d_mix", F32)):
            dbg_t[nm] = nc.dram_tensor(nm, [128, NKC, TT], dt_, kind="ExternalOutput").ap()
        for nm, dt_ in (("s_oa", BF16), ("s_oh", BF16), ("s_gated", BF16), ("s_xm", F32)):
            dbg_t[nm] = nc.dram_tensor(nm, [128, NKC, DEC_SEQ], dt_, kind="ExternalOutput").ap()

    Kc = nc.dram_tensor("Kc", [DEPTH, NH, 128, SEQ], BF16, kind="Internal").ap()
    Vc = nc.dram_tensor("Vc", [DEPTH, 4, SEQ, 512], BF16, kind="Internal").ap()
    Sc = nc.dram_tensor("Sc", [DEPTH, 128, NH * 128], F32, kind="Internal").ap()
    Kcs = nc.dram_tensor("Kcs", [DEPTH, NH, 128, PAST], BF16, kind="Internal").ap()
    Vcs = nc.dram_tensor("Vcs", [DEPTH, 4, PAST + 128, 512], BF16, kind="Internal").ap()
    r_Kc = [Res(f"Kc{l}") for l in range(DEPTH)]
    r_Vc = [Res(f"Vc{l}") for l in range(DEPTH)]
    r_Sc = [Res(f"Sc{l}") for l in range(DEPTH)]
    r_Kcs = [Res(f"Kcs{l}") for l in range(DEPTH)]
    r_Vcs = [Res(f"Vcs{l}") for l in range(DEPTH)]

    SB_BASE, SB_END = 16512, 229376
    cur = [SB_BASE]

    def sb(name, shape, dt, at=None):
        nbytes = int(np.prod(shape[1:])) * (4 if dt == F32 else 2)
        nbytes = (nbytes + 63) // 64 * 64
        if at is None:
            off = cur[0]
            cur[0] += nbytes
            assert cur[0] <= SB_END, f"SBUF overflow at {name}: {cur[0]}"
        else:
            off = at
        t = nc.alloc_sbuf_tensor_at(name, list(shape), dt, offset=off)
        return Buf(t, Res(name)), off, nbytes

    def sbb(name, shape, dt):
        return sb(name, shape, dt)[0]

    class Ctx:
        pass

    pc = Ctx()
    pc.kind, pc.n, pc.blk, pc.nblk, pc.nch, pc.L = "p", TT, 128, 4, 2, 64
    sc_ = Ctx()
    sc_.kind, sc_.n, sc_.blk, sc_.nblk, sc_.nch, sc_.L = "s", DEC_SEQ, 32, 1, 1, 32

    pc.xT = sbb("xT", [128, NKC, TT], F32).split(NKC)
    pc.hT = sbb("hT", [128, NKC, TT], BF16).split(NKC)
    pc.oa, oa_off, _ = sb("oa", [128, NKC, TT], BF16)
    pc.oh = sbb("oh", [128, NKC, TT], BF16)
    vbuf = pc.oh
    wring, wring_alt = [], []
    for i in range(nslot):
        b_, off_, _ = sb(f"w{i}", [128, NKC, 512], BF16)
        wring.append(b_)
        wring_alt.append(Buf(nc.alloc_sbuf_tensor_at(f"w{i}a", [128, 4, 2048], BF16, offset=off_), b_.r))
    kpeT = [sbb(f"kpeT{l}", [64, SEQ], BF16) for l in range(DEPTH)]
    gains = sbb("gains", [128, DEPTH, NG], F32)
    ident = sbb("ident", [128, 128], BF16)
    ones_bf = sbb("ones_bf", [128, 128], BF16)
    zeros_bf = sbb("zeros_bf", [128, 512], BF16)
    pc.m1 = sbb("m1", [128, 132], F32)
    pc.m2 = sbb("m2", [128, 128], F32)
    pc.mask = sbb("maskbd", [128, 128], F32)
    pc.ropec = sbb("ropec", [64, TT], F32)
    pc.ropes = sbb("ropes", [64, TT], F32)
    lbt = sbb("lbt", [128, 512], F32)
    omlt = sbb("omlt", [128, 512], F32)
    pc.sq = [sbb(f"sq{i}", [128, TT], BF16) for i in range(2)]
    pc.rstd = sbb("rstd", [128, TT], F32)
    pc.ntmp = sbb("ntmp", [128, TT], F32)
    kb0, kb_off, kb_bytes = sb("Kbuf0", [128, SEQ], BF16)
    kb1, kb1_off, _ = sb("Kbuf1", [128, SEQ], BF16)
    Sbuf = sb("Sbuf", [128, NH, 128], F32, at=kb_off)[0]
    Sres = [Res(f"S{h}") for h in range(NH)]
    for r_ in Sres:
        r_.aliases = (kb0.r, kb1.r)
    kb0.r.aliases = tuple(Sres)
    kb1.r.aliases = tuple(Sres)
    Kbuf = [kb0, kb1]
    pc.S_ap = lambda h: Sbuf.t[:, h, :]
    pc.S_res = lambda h: Sres[h]
    pc.mixT, mix_off, mix_bytes = sb("mixT", [128, NKC, TT], F32)
    sc_cur = [mix_off]
    scratch_all = []

    def scr(name, shape, dt):
        b, off, nb = sb(name, shape, dt, at=sc_cur[0])
        sc_cur[0] += nb
        assert sc_cur[0] <= mix_off + mix_bytes, f"scratch overflow {name}"
        scratch_all.append((b, off, nb))
        return b

    def alloc_scratch(c, alloc, n, blk):
        c.qlatn = alloc("qlatn", [128, 4, n], BF16)
        c.ckvbf = alloc("ckvbf", [128, 4, n], BF16)
        c.tmp4 = alloc("tmp4", [128, 4, n], F32)
        return c

    alloc_scratch(pc, scr, TT, 128)
    NSTG = 6
    stage = [scr(f"stage{i}", [128, TT], BF16) for i in range(NSTG)]
    pc.qn = [scr(f"qn{i}", [128, TT], BF16) for i in range(2)]
    pc.qr = [scr(f"qr{i}", [64, TT], BF16) for i in range(2)]
    pc.pT = [scr(f"pT{i}", [128, TT], BF16) for i in range(3)]
    pc.recip = scr("recip", [128, TT], F32)

    def alloc_hgrn(c, alloc, n, blk, nch, batched=False):
        c.hqT = alloc("hqT", [128, 4, n], BF16)
        c.hgT = alloc("hgT", [128, 4, n], BF16)
        c.logf = alloc("logf", [blk, n // blk, 512], F32)
        c.kkb = alloc("kkb", [blk, n // blk, 512], BF16)
        c.vtb = alloc("vtb", [blk, n // blk, 512], BF16)
        if batched:
            c.bufA = alloc("bufA", [128, 512], BF16)
            c.bufB = alloc("bufB", [128, 512], BF16)
            c.qtil4 = alloc("qtil4", [128, 4, 128], BF16)
            c.khat4 = alloc("khat4", [128, 512], BF16)
            c.eBs = alloc("eBs", [128, 16], F32)
            c.eL = alloc("eL", [128, 4, 2], F32)
            c.Smb4 = alloc("Smb4", [128, 4, 128], BF16)
            c.Sd4 = alloc("Sd4", [128, 4, 128], F32)
            return
        c.expB = [alloc(f"expB{i}", [128, blk + 2 * nch], F32) for i in range(2)]
        c.expE = [alloc(f"expE{i}", [blk, 128], F32) for i in range(2)]
        c.qtil = [alloc(f"qtil{i}", [128, blk], BF16) for i in range(2)]
        c.khat = [alloc(f"khat{i}", [blk, 128], BF16) for i in range(2)]
        c.ktil = [alloc(f"ktil{i}", [128, blk], BF16) for i in range(2)]
        c.ATm = [alloc(f"ATm{i}", [blk, blk], BF16) for i in range(2)]
        c.Smb = [alloc(f"Smb{i}", [128, 128], BF16) for i in range(2)]
        c.Sd = [alloc(f"Sd{i}", [128, 128], F32) for i in range(2)]

    sc_cur[0] = mix_off
    alloc_hgrn(pc, scr, TT, 128, 2, batched=True)
    sc_cur[0] = mix_off
    pc.sga = scr("sga", [128, NKC, TT], BF16)
    pc.sgb = scr("sgb", [128, NKC, TT], BF16)
    pc.mixT.r.aliases = tuple(b.r for b, _, _ in scratch_all)
    for b, off, nb in scratch_all:
        b.r.aliases = (pc.mixT.r,) + tuple(o.r for o, o_off, o_nb in scratch_all
                                           if o is not b and o_off < off + nb and off < o_off + o_nb)

    samp_start = cur[0]
    samp_bufs = []
    if with_sample:
        s = sc_
        n_s = DEC_SEQ

        def ssb(name, shape, dt):
            b_ = sbb("s_" + name, shape, dt)
            samp_bufs.append(b_)
            return b_
        s.xT = ssb("xT", [128, NKC, n_s], F32).split(NKC)
        s.hT = ssb("hT", [128, NKC, n_s], BF16).split(NKC)
        s.oa = ssb("oa", [128, NKC, n_s], BF16)
        s.oh = ssb("oh", [128, NKC, n_s], BF16)
        s.mixT, smix_off, _ = sb("s_mixT", [128, NKC, n_s], F32)
        alloc_scratch(s, ssb, n_s, 32)
        s.qn4 = ssb("qn4", [128, 4, n_s], BF16)
        s.qr4 = ssb("qr4", [64, 4, n_s], BF16)
        s.pT = [ssb(f"pT{i}", [128, n_s], BF16) for i in range(3)]
        s.recip = ssb("recip", [128, 128], F32)
        s.knew = ssb("knew", [128, NH, n_s], BF16)
        s.vnew = ssb("vnew", [32, 512], BF16)
        s.kpeS = ssb("kpeS", [64, 2048 + n_s], BF16)
        alloc_hgrn(s, ssb, n_s, 32, 1)
        s.sga = sb("s_sga", [128, NKC, n_s], BF16, at=smix_off)[0]
        s.sgb = sb("s_sgb", [128, NKC, n_s], BF16, at=smix_off + NKC * n_s * 2)[0]
        s.sga.r.aliases = (s.mixT.r,)
        s.sgb.r.aliases = (s.mixT.r,)
        s.mixT.r.aliases = (s.sga.r, s.sgb.r)
        s.Sg = ssb("Sg", [128, 4, 128], F32)
        s.Sg_res = [Res(f"sS{i}") for i in range(4)]
        s.S_ap = lambda h: s.Sg.t[:, h % 4, :]
        s.S_res = lambda h: s.Sg_res[h % 4]
        s.sq = [ssb(f"sq{i}", [128, 128], BF16) for i in range(2)]
        s.rstd = ssb("rstd", [128, 128], F32)
        s.ntmp = ssb("ntmp", [128, 128], F32)
        s.ropec = ssb("ropec", [64, n_s], F32)
        s.ropes = ssb("ropes", [64, n_s], F32)
        s.m1 = ssb("m1", [32, 34], F32)
        s.m2 = ssb("m2", [32, 32], F32)
        s.mask = ssb("mask", [32, 32], F32)
        slabs = [sb(f"s_slab{i}", [128, 4, 512], BF16, at=oa_off + i * 4096)[0] for i in range(2)]
        for sl_ in slabs:
            sl_.r.aliases = (pc.oa.r,)
        pc.oa.r.aliases = tuple(sl_.r for sl_ in slabs)
    extra_slot = with_sample and (cur[0] - samp_start) >= NKC * 512 * 2 and n_tiles > 1
    if extra_slot:
        b_ = sb("w2", [128, NKC, 512], BF16, at=samp_start)[0]
        samp_res = []
        for x_ in samp_bufs:
            samp_res.extend(x_.all())
        samp_res.extend([sc_.mixT.r, sc_.sga.r, sc_.sgb.r] + list(sc_.Sg_res))
        b_.r.aliases = tuple(samp_res)
        wring.append(b_)
        wring_alt.append(Buf(nc.alloc_sbuf_tensor_at("w2a", [128, 4, 2048], BF16, offset=samp_start), b_.r))
    sbuf_used = cur[0] - SB_BASE

    psum = []
    for i in range(8):
        t = nc.alloc_psum_tensor(f"ps{i}", [128, 512], F32)
        psum.append(Buf(t, Res(f"ps{i}")))

    st = contextlib.ExitStack()
    esems = {e: st.enter_context(nc.semaphore(f"s_{e}")) for e in ENGINES}

    def newsem(name):
        return DmaSem(st.enter_context(nc.semaphore(name)))

    wsem = [newsem(f"wsem{i}") for i in range(nslot + 1)]
    sem_const = newsem("const")
    sem_x = newsem("xload")
    sem_out_y = newsem("out_y")
    sem_out_ckv = newsem("out_ckv")
    sem_out_kpe = newsem("out_kpe")
    sem_out_st = newsem("out_st")
    sem_kv = [newsem("kvld0"), newsem("kvld1")]
    sem_vld = newsem("vld")
    sem_kvst = [newsem(f"kvst{i}") for i in range(NSTG)]
    sem_Sld = newsem("sld")
    sem_Sst = [newsem("sst0"), newsem("sst1")]
    sem_rc = newsem("ropec")
    sem_rs = newsem("ropes")
    sem_lb = newsem("lb")
    sem_oml = newsem("oml")
    sem_dbg = newsem("dbg")
    sem_sx = newsem("s_x")
    sem_sout = [newsem(f"s_out{i}") for i in range(4)]
    sem_slab = [newsem("s_slab0"), newsem("s_slab1")]
    sem_skpe = newsem("s_kpe")
    sem_sS = newsem("s_Sld")
    sem_svn = newsem("s_vnew")
    out_toks = []

    def dbg(nm, buf):
        if debug:
            out_toks.append(P.dma(SP, dbg_t[nm], buf.t[:], sem_dbg, reads=buf.all()))

    evac_flip = [0]

    def ew_engine():
        evac_flip[0] ^= 1
        return ACT if evac_flip[0] else DVE

    def act(out, in_, func, reads, writes, bias=None, scale=None):
        kw = {}
        if bias is not None:
            kw["bias"] = bias
        if scale is not None:
            kw["scale"] = scale
        return P.op(ACT, lambda e: e.activation(out=out, in_=in_, func=func, **kw), reads, writes)

    def copy_any(out, in_, reads, writes, eng=None):
        eng = eng or ew_engine()
        if eng == ACT:
            return P.op(ACT, lambda e: e.activation(out=out, in_=in_, func=AF.Copy), reads, writes)
        return P.op(DVE, lambda e: e.tensor_copy(out=out, in_=in_), reads, writes)

    def tt(out, in0, in1, op, reads, writes):
        return P.op(DVE, lambda e: e.tensor_tensor(out=out, in0=in0, in1=in1, op=op), reads, writes)

    def ts(out, in0, s1, s2, op0, op1, reads, writes):
        if s2 is None:
            return P.op(DVE, lambda e: e.tensor_scalar(out=out, in0=in0, scalar1=s1, scalar2=None, op0=op0), reads, writes)
        return P.op(DVE, lambda e: e.tensor_scalar(out=out, in0=in0, scalar1=s1, scalar2=s2, op0=op0, op1=op1), reads, writes)

    def stt(out, in0, scalar, in1, op0, op1, reads, writes):
        return P.op(DVE, lambda e: e.scalar_tensor_tensor(out=out, in0=in0, scalar=scalar, in1=in1, op0=op0, op1=op1), reads, writes)

    def mm(out, lhsT, rhs, start, stop, reads, writes):
        return P.op(PE, lambda e: e.matmul(out, lhsT, rhs, start=start, stop=stop, skip_group_check=True), reads, writes)

    class WStream:
        def __init__(self):
            self.blocks = []
            self.next_load = 0
            self.next_use = 0
            self.slot_last = {}
            self.cnt = {}

        def add(self, w2d, r0, nrows, c0, ncols, ring):
            nkc = nrows // 128
            v = w2d[r0:r0 + nrows, c0:c0 + ncols].rearrange("(kc p) n -> p kc n", p=128)
            k = self.cnt.get(ring, 0)
            self.cnt[ring] = k + 1
            self.blocks.append((v, nkc, ncols, k % ring))

        def _view(self, k):
            v, nkc, ncols, sl = self.blocks[k]
            return (wring_alt if ncols > 512 else wring)[sl]

        def _issue(self, k):
            v, nkc, ncols, sl = self.blocks[k]
            P.dma(POOL, self._view(k).t[:, 0:nkc, 0:ncols], v, wsem[sl], writes=[wring[sl].r])

        def prefetch(self):
            while self.next_load < len(self.blocks):
                sl = self.blocks[self.next_load][3]
                if self.slot_last.get(sl, -1) >= self.next_use:
                    break
                self._issue(self.next_load)
                self.slot_last[sl] = self.next_load
                self.next_load += 1

        def get(self):
            self.prefetch()
            k = self.next_use
            assert k < self.next_load
            self.next_use += 1
            return self._view(k)

    ws = WStream()

    def layer_blocks(l, ring):
        wi = w_in[l]
        n0 = len(ws.blocks)
        _add = ws.add
        ws_add = lambda *a: _add(*a, ring)
        ws_add(wi, 0, 2048, C_Q, 512)
        ws_add(wi, 0, 2048, C_KV, 512)
        ws_add(wi, 0, 2048, C_KPE, 64)
        ws_add(wi, 0, 2048, C_KPESW, 64)
        ws_add(w_uk[l], 0, 512, 0, 2048)
        ws_add(w_uv[l], 0, 512, 0, 2048)
        for g in range(4):
            ws_add(w_uq[l], 0, 512, g * 1024, 1024)
        for g in range(4):
            ws_add(wi, 0, 2048, C_HQ + g * 512, 512)
            ws_add(wi, 0, 2048, C_HF + g * 512, 512)
            ws_add(wi, 0, 2048, C_HI + g * 512, 512)
            ws_add(wi, 0, 2048, C_HG + g * 512, 512)
        for g in range(4):
            ws_add(wi, 0, 2048, C_GA + g * 512, 512)
        for g in range(4):
            ws_add(wi, 0, 2048, C_GB + g * 512, 512)
        for g in range(4):
            ws_add(w_oa[l], 0, 2048, g * 512, 512)
            ws_add(w_ob[l], 0, 2048, g * 512, 512)
        for g in range(4):
            ws_add(w_out[l], 0, 2048, g * 512, 512)
        for q in range(4):
            for g in range(4):
                ws_add(w_up[l], 0, 2048, q * 2048 + g * 512, 512)
            for g in range(4):
                ws_add(w_down[l], q * 2048, 2048, g * 512, 512)
        return len(ws.blocks) - n0

    nblocks_layer = 0
    for it in range(n_tiles):
        for l in range(n_layers):
            nblocks_layer = layer_blocks(l, nslot + 1 if (extra_slot and it > 0) else nslot)

    c_toks = []
    ident_f = pc.ntmp
    c_toks.append(P.dma(SP, gains.t[:], gains_d.rearrange("l p g -> p l g"), sem_const, writes=[gains.r]))
    c_toks.append(P.dma(SP, pc.m1.t[:], cm1, sem_const, writes=[pc.m1.r]))
    c_toks.append(P.dma(SP, pc.m2.t[:], cm2, sem_const, writes=[pc.m2.r]))
    c_toks.append(P.dma(SP, pc.mask.t[:], cmask, sem_const, writes=[pc.mask.r]))
    c_toks.append(P.dma(SP, ident_f.t[:, 0:128], cident, sem_const, writes=[ident_f.r]))
    if with_sample:
        c_toks.append(P.dma(SP, sc_.m1.t[:], cm1s, sem_const, writes=[sc_.m1.r]))
        c_toks.append(P.dma(SP, sc_.m2.t[:], cm2s, sem_const, writes=[sc_.m2.r]))
        c_toks.append(P.dma(SP, sc_.mask.t[:], cmasks, sem_const, writes=[sc_.mask.r]))
        c_toks.append(P.dma(SP, sc_.ropec.t[:], rope_s[0], sem_const, writes=[sc_.ropec.r]))
        c_toks.append(P.dma(SP, sc_.ropes.t[:], rope_s[1], sem_const, writes=[sc_.ropes.r]))
    for e_ in (DVE, ACT, PE):
        P.wait_all(e_, c_toks)
    P.op(DVE, lambda e: e.tensor_copy(out=ident.t[:], in_=ident_f.t[:, 0:128]), [ident_f.r], [ident.r])
    P.op(DVE, lambda e: e.memset(ones_bf.t[:], 1.0), [], [ones_bf.r])
    P.op(DVE, lambda e: e.memset(zeros_bf.t[:], 0.0), [], [zeros_bf.r])

    def rms_rstd(c, src_chunks, n, nfeat, reads):
        bank = psum[7]
        nch_ = len(src_chunks)
        for i, a in enumerate(src_chunks):
            s_ = c.sq[i % 2]
            rd = [reads[i]] if len(reads) == nch_ and nch_ > 1 else reads
            act(s_.t[:, 0:n], a, AF.Square, rd, [s_.r])
            mm(bank.t[:, 0:n], ones_bf.t[:, :], s_.t[:, 0:n], i == 0, i == nch_ - 1, [s_.r, ones_bf.r], [bank.r])
        act(c.ntmp.t[:, 0:n], bank.t[:, 0:n], AF.Sqrt, [bank.r], [c.ntmp.r], bias=float(EPS), scale=1.0 / nfeat)
        P.op(DVE, lambda e: e.reciprocal(out=c.rstd.t[:, 0:n], in_=c.ntmp.t[:, 0:n]), [c.ntmp.r], [c.rstd.r])

    def gemm_fm(slot, nchunks, rhs_buf, nk, n, consume, col0=0, banks=(0, 1, 2, 3)):
        for j in range(nchunks):
            bank = psum[banks[j % len(banks)]]
            for kc in range(nk):
                mm(bank.t[:, 0:n], slot.t[:, kc, col0 + j * 128: col0 + (j + 1) * 128], rhs_buf.t[:, kc, 0:n],
                   kc == 0, kc == nk - 1, [slot.r, rhs_buf.ck(kc)], [bank.r])
            consume(j, bank)

    def gemm_tm(slot, c, consume, banks=(0, 1, 2, 3)):
        for tb in range(c.nblk):
            bank = psum[banks[tb % len(banks)]]
            for kc in range(NKC):
                mm(bank.t[0:c.blk, 0:512], c.hT.t[:, kc, tb * c.blk:(tb + 1) * c.blk], slot.t[:, kc, 0:512],
                   kc == 0, kc == NKC - 1, [slot.r, c.hT.ck(kc)], [bank.r])
            consume(tb, bank)

    def layer_gen(c, l, it):
        n = c.n
        G = gains.t
        isp = c.kind == "p"
        t0 = it * TT

        rms_rstd(c, [c.xT.t[:, k, :] for k in range(NKC)], n, D_MODEL, c.xT.all())
        for k in range(NKC):
            stt(c.hT.t[:, k, :], c.xT.t[:, k, :], G[:, l, G_PRE + k:G_PRE + k + 1], c.rstd.t[:, 0:n], ALU.mult, ALU.mult,
                [c.xT.ck(k), c.rstd.r, gains.r], [c.hT.ck(k)])

        def lat_block(slot, dst_bf, gcol, out_dram=None, osem=None):
            def cons(j, bank):
                copy_any(c.tmp4.t[:, j, :], bank.t[:, 0:n], [bank.r], [c.tmp4.r])
            gemm_fm(slot, 4, c.hT, NKC, n, cons)
            rms_rstd(c, [c.tmp4.t[:, k, :] for k in range(4)], n, 512, [c.tmp4.r])
            for k in range(4):
                stt(c.tmp4.t[:, k, :], c.tmp4.t[:, k, :], G[:, l, gcol + k:gcol + k + 1], c.rstd.t[:, 0:n], ALU.mult, ALU.mult,
                    [c.tmp4.r, c.rstd.r, gains.r], [c.tmp4.r])
                copy_any(dst_bf.t[:, k, :], c.tmp4.t[:, k, :], [c.tmp4.r], [dst_bf.r])
            if out_dram is not None:
                out_toks.append(P.dma(SP, out_dram, c.tmp4.t[:], osem, reads=[c.tmp4.r]))

        slot = yield
        lat_block(slot, c.qlatn, G_QN)
        slot = yield
        if isp:
            lat_block(slot, c.ckvbf, G_KVN, ckvT_p[l, :, t0:t0 + TT].rearrange("(k p) t -> p k t", p=128), sem_out_ckv)
        else:
            lat_block(slot, c.ckvbf, G_KVN, ckvT_s[l].rearrange("(k p) t -> p k t", p=128), sem_sout[1])

        if isp and l == 0:
            P.dma(SP, c.ropec.t[:], rope_p[0, :, t0:t0 + TT], sem_rc, writes=[c.ropec.r])
            P.dma(SP, c.ropes.t[:], rope_p[1, :, t0:t0 + TT], sem_rs, writes=[c.ropes.r])
        s1 = yield
        b1 = psum[0]
        for kc in range(NKC):
            mm(b1.t[0:64, 0:n], s1.t[:, kc, 0:64], c.hT.t[:, kc, :], kc == 0, kc == NKC - 1, [s1.r, c.hT.ck(kc)], [b1.r])
        tt(c.tmp4.t[0:64, 0, :], b1.t[0:64, 0:n], c.ropec.t[:, :], ALU.mult, [b1.r, c.ropec.r], [c.tmp4.r])
        s2 = yield
        b2 = psum[1]
        for kc in range(NKC):
            mm(b2.t[0:64, 0:n], s2.t[:, kc, 0:64], c.hT.t[:, kc, :], kc == 0, kc == NKC - 1, [s2.r, c.hT.ck(kc)], [b2.r])
        tt(c.tmp4.t[0:64, 1, :], b2.t[0:64, 0:n], c.ropes.t[:, :], ALU.mult, [b2.r, c.ropes.r], [c.tmp4.r])
        tt(c.tmp4.t[0:64, 2, :], c.tmp4.t[0:64, 0, :], c.tmp4.t[0:64, 1, :], ALU.add, [c.tmp4.r], [c.tmp4.r])
        if isp:
            copy_any(kpeT[l].t[:, t0:t0 + TT], c.tmp4.t[0:64, 2, :], [c.tmp4.r], [kpeT[l].r], eng=ACT)
            out_toks.append(P.dma(SP, kpeT_p[l, :, t0:t0 + TT], c.tmp4.t[0:64, 2, :], sem_out_kpe, reads=[c.tmp4.r]))
        else:
            copy_any(c.kpeS.t[:, 2048:2048 + n], c.tmp4.t[0:64, 2, :], [c.tmp4.r], [c.kpeS.r], eng=ACT)
            out_toks.append(P.dma(SP, kpeT_s[l], c.tmp4.t[0:64, 2, :], sem_sout[2], reads=[c.tmp4.r]))

        slot = yield
        kv_st = []
        if isp:
            for h in range(NH):
                bank = psum[h % 4]
                for kc in range(4):
                    mm(bank.t[:, 0:n], slot.t[:, kc, h * 128:(h + 1) * 128], c.ckvbf.t[:, kc, :], kc == 0, kc == 3,
                       [slot.r, c.ckvbf.r], [bank.r])
                sg = stage[h % NSTG]
                copy_any(sg.t[:, :], bank.t[:, 0:n], [bank.r], [sg.r])
                kv_st.append(P.dma(SP if h % 2 == 0 else ACT, Kc[l, h, :, t0:t0 + TT], sg.t[:, :], sem_kvst[h % NSTG], reads=[sg.r], writes=[r_Kc[l]]))
        else:
            for h in range(NH):
                bank = psum[h % 4]
                for kc in range(4):
                    mm(bank.t[:, 0:n], slot.t[:, kc, h * 128:(h + 1) * 128], c.ckvbf.t[:, kc, :], kc == 0, kc == 3,
                       [slot.r, c.ckvbf.r], [bank.r])
                copy_any(c.knew.t[:, h, :], bank.t[:, 0:n], [bank.r], [c.knew.r])
            cnt = 0

            def load_slab(sl_):
                sb_ = slabs[sl_ % 2]
                P.dma(POOL, sb_.t[:], cache_ckvT[l, :, sl_ * 512:(sl_ + 1) * 512].rearrange("(k p) t -> p k t", p=128),
                      sem_slab[sl_ % 2], writes=[sb_.r])

            load_slab(0)
            for sl in range(PAST // 512):
                if sl + 1 < PAST // 512:
                    load_slab(sl + 1)
                slab = slabs[sl % 2]
                for h in range(NH):
                    bank = psum[cnt % 4]
                    for kc in range(4):
                        mm(bank.t[:, 0:512], slot.t[:, kc, h * 128:(h + 1) * 128], slab.t[:, kc, :], kc == 0, kc == 3,
                           [slot.r, slab.r], [bank.r])
                    sg = stage[cnt % NSTG]
                    copy_any(sg.t[:, :], bank.t[:, 0:512], [bank.r], [sg.r])
                    kv_st.append(P.dma(SP if cnt % 2 == 0 else ACT, Kcs[l, h, :, sl * 512:(sl + 1) * 512], sg.t[:, :], sem_kvst[cnt % NSTG],
                                       reads=[sg.r], writes=[r_Kcs[l]]))
                    cnt += 1
        slot = yield
        cnt = 0
        if isp:
            for tb in range(TT // 128):
                for g in range(4):
                    bank = psum[cnt % 4]
                    for kc in range(4):
                        mm(bank.t[:, 0:512], c.ckvbf.t[:, kc, tb * 128:(tb + 1) * 128], slot.t[:, kc, g * 512:(g + 1) * 512],
                           kc == 0, kc == 3, [slot.r, c.ckvbf.r], [bank.r])
                    sg = stage[cnt % NSTG]
                    copy_any(sg.t[:, :], bank.t[:, 0:512], [bank.r], [sg.r])
                    kv_st.append(P.dma(SP if cnt % 2 == 0 else ACT, Vc[l, g, t0 + tb * 128:t0 + (tb + 1) * 128, :], sg.t[:, :], sem_kvst[cnt % NSTG],
                                       reads=[sg.r], writes=[r_Vc[l]]))
                    cnt += 1
        else:
            for g in range(4):
                bank = psum[cnt % 4]
                for kc in range(4):
                    mm(bank.t[0:n, 0:512], c.ckvbf.t[:, kc, 0:n], slot.t[:, kc, g * 512:(g + 1) * 512],
                       kc == 0, kc == 3, [slot.r, c.ckvbf.r], [bank.r])
                sg = stage[cnt % NSTG]
                copy_any(sg.t[0:n, :], bank.t[0:n, 0:512], [bank.r], [sg.r])
                kv_st.append(P.dma(SP, Vcs[l, g, PAST:PAST + n, :], sg.t[0:n, :], sem_kvst[cnt % NSTG],
                                   reads=[sg.r], writes=[r_Vcs[l]]))
                cnt += 1
            load_slab(0)
            for sl in range(PAST // 512):
                if sl + 1 < PAST // 512:
                    load_slab(sl + 1)
                slab = slabs[sl % 2]
                for tb in range(4):
                    for g in range(4):
                        bank = psum[cnt % 4]
                        for kc in range(4):
                            mm(bank.t[:, 0:512], slab.t[:, kc, tb * 128:(tb + 1) * 128], slot.t[:, kc, g * 512:(g + 1) * 512],
                               kc == 0, kc == 3, [slot.r, slab.r], [bank.r])
                        sg = stage[cnt % NSTG]
                        copy_any(sg.t[:, :], bank.t[:, 0:512], [bank.r], [sg.r])
                        r0 = sl * 512 + tb * 128
                        kv_st.append(P.dma(SP if cnt % 2 == 0 else ACT, Vcs[l, g, r0:r0 + 128, :], sg.t[:, :], sem_kvst[cnt % NSTG],
                                           reads=[sg.r], writes=[r_Vcs[l]]))
                        cnt += 1
        last = {}
        for tk in kv_st:
            last[id(tk.dsem)] = tk
        P.wait_all(SP, list(last.values()))

        def q_proj(wq, hh, qn_ap, qr_ap):
            c0 = hh * 256
            bq, bp, bs = psum[6], psum[7], psum[0]
            for kc in range(4):
                mm(bq.t[:, 0:n], wq.t[:, kc, c0:c0 + 128], c.qlatn.t[:, kc, :], kc == 0, kc == 3, [wq.r, c.qlatn.r], [bq.r])
            for kc in range(4):
                mm(bp.t[0:64, 0:n], wq.t[:, kc, c0 + 128:c0 + 192], c.qlatn.t[:, kc, :], kc == 0, kc == 3, [wq.r, c.qlatn.r], [bp.r])
            for kc in range(4):
                mm(bs.t[0:64, 0:n], wq.t[:, kc, c0 + 192:c0 + 256], c.qlatn.t[:, kc, :], kc == 0, kc == 3, [wq.r, c.qlatn.r], [bs.r])
            return bq, bp, bs

        hcnt = 0
        for g in range(4):
            wq = yield
            if isp:
                Lk = t0 + TT
                nkb = Lk // 128
                P.dma(SP, vbuf.t[:, 0:nkb, :], Vc[l, g, 0:Lk, :].rearrange("(kb p) c -> p kb c", p=128), sem_vld,
                      reads=[r_Vc[l]], writes=[vbuf.r])

                def kload(hh_, hc_):
                    kbuf_ = Kbuf[hc_ % 2]
                    P.dma(SP, kbuf_.t[:, 0:Lk], Kc[l, g * 4 + hh_, :, 0:Lk], sem_kv[hc_ % 2], reads=[r_Kc[l]], writes=[kbuf_.r])

                def qprep(hh_, hc_):
                    qnb_, qrb_ = c.qn[hc_ % 2], c.qr[hc_ % 2]
                    c0 = hh_ * 256
                    bq, bp = psum[6], psum[7]
                    bs = psum[2] if hc_ % 2 == 0 else psum[4]
                    for kc in range(4):
                        mm(bq.t[:, 0:n], wq.t[:, kc, c0:c0 + 128], c.qlatn.t[:, kc, :], kc == 0, kc == 3, [wq.r, c.qlatn.r], [bq.r])
                    for kc in range(4):
                        mm(bp.t[0:64, 0:n], wq.t[:, kc, c0 + 128:c0 + 192], c.qlatn.t[:, kc, :], kc == 0, kc == 3, [wq.r, c.qlatn.r], [bp.r])
                    for kc in range(4):
                        mm(bs.t[0:64, 0:n], wq.t[:, kc, c0 + 192:c0 + 256], c.qlatn.t[:, kc, :], kc == 0, kc == 3, [wq.r, c.qlatn.r], [bs.r])
                    copy_any(qnb_.t[:, :], bq.t[:, 0:n], [bq.r], [qnb_.r], eng=ACT)
                    tt(c.tmp4.t[0:64, 0, :], bp.t[0:64, 0:n], c.ropec.t[:, :], ALU.mult, [bp.r, c.ropec.r], [c.tmp4.r])
                    tt(c.tmp4.t[0:64, 1, :], bs.t[0:64, 0:n], c.ropes.t[:, :], ALU.mult, [bs.r, c.ropes.r], [c.tmp4.r])
                    tt(qrb_.t[:, :], c.tmp4.t[0:64, 0, :], c.tmp4.t[0:64, 1, :], ALU.add, [c.tmp4.r], [qrb_.r])

                kload(0, hcnt)
                qprep(0, hcnt)
                for hh in range(4):
                    h = g * 4 + hh
                    kbuf = Kbuf[hcnt % 2]
                    qnb, qrb = c.qn[hcnt % 2], c.qr[hcnt % 2]
                    bo, bsum = (psum[2], psum[3]) if hcnt % 2 == 0 else (psum[4], psum[5])
                    if hh + 1 < 4:
                        kload(hh + 1, hcnt + 1)

                    def scores(kb):
                        d = kb - it * 4
                        qs = max(d, 0) * 128
                        nn = n - qs
                        bsc = psum[kb % 2]
                        mm(bsc.t[:, 0:nn], kbuf.t[:, kb * 128:(kb + 1) * 128], qnb.t[:, qs:n], True, False,
                           [kbuf.r, qnb.r], [bsc.r])
                        mm(bsc.t[:, 0:nn], kpeT[l].t[0:64, kb * 128:(kb + 1) * 128], qrb.t[0:64, qs:n], False, True,
                           [kpeT[l].r, qrb.r], [bsc.r])

                    scores(0)
                    for kb in range(nkb):
                        if kb + 1 < nkb:
                            scores(kb + 1)
                        if kb == min(1, nkb - 1) and hh + 1 < 4:
                            qprep(hh + 1, hcnt + 1)
                        d = kb - it * 4
                        qs = max(d, 0) * 128
                        nn = n - qs
                        bsc = psum[kb % 2]
                        pb = c.pT[kb % 3]
                        act(pb.t[:, 0:nn], bsc.t[:, 0:nn], AF.Exp, [bsc.r], [pb.r], scale=ATTN_SCALE)
                        if d >= 0:
                            P.op(DVE, lambda e, pb=pb: e.memset(pb.t[64:128, 0:64], 0.0), [], [pb.r])
                        mm(bo.t[:, qs:n], vbuf.t[:, kb, hh * 128:(hh + 1) * 128], pb.t[:, 0:nn], kb == 0, kb == nkb - 1,
                           [vbuf.r, pb.r], [bo.r])
                        mm(bsum.t[:, qs:n], ones_bf.t[:, :], pb.t[:, 0:nn], kb == 0, kb == nkb - 1,
                           [ones_bf.r, pb.r], [bsum.r])
                    P.op(DVE, lambda e, bsum=bsum: e.reciprocal(out=c.recip.t[:, 0:n], in_=bsum.t[:, 0:n]), [bsum.r], [c.recip.r])
                    tt(c.oa.t[:, h, :], bo.t[:, 0:n], c.recip.t[:, 0:n], ALU.mult, [bo.r, c.recip.r], [c.oa.r])
                    hcnt += 1
            else:
                for hh in range(4):
                    bq, bp, bs = q_proj(wq, hh, None, None)
                    copy_any(c.qn4.t[:, hh, :], bq.t[:, 0:n], [bq.r], [c.qn4.r], eng=ACT)
                    tt(c.tmp4.t[0:64, 0, :], bp.t[0:64, 0:n], c.ropec.t[:, :], ALU.mult, [bp.r, c.ropec.r], [c.tmp4.r])
                    tt(c.tmp4.t[0:64, 1, :], bs.t[0:64, 0:n], c.ropes.t[:, :], ALU.mult, [bs.r, c.ropes.r], [c.tmp4.r])
                    tt(c.qr4.t[:, hh, :], c.tmp4.t[0:64, 0, :], c.tmp4.t[0:64, 1, :], ALU.add, [c.tmp4.r], [c.qr4.r])
                bo, bsum = psum[2], psum[3]
                mm(bo.t[:, 0:128], zeros_bf.t[:, 0:128], zeros_bf.t[:, 0:128], True, False, [zeros_bf.r], [bo.r])
                mm(bsum.t[:, 0:128], zeros_bf.t[:, 0:128], zeros_bf.t[:, 0:128], True, False, [zeros_bf.r], [bsum.r])
                pcnt = 0
                for half in range(2):
                    P.dma(POOL, c.kpeS.t[:, 0:2048], cache_kpeT[l, :, half * 2048:(half + 1) * 2048], sem_skpe,
                          writes=[c.kpeS.r])
                    P.dma(SP, vbuf.t[:, 0:16, :], Vcs[l, g, half * 2048:(half + 1) * 2048, :].rearrange("(kb p) c -> p kb c", p=128),
                          sem_vld, reads=[r_Vcs[l]], writes=[vbuf.r])
                    for hh in range(4):
                        h = g * 4 + hh
                        kbuf = Kbuf[hcnt % 2]
                        P.dma(SP, kbuf.t[:, 0:2048], Kcs[l, h, :, half * 2048:(half + 1) * 2048], sem_kv[hcnt % 2],
                              reads=[r_Kcs[l]], writes=[kbuf.r])
                        hcnt += 1

                        def sscores(kb_, pc_):
                            bsc_ = psum[pc_ % 2]
                            mm(bsc_.t[:, 0:n], kbuf.t[:, kb_ * 128:(kb_ + 1) * 128], c.qn4.t[:, hh, :], True, False,
                               [kbuf.r, c.qn4.r], [bsc_.r])
                            mm(bsc_.t[:, 0:n], c.kpeS.t[0:64, kb_ * 128:(kb_ + 1) * 128], c.qr4.t[0:64, hh, :], False, True,
                               [c.kpeS.r, c.qr4.r], [bsc_.r])

                        sscores(0, pcnt)
                        for kb in range(16):
                            if kb + 1 < 16:
                                sscores(kb + 1, pcnt + 1)
                            bsc = psum[pcnt % 2]
                            pb = c.pT[pcnt % 3]
                            act(pb.t[:, 0:n], bsc.t[:, 0:n], AF.Exp, [bsc.r], [pb.r], scale=ATTN_SCALE)
                            mm(bo.t[:, hh * n:(hh + 1) * n], vbuf.t[:, kb, hh * 128:(hh + 1) * 128], pb.t[:, 0:n], False, False,
                               [vbuf.r, pb.r], [bo.r])
                            mm(bsum.t[:, hh * n:(hh + 1) * n], ones_bf.t[:, :], pb.t[:, 0:n], False, False,
                               [ones_bf.r, pb.r], [bsum.r])
                            pcnt += 1
                P.dma(SP, c.vnew.t[:, :], Vcs[l, g, PAST:PAST + n, :], sem_svn, reads=[r_Vcs[l]], writes=[c.vnew.r])
                for hh in range(4):
                    h = g * 4 + hh
                    bsc = psum[pcnt % 2]
                    mm(bsc.t[0:n, 0:n], c.knew.t[:, h, :], c.qn4.t[:, hh, :], True, False, [c.knew.r, c.qn4.r], [bsc.r])
                    mm(bsc.t[0:n, 0:n], c.kpeS.t[0:64, 2048:2048 + n], c.qr4.t[0:64, hh, :], False, True,
                       [c.kpeS.r, c.qr4.r], [bsc.r])
                    pb = c.pT[pcnt % 3]
                    act(pb.t[0:n, 0:n], bsc.t[0:n, 0:n], AF.Exp, [bsc.r], [pb.r], scale=ATTN_SCALE)
                    mm(bo.t[:, hh * n:(hh + 1) * n], c.vnew.t[0:n, hh * 128:(hh + 1) * 128], pb.t[0:n, 0:n], False, False,
                       [c.vnew.r, pb.r], [bo.r])
                    mm(bsum.t[:, hh * n:(hh + 1) * n], ones_bf.t[0:n, :], pb.t[0:n, 0:n], False, False,
                       [ones_bf.r, pb.r], [bsum.r])
                    pcnt += 1
                P.op(DVE, lambda e, bsum=bsum: e.reciprocal(out=c.recip.t[:, 0:128], in_=bsum.t[:, 0:128]), [bsum.r], [c.recip.r])
                tt(c.oa.t[:, g * 4:(g + 1) * 4, :], bo.t[:, 0:128].rearrange("p (h t) -> p h t", h=4),
                   c.recip.t[:, 0:128].rearrange("p (h t) -> p h t", h=4), ALU.mult, [bo.r, c.recip.r], [c.oa.r])
        dbg("d_oa" if isp else "s_oa", c.oa)

        blk, nch, L = c.blk, c.nch, c.L
        for g in range(4):
            if not isp:
                P.dma(SP, c.Sg.t[:], state_s[l, g * 4:(g + 1) * 4].rearrange("h k v -> k h v"), sem_sS, writes=c.Sg_res)
            slot = yield
            if isp and g == 0:
                if it == 0:
                    P.op(DVE, lambda e: e.memset(Sbuf.t[:], 0.0), [], Sres)
                else:
                    P.dma(SP, Sbuf.t[:], Sc[l].rearrange("p (h v) -> p h v", h=NH), sem_Sld, reads=[r_Sc[l]], writes=Sres)
            if isp and l > 0:
                cs_ = slice(g * 512, (g + 1) * 512)
                P.dma(SP, lbt.t[:], hg_lb[l, cs_].partition_broadcast(128), sem_lb, writes=[lbt.r])
                P.dma(SP, omlt.t[:], hg_lb[0, cs_].partition_broadcast(128), sem_oml, writes=[omlt.r])
                tt(lbt.t[:], lbt.t[:], omlt.t[:], ALU.subtract, [lbt.r, omlt.r], [lbt.r])
                act(lbt.t[:], lbt.t[:], AF.Sigmoid, [lbt.r], [lbt.r])
                ts(omlt.t[:], lbt.t[:], -1.0, 1.0, ALU.mult, ALU.add, [lbt.r], [omlt.r])

            def cons_q(j, bank):
                act(c.hqT.t[:, j, :], bank.t[:, 0:n], AF.Silu, [bank.r], [c.hqT.r])
            gemm_fm(slot, 4, c.hT, NKC, n, cons_q)
            slot = yield

            def cons_f(tb, bank):
                lf = c.logf.t[:, tb, :]
                act(lf, bank.t[0:blk, 0:512], AF.Sigmoid, [bank.r], [c.logf.r])
                if l > 0:
                    tt(lf, lf, omlt.t[0:blk, :], ALU.mult, [c.logf.r, omlt.r], [c.logf.r])
                    tt(lf, lf, lbt.t[0:blk, :], ALU.add, [c.logf.r, lbt.r], [c.logf.r])
                ts(c.kkb.t[:, tb, :], lf, -1.0, 1.0, ALU.mult, ALU.add, [c.logf.r], [c.kkb.r])
                act(lf, lf, AF.Ln, [c.logf.r], [c.logf.r])
            gemm_tm(slot, c, cons_f)
            slot = yield

            def cons_v(tb, bank):
                copy_any(c.vtb.t[:, tb, :], bank.t[0:blk, 0:512], [bank.r], [c.vtb.r])
            gemm_tm(slot, c, cons_v)
            slot = yield

            def cons_g(j, bank):
                act(c.hgT.t[:, j, :], bank.t[:, 0:n], AF.Silu, [bank.r], [c.hgT.r])
            gemm_fm(slot, 4, c.hT, NKC, n, cons_g)
            bcnt = 0
            W4 = 4 * blk
            for tb in range(c.nblk):
                if isp:
                    bO = psum[4 + (tb % 2)]
                    mm(bO.t[:, 0:512], zeros_bf.t[:, 0:128], zeros_bf.t[:, 0:512], True, False, [zeros_bf.r], [bO.r])
                    tsl = slice(tb * 128, (tb + 1) * 128)
                    lf = c.logf.t[:, tb, :]
                    bBm, bEk, bSm, bD0, bD1 = psum[0], psum[1], psum[2], psum[3], psum[6]
                    Sg = Sbuf.t[:, g * 4:(g + 1) * 4, :]
                    Sg_res = [Sres[g * 4 + i] for i in range(4)]
                    eBs4 = c.eBs.t[:, :].rearrange("p (h c t) -> p h c t", h=4, c=2)
                    HS = [slice(i * 128, (i + 1) * 128) for i in range(4)]
                    for hh in range(4):
                        mm(bBm.t[:, HS[hh]], lf[:, HS[hh]], c.m1.t[:, 0:128], True, True, [c.logf.r, c.m1.r], [bBm.r])
                    for hh in range(4):
                        mm(bSm.t[:, hh * 4:(hh + 1) * 4], lf[:, HS[hh]], c.m1.t[:, 128:132], True, True, [c.logf.r, c.m1.r], [bSm.r])
                    mm(bEk.t[:, 0:512], c.m2.t[:, :], lf[:, 0:512], True, True, [c.logf.r, c.m2.r], [bEk.r])
                    act(c.bufA.t[:, :], bBm.t[:, 0:512], AF.Exp, [bBm.r], [c.bufA.r])
                    act(c.eBs.t[:, :], bSm.t[:, 0:16], AF.Exp, [bSm.r], [c.eBs.r])
                    act(c.bufB.t[:, :], bEk.t[:, 0:512], AF.Exp, [bEk.r], [c.bufB.r])
                    tt(c.eL.t[:, :, :], eBs4[:, :, :, 0], eBs4[:, :, :, 1], ALU.mult, [c.eBs.r], [c.eL.r])
                    tt(c.qtil4.t[:, :, :], c.hqT.t[:, :, tsl], c.bufA.t[:, :].rearrange("p (h t) -> p h t", h=4), ALU.mult,
                       [c.hqT.r, c.bufA.r], [c.qtil4.r])
                    tt(c.khat4.t[:, :], c.kkb.t[:, tb, :], c.bufB.t[:, :], ALU.mult, [c.kkb.r, c.bufB.r], [c.khat4.r])
                    for hh in range(4):
                        mm(bEk.t[:, HS[hh]], c.khat4.t[:, HS[hh]], ident.t[:, :], True, True, [c.khat4.r, ident.r], [bEk.r])
                    copy_any(c.bufA.t[:, :], bEk.t[:, 0:512], [bEk.r], [c.bufA.r], eng=ACT)
                    for hh in range(4):
                        mm(bBm.t[:, HS[hh]], c.bufA.t[:, HS[hh]], c.qtil4.t[:, hh, :], True, True, [c.bufA.r, c.qtil4.r], [bBm.r])
                    tt(c.bufB.t[:, :].rearrange("p (h t) -> p h t", h=4), bBm.t[:, 0:512].rearrange("p (h t) -> p h t", h=4),
                       c.mask.t[:, :].unsqueeze(1).broadcast_to([128, 4, 128]), ALU.mult, [bBm.r, c.mask.r], [c.bufB.r])
                    for hh in range(4):
                        mm(bO.t[:, HS[hh]], c.vtb.t[:, tb, HS[hh]], c.bufB.t[:, HS[hh]], False, False, [c.vtb.r, c.bufB.r], [bO.r])
                    for hh in range(4):
                        mm(bD0.t[:, HS[hh]], c.khat4.t[0:64, HS[hh]], c.vtb.t[0:64, tb, HS[hh]], True, True,
                           [c.khat4.r, c.vtb.r], [bD0.r])
                        mm(bD1.t[:, HS[hh]], c.khat4.t[64:128, HS[hh]], c.vtb.t[64:128, tb, HS[hh]], True, True,
                           [c.khat4.r, c.vtb.r], [bD1.r])
                    for ch in range(2):
                        bD = bD0 if ch == 0 else bD1
                        em = eBs4[:, :, ch, 0:1].broadcast_to([128, 4, 128])
                        elm = eBs4[:, :, ch, 1:2].broadcast_to([128, 4, 128])
                        el = c.eL.t[:, :, ch:ch + 1].broadcast_to([128, 4, 128])
                        tt(c.Smb4.t[:, :, :], Sg, em, ALU.mult, Sg_res + [c.eBs.r], [c.Smb4.r])
                        tt(c.Sd4.t[:, :, :], Sg, el, ALU.mult, Sg_res + [c.eL.r], [c.Sd4.r])
                        for hh in range(4):
                            mm(bO.t[:, hh * 128 + ch * 64: hh * 128 + (ch + 1) * 64], c.Smb4.t[:, hh, :],
                               c.qtil4.t[:, hh, ch * 64:(ch + 1) * 64], False, False, [c.Smb4.r, c.qtil4.r], [bO.r])
                        tt(Sg, bD.t[:, 0:512].rearrange("p (h v) -> p h v", h=4), elm, ALU.mult, [bD.r, c.eBs.r], Sg_res)
                        tt(Sg, Sg, c.Sd4.t[:, :, :], ALU.add, Sg_res + [c.Sd4.r], Sg_res)
                else:
                    bO = psum[4 + (tb % 2)]
                    mm(bO.t[:, 0:W4], zeros_bf.t[:, 0:128], zeros_bf.t[:, 0:W4], True, False, [zeros_bf.r], [bO.r])
                    for hh in range(4):
                        h = g * 4 + hh
                        i2 = bcnt % 2
                        cs = slice(hh * 128, (hh + 1) * 128)
                        tsl = slice(tb * blk, (tb + 1) * blk)
                        ocs = slice(hh * blk, (hh + 1) * blk)
                        bB, bE, bK, bA = psum[0], psum[1], psum[2], psum[3]
                        bDs = (psum[6], psum[7])
                        eBb, eEb, qtb, khb, ktb, atb = c.expB[i2], c.expE[i2], c.qtil[i2], c.khat[i2], c.ktil[i2], c.ATm[i2]
                        nb2 = blk + 2 * nch
                        mm(bB.t[:, 0:nb2], c.logf.t[:, tb, cs], c.m1.t[:, :], True, True, [c.logf.r, c.m1.r], [bB.r])
                        mm(bE.t[0:blk, 0:128], c.m2.t[:, :], c.logf.t[:, tb, cs], True, True, [c.logf.r, c.m2.r], [bE.r])
                        act(eBb.t[:, :], bB.t[:, 0:nb2], AF.Exp, [bB.r], [eBb.r])
                        act(eEb.t[:, :], bE.t[0:blk, 0:128], AF.Exp, [bE.r], [eEb.r])
                        tt(qtb.t[:, :], c.hqT.t[:, hh, tsl], eBb.t[:, 0:blk], ALU.mult, [c.hqT.r, eBb.r], [qtb.r])
                        tt(khb.t[:, :], c.kkb.t[:, tb, cs], eEb.t[:, :], ALU.mult, [c.kkb.r, eEb.r], [khb.r])
                        mm(bK.t[:, 0:blk], khb.t[:, :], ident.t[0:blk, 0:blk], True, True, [khb.r, ident.r], [bK.r])
                        copy_any(ktb.t[:, :], bK.t[:, 0:blk], [bK.r], [ktb.r], eng=ACT)
                        mm(bA.t[0:blk, 0:blk], ktb.t[:, :], qtb.t[:, :], True, True, [ktb.r, qtb.r], [bA.r])
                        tt(atb.t[:, :], bA.t[0:blk, 0:blk], c.mask.t[:, :], ALU.mult, [bA.r, c.mask.r], [atb.r])
                        mm(bO.t[:, ocs], c.vtb.t[:, tb, cs], atb.t[:, :], False, False, [c.vtb.r, atb.r], [bO.r])
                        for ch in range(nch):
                            mm(bDs[ch].t[:, 0:128], khb.t[ch * L:(ch + 1) * L, :], c.vtb.t[ch * L:(ch + 1) * L, tb, cs], True, True,
                               [khb.r, c.vtb.r], [bDs[ch].r])
                        for ch in range(nch):
                            cm, cl = blk + 2 * ch, blk + 2 * ch + 1
                            eB = eBb.t
                            Sap, Sr = c.S_ap(h), c.S_res(h)
                            ts(c.Smb[ch].t[:, :], Sap, eB[:, cm:cm + 1], None, ALU.mult, None, [Sr, eBb.r], [c.Smb[ch].r])
                            ts(c.Sd[ch].t[:, :], Sap, eB[:, cm:cm + 1], eB[:, cl:cl + 1], ALU.mult, ALU.mult, [Sr, eBb.r], [c.Sd[ch].r])
                            mm(bO.t[:, hh * blk + ch * L: hh * blk + (ch + 1) * L], c.Smb[ch].t[:, :],
                               qtb.t[:, ch * L:(ch + 1) * L], False, False, [c.Smb[ch].r, qtb.r], [bO.r])
                            stt(Sap, bDs[ch].t[:, 0:128], eB[:, cl:cl + 1], c.Sd[ch].t[:, :], ALU.mult, ALU.add,
                                [bDs[ch].r, eBb.r, c.Sd[ch].r], [Sr])
                        bcnt += 1
                s0 = c.sq[0]
                bN = psum[2]
                act(s0.t[:, 0:W4], bO.t[:, 0:W4], AF.Square, [bO.r], [s0.r])
                mm(bN.t[:, 0:W4], ones_bf.t[:, :], s0.t[:, 0:W4], True, True, [s0.r, ones_bf.r], [bN.r])
                act(c.ntmp.t[:, 0:W4], bN.t[:, 0:W4], AF.Sqrt, [bN.r], [c.ntmp.r], bias=float(EPS), scale=1.0 / 128)
                P.op(DVE, lambda e: e.reciprocal(out=c.rstd.t[:, 0:W4], in_=c.ntmp.t[:, 0:W4]), [c.ntmp.r], [c.rstd.r])
                stt(c.ntmp.t[:, 0:W4], bO.t[:, 0:W4], G[:, l, G_HGN:G_HGN + 1], c.rstd.t[:, 0:W4], ALU.mult, ALU.mult,
                    [bO.r, c.rstd.r, gains.r], [c.ntmp.r])
                tt(c.oh.t[:, g * 4:(g + 1) * 4, tb * blk:(tb + 1) * blk],
                   c.ntmp.t[:, 0:W4].rearrange("p (h t) -> p h t", h=4),
                   c.hgT.t[:, :, tb * blk:(tb + 1) * blk], ALU.mult, [c.ntmp.r, c.hgT.r], [c.oh.r])
            if not isp:
                out_toks.append(P.dma(SP, st_s[l, g * 4:(g + 1) * 4].rearrange("h k v -> k h v"), c.Sg.t[:], sem_sout[3],
                                      reads=c.Sg_res))
        if isp:
            if it == n_tiles - 1:
                out_toks.append(P.dma(SP, st_p[l].rearrange("h k v -> k h v"), Sbuf.t[:], sem_out_st, reads=Sres))
            else:
                P.dma(SP, Sc[l].rearrange("p (h v) -> p h v", h=NH), Sbuf.t[:], sem_Sst[l], reads=Sres, writes=[r_Sc[l]])
        dbg("d_oh" if isp else "s_oh", c.oh)

        for g in range(4):
            slot = yield

            def cons_a(j, bank, g=g):
                act(c.sga.t[:, g * 4 + j, :], bank.t[:, 0:n], AF.Sigmoid, [bank.r], [c.sga.r])
            gemm_fm(slot, 4, c.hT, NKC, n, cons_a)
        for g in range(4):
            slot = yield

            def cons_b(j, bank, g=g):
                act(c.sgb.t[:, g * 4 + j, :], bank.t[:, 0:n], AF.Sigmoid, [bank.r], [c.sgb.r])
            gemm_fm(slot, 4, c.hT, NKC, n, cons_b)
        for g in range(4):
            slot = yield

            def cons_oa(j, bank, g=g):
                m = g * 4 + j
                tt(c.sga.t[:, m, :], bank.t[:, 0:n], c.sga.t[:, m, :], ALU.mult, [bank.r, c.sga.r], [c.sga.r])
            gemm_fm(slot, 4, c.oa, NKC, n, cons_oa)
            slot = yield

            def cons_ob(j, bank, g=g):
                m = g * 4 + j
                tt(c.sgb.t[:, m, :], bank.t[:, 0:n], c.sgb.t[:, m, :], ALU.mult, [bank.r, c.sgb.r], [c.sgb.r])
                tt(c.hT.t[:, m, :], c.sga.t[:, m, :], c.sgb.t[:, m, :], ALU.add, [c.sga.r, c.sgb.r], [c.hT.ck(m)])
            gemm_fm(slot, 4, c.oh, NKC, n, cons_ob, banks=(4, 5, 6, 7))
        dbg("d_gated" if isp else "s_gated", c.hT)

        for g in range(4):
            slot = yield

            def cons_o(j, bank, g=g):
                copy_any(c.mixT.t[:, g * 4 + j, :], bank.t[:, 0:n], [bank.r], [c.mixT.r])
            gemm_fm(slot, 4, c.hT, NKC, n, cons_o)

        def post_norm_residual(gcol):
            rms_rstd(c, [c.mixT.t[:, k, :] for k in range(NKC)], n, D_MODEL, [c.mixT.r])
            for k in range(NKC):
                stt(c.mixT.t[:, k, :], c.mixT.t[:, k, :], G[:, l, gcol + k:gcol + k + 1], c.rstd.t[:, 0:n], ALU.mult, ALU.mult,
                    [c.mixT.r, c.rstd.r, gains.r], [c.mixT.r])
                tt(c.xT.t[:, k, :], c.xT.t[:, k, :], c.mixT.t[:, k, :], ALU.add, [c.xT.ck(k), c.mixT.r], [c.xT.ck(k)])
        if isp:
            dbg("d_mix", c.mixT)
        post_norm_residual(G_POST)
        dbg("d_xm" if isp else "s_xm", c.xT)

        rms_rstd(c, [c.xT.t[:, k, :] for k in range(NKC)], n, D_MODEL, c.xT.all())
        for k in range(NKC):
            stt(c.hT.t[:, k, :], c.xT.t[:, k, :], G[:, l, G_PREM + k:G_PREM + k + 1], c.rstd.t[:, 0:n], ALU.mult, ALU.mult,
                [c.xT.ck(k), c.rstd.r, gains.r], [c.hT.ck(k)])
        for q in range(4):
            for g in range(4):
                slot = yield

                def cons_u(j, bank, g=g):
                    k = g * 4 + j
                    act(c.ntmp.t[:, 0:n], bank.t[:, 0:n], AF.Relu, [bank.r], [c.ntmp.r])
                    tt(c.oa.t[:, k, :], c.ntmp.t[:, 0:n], c.ntmp.t[:, 0:n], ALU.mult, [c.ntmp.r], [c.oa.r])
                gemm_fm(slot, 4, c.hT, NKC, n, cons_u)
            for g in range(4):
                slot = yield

                def cons_d(j, bank, g=g, q=q):
                    m = g * 4 + j
                    if q == 0:
                        copy_any(c.mixT.t[:, m, :], bank.t[:, 0:n], [bank.r], [c.mixT.r])
                    else:
                        tt(c.mixT.t[:, m, :], bank.t[:, 0:n], c.mixT.t[:, m, :], ALU.add, [bank.r, c.mixT.r], [c.mixT.r])
                gemm_fm(slot, 4, c.oa, NKC, n, cons_d, banks=(4, 5, 6, 7))
        post_norm_residual(G_POSTM)

    def run_layer(l, it, ctxs):
        gens = [layer_gen(c, l, it) for c in ctxs]
        for g_ in gens:
            next(g_)
        nb = 0
        while True:
            slot = ws.get()
            nb += 1
            done = 0
            for g_ in gens:
                try:
                    g_.send(slot)
                except StopIteration:
                    done += 1
            if done:
                assert done == len(gens)
                break
        assert nb == nblocks_layer, (nb, nblocks_layer)

    for it in range(n_tiles):
        t0 = it * TT
        P.dma(SP, pc.xT.t[:], xT_p[:, t0:t0 + TT].rearrange("(k p) t -> p k t", p=128), sem_x, writes=pc.xT.all())
        ctxs = [pc]
        if with_sample and it == 0:
            P.dma(SP, sc_.xT.t[:], xT_s.rearrange("(k p) t -> p k t", p=128), sem_sx, writes=sc_.xT.all())
            ctxs.append(sc_)
        for l in range(n_layers):
            run_layer(l, it, ctxs)
        out_toks.append(P.dma(SP, yT_p[:, t0:t0 + TT].rearrange("(k p) t -> p k t", p=128), pc.xT.t[:], sem_out_y, reads=pc.xT.all()))
        if with_sample and it == 0:
            out_toks.append(P.dma(SP, yT_s.rearrange("(k p) t -> p k t", p=128), sc_.xT.t[:], sem_sout[0], reads=sc_.xT.all()))

    last = {}
    for tk in out_toks:
        last[id(tk.dsem)] = tk
    P.wait_all(SP, list(last.values()))
    stats = P.emit(esems)
    st.close()
    stats["sbuf_used"] = sbuf_used
    return nc, stats


def _rope_tables(pos):
    half = QK_ROPE // 2
    inv = (np.float32(10000.0) ** (-np.arange(half, dtype=np.float32) / np.float32(half))).astype(np.float32)
    ang = pos.astype(np.float32)[:, None] * inv[None, :]
    cos = np.cos(ang).astype(np.float32).T
    sin = np.sin(ang).astype(np.float32).T
    cos2 = np.concatenate([cos, cos], axis=0)
    sin2 = np.concatenate([-sin, sin], axis=0)
    return np.ascontiguousarray(np.stack([cos2, sin2]))


def _hgrn_consts():
    idx = np.arange(128)
    ch = idx // 64
    loc = idx % 64
    same = ch[:, None] == ch[None, :]
    m1 = np.zeros((128, 132), np.float32)
    le = (loc[:, None] <= loc[None, :]) & same
    mid = (loc[:, None] <= 31) & same
    m1[:, :128] = le.astype(np.float32) - mid.astype(np.float32)
    for c in range(2):
        m1[:, 128 + 2 * c] = ((ch == c) & (loc <= 31)).astype(np.float32)
        m1[:, 129 + 2 * c] = ((ch == c) & (loc > 31)).astype(np.float32)
    m2 = (mid.astype(np.float32) - le.astype(np.float32)).astype(np.float32)
    mask = ((loc[:, None] <= loc[None, :]) & same).astype(np.float32)
    return m1, m2, mask, np.eye(128, dtype=np.float32)


def _hgrn_consts_sample():
    i = np.arange(32)
    le = (i[:, None] <= i[None, :])
    mid = (i[:, None] <= 15) & np.ones((32, 32), bool)
    m1 = np.zeros((32, 34), np.float32)
    m1[:, :32] = le.astype(np.float32) - mid.astype(np.float32)
    m1[:, 32] = (i <= 15).astype(np.float32)
    m1[:, 33] = (i > 15).astype(np.float32)
    m2 = (mid.astype(np.float32) - le.astype(np.float32)).astype(np.float32)
    mask = le.astype(np.float32)
    return m1, m2, mask


_PROG_CACHE = {}


def _get_program(n_tiles, n_layers):
    key = (n_tiles, n_layers)
    if key not in _PROG_CACHE:
        _PROG_CACHE[key] = build_program(n_tiles, n_layers)
    return _PROG_CACHE[key]


def _col(v, n):
    return np.ascontiguousarray(v.reshape(n, 128).T)


def prepare_inputs(inp):
    f = lambda a: np.ascontiguousarray(np.asarray(a, dtype=np.float32))
    w_in = f(inp["w_in"])
    kpe = w_in[:, :, C_KPE:C_KPE + 64]
    w_in_ext = np.concatenate([w_in, kpe[:, :, 32:64], kpe[:, :, 0:32]], axis=2)
    wuq = f(inp["w_uq"]).reshape(DEPTH, Q_LORA, NH, 192)
    wuq_ext = np.concatenate([wuq, wuq[..., 160:192], wuq[..., 128:160]], axis=3).reshape(DEPTH, Q_LORA, NH * 256)
    gains = np.zeros((DEPTH, 128, NG), np.float32)
    for l in range(DEPTH):
        gains[l, :, G_PRE:G_PRE + 16] = _col(f(inp["pre_mix_g"])[l], 16)
        gains[l, :, G_POST:G_POST + 16] = _col(f(inp["post_mix_g"])[l], 16)
        gains[l, :, G_PREM:G_PREM + 16] = _col(f(inp["pre_mlp_g"])[l], 16)
        gains[l, :, G_POSTM:G_POSTM + 16] = _col(f(inp["post_mlp_g"])[l], 16)
        gains[l, :, G_QN:G_QN + 4] = _col(f(inp["q_norm_g"])[l], 4)
        gains[l, :, G_KVN:G_KVN + 4] = _col(f(inp["kv_norm_g"])[l], 4)
        gains[l, :, G_HGN] = f(inp["hg_norm_g"])[l]
    m1, m2, mask, ident = _hgrn_consts()
    m1s, m2s, masks = _hgrn_consts_sample()
    shared = {
        "rope_p": _rope_tables(np.arange(SEQ)),
        "rope_s": _rope_tables(PAST + np.arange(DEC_SEQ)),
        "cm1": m1, "cm2": m2, "cmask": mask, "cident": ident,
        "cm1s": m1s, "cm2s": m2s, "cmasks": masks,
        "gains": gains,
        "hg_lb": f(inp["hg_lb"]),
        "w_in": np.ascontiguousarray(w_in_ext),
        "w_uq": np.ascontiguousarray(wuq_ext),
        "w_uk": f(inp["w_uk"]).reshape(DEPTH, KV_LORA, 2048),
        "w_uv": f(inp["w_uv"]).reshape(DEPTH, KV_LORA, 2048),
        "w_oa": f(inp["w_oa"]), "w_ob": f(inp["w_ob"]), "w_out": f(inp["w_out"]),
        "w_up": f(inp["w_up"]), "w_down": f(inp["w_down"]),
    }
    xp = f(inp["x_prompt"])
    xs = f(inp["x_sample"])
    cckv = f(inp["cache_ckv"])
    ckpe = f(inp["cache_kpe"])
    sth = f(inp["state_hgrn"])
    in_maps = []
    for c in range(8):
        m = dict(shared)
        m["xT_p"] = np.ascontiguousarray(xp[c % 4].T)
        m["xT_s"] = np.ascontiguousarray(xs[c].T)
        m["cache_ckvT"] = np.ascontiguousarray(cckv[:, c].transpose(0, 2, 1))
        m["cache_kpeT"] = np.ascontiguousarray(ckpe[:, c].transpose(0, 2, 1))
        m["state_s"] = np.ascontiguousarray(sth[:, c])
        in_maps.append(m)
    return in_maps


def kernel(**inp):
    nc, _ = _get_program(4, DEPTH)
    in_maps = prepare_inputs(inp)
    res = run_bass_kernel_spmd(nc, in_maps, core_ids=list(range(8)))
    R = res.results
    B = 4
    y_p = np.stack([R[b]["yT_p"].T for b in range(B)])
    ckv_p = np.stack([np.stack([R[b]["ckvT_p"][l].T for b in range(B)]) for l in range(DEPTH)])
    kpe_p = np.stack([np.stack([R[b]["kpeT_p"][l].T for b in range(B)]) for l in range(DEPTH)])
    st_p = np.stack([np.stack([R[b]["st_p"][l] for b in range(B)]) for l in range(DEPTH)])
    y_s = np.stack([R[c]["yT_s"].T for c in range(8)])
    ckv_s = np.stack([np.stack([R[c]["ckvT_s"][l].T for c in range(8)]) for l in range(DEPTH)])
    kpe_s = np.stack([np.stack([R[c]["kpeT_s"][l].T for c in range(8)]) for l in range(DEPTH)])
    st_s = np.stack([np.stack([R[c]["st_s"][l] for c in range(8)]) for l in range(DEPTH)])
    return (np.ascontiguousarray(y_p), np.ascontiguousarray(y_s), np.ascontiguousarray(ckv_p),
            np.ascontiguousarray(kpe_p), np.ascontiguousarray(st_p), np.ascontiguousarray(ckv_s),
            np.ascontiguousarray(kpe_s), np.ascontiguousarray(st_s))
```

```python
import contextlib
import numpy as np
import concourse.bass as bass
import concourse.mybir as mybir
from concourse.bass_utils import run_bass_kernel_spmd

F32 = mybir.dt.float32
BF16 = mybir.dt.bfloat16
ALU = mybir.AluOpType
AF = mybir.ActivationFunctionType

PE, ACT, DVE, POOL, SP = "tensor", "scalar", "vector", "gpsimd", "sync"
ENGINES = (PE, ACT, DVE, POOL, SP)
STRICT_SAME_ENGINE = True

D_MODEL = 2048
SEQ = 2048
DEPTH = 2
DEC_SEQ = 32
PAST = 4096
NH = 16
QK_NOPE = 128
QK_ROPE = 64
KV_LORA = 512
Q_LORA = 512
D_FF = 8192
EPS = 1e-6
ATTN_SCALE = float((QK_NOPE + QK_ROPE) ** -0.5)
D_IN = 13376
C_Q, C_KV, C_KPE, C_HQ, C_HF, C_HI, C_HG, C_GA, C_GB = 0, 512, 1024, 1088, 3136, 5184, 7232, 9280, 11328
C_KPESW = D_IN
D_IN_EXT = D_IN + 64
TT = 512
NKC = D_MODEL // 128
G_PRE, G_POST, G_PREM, G_POSTM, G_QN, G_KVN, G_HGN, NG = 0, 16, 32, 48, 64, 68, 72, 73


class Res:
    __slots__ = ("name", "last_w", "readers", "aliases")

    def __init__(self, name):
        self.name = name
        self.last_w = None
        self.readers = []
        self.aliases = ()


class DmaSem:
    def __init__(self, handle):
        self.handle = handle
        self.total = 0


class Instr:
    __slots__ = ("eng", "fn", "deps", "signal", "count", "dsem", "dval", "is_dma")

    def __init__(self, eng, fn):
        self.eng = eng
        self.fn = fn
        self.deps = []
        self.signal = False
        self.count = 0
        self.dsem = None
        self.dval = 0
        self.is_dma = False


class Prog:
    def __init__(self, nc):
        self.nc = nc
        self.q = {e: [] for e in ENGINES}
        self.n_instr = 0

    def _track(self, ins, reads, writes):
        eng = ins.eng
        deps = ins.deps
        for r in reads:
            w = r.last_w
            if w is not None and not (w.eng == PE and eng == PE and not w.is_dma and not ins.is_dma):
                deps.append(w)
            r.readers.append(ins)
        same_ok = STRICT_SAME_ENGINE and eng != PE
        for r0 in writes:
            for r in (r0,) + tuple(r0.aliases):
                w = r.last_w
                if w is not None and w is not ins and (w.is_dma or ins.is_dma or w.eng != eng or same_ok):
                    deps.append(w)
                for rd in r.readers:
                    if rd is ins:
                        continue
                    if rd.is_dma or ins.is_dma or rd.eng != eng or same_ok:
                        deps.append(rd)
            r0.last_w = ins
            r0.readers = []

    def op(self, eng, fn, reads=(), writes=()):
        ins = Instr(eng, fn)
        self._track(ins, reads, writes)
        self.q[eng].append(ins)
        self.n_instr += 1
        return ins

    def dma(self, eng, out, in_, dsem, reads=(), writes=(), **kw):
        def fn(e, out=out, in_=in_, kw=kw):
            return e.dma_start(out=out, in_=in_, **kw)
        ins = Instr(eng, fn)
        ins.is_dma = True
        dsem.total += 16
        ins.dsem = dsem
        ins.dval = dsem.total
        self._track(ins, reads, writes)
        self.q[eng].append(ins)
        self.n_instr += 1
        return ins

    def wait_all(self, eng, toks):
        ins = Instr(eng, None)
        ins.deps = list(toks)
        self.q[eng].append(ins)
        return ins

    def emit(self, esems):
        for e in ENGINES:
            for ins in self.q[e]:
                for d in ins.deps:
                    if not d.is_dma:
                        d.signal = True
        for e in ENGINES:
            c = 0
            for ins in self.q[e]:
                if ins.signal and not ins.is_dma:
                    c += 1
                    ins.count = c
        stats = {}
        nc = self.nc
        with nc.Block() as block:
            for e in ENGINES:
                lst = self.q[e]
                if not lst:
                    continue

                def body(engine, lst=lst, e=e):
                    seen = {}
                    nw = 0
                    for ins in lst:
                        need = {}
                        for d in ins.deps:
                            if d.is_dma:
                                key, val, h = id(d.dsem), d.dval, d.dsem.handle
                            else:
                                key, val, h = d.eng, d.count, esems[d.eng]
                            if seen.get(key, 0) >= val:
                                continue
                            if key not in need or need[key][1] < val:
                                need[key] = (h, val)
                        for key, (h, val) in need.items():
                            engine.wait_ge(h, val)
                            seen[key] = val
                            nw += 1
                        if ins.fn is None:
                            continue
                        bi = ins.fn(engine)
                        if ins.is_dma:
                            bi.then_inc(ins.dsem.handle, 16)
                        elif ins.signal:
                            bi.then_inc(esems[e], 1)
                    stats[e] = (len(lst), nw)

                getattr(block, e)(body)
        return stats


class Buf:
    __slots__ = ("t", "r", "rc")

    def __init__(self, t, r):
        self.t = t
        self.r = r
        self.rc = None

    def split(self, n):
        self.rc = [Res(f"{self.r.name}.{i}") for i in range(n)]
        return self

    def ck(self, k):
        return self.rc[k] if self.rc is not None else self.r

    def all(self):
        return list(self.rc) if self.rc is not None else [self.r]


def build_program(n_tiles=4, n_layers=DEPTH, nslot=2, with_sample=True, debug=False):
    nc = bass.Bass("TRN2", target_bir_lowering=False)
    P = Prog(nc)

    def din(name, shape, dt=F32):
        return nc.dram_tensor(name, list(shape), dt, kind="ExternalInput").ap()

    def dout(name, shape, dt=F32):
        return nc.dram_tensor(name, list(shape), dt, kind="ExternalOutput").ap()

    xT_p = din("xT_p", [D_MODEL, SEQ])
    rope_p = din("rope_p", [2, 64, SEQ])
    cm1 = din("cm1", [128, 132])
    cm2 = din("cm2", [128, 128])
    cmask = din("cmask", [128, 128])
    cident = din("cident", [128, 128])
    gains_d = din("gains", [DEPTH, 128, NG])
    hg_lb = din("hg_lb", [DEPTH, D_MODEL])
    w_in = din("w_in", [DEPTH, D_MODEL, D_IN_EXT])
    w_uq = din("w_uq", [DEPTH, Q_LORA, 4096])
    w_uk = din("w_uk", [DEPTH, KV_LORA, 2048])
    w_uv = din("w_uv", [DEPTH, KV_LORA, 2048])
    w_oa = din("w_oa", [DEPTH, D_MODEL, D_MODEL])
    w_ob = din("w_ob", [DEPTH, D_MODEL, D_MODEL])
    w_out = din("w_out", [DEPTH, D_MODEL, D_MODEL])
    w_up = din("w_up", [DEPTH, D_MODEL, D_FF])
    w_down = din("w_down", [DEPTH, D_FF, D_MODEL])
    xT_s = din("xT_s", [D_MODEL, DEC_SEQ])
    rope_s = din("rope_s", [2, 64, DEC_SEQ])
    cache_ckvT = din("cache_ckvT", [DEPTH, KV_LORA, PAST])
    cache_kpeT = din("cache_kpeT", [DEPTH, 64, PAST])
    state_s = din("state_s", [DEPTH, NH, 128, 128])
    cm1s = din("cm1s", [32, 34])
    cm2s = din("cm2s", [32, 32])
    cmasks = din("cmasks", [32, 32])

    yT_p = dout("yT_p", [D_MODEL, SEQ])
    ckvT_p = dout("ckvT_p", [DEPTH, KV_LORA, SEQ])
    kpeT_p = dout("kpeT_p", [DEPTH, 64, SEQ])
    st_p = dout("st_p", [DEPTH, NH, 128, 128])
    yT_s = dout("yT_s", [D_MODEL, DEC_SEQ])
    ckvT_s = dout("ckvT_s", [DEPTH, KV_LORA, DEC_SEQ])
    kpeT_s = dout("kpeT_s", [DEPTH, 64, DEC_SEQ])
    st_s = dout("st_s", [DEPTH, NH, 128, 128])
    dbg_t = {}
    if debug:
        for nm, dt_ in (("d_oa", BF16), ("d_oh", BF16), ("d_gated", BF16), ("d_xm", F32), ("d_mix", F32)):
            dbg_t[nm] = nc.dram_tensor(nm, [128, NKC, TT], dt_, kind="ExternalOutput").ap()
        for nm, dt_ in (("s_oa", BF16), ("s_oh", BF16), ("s_gated", BF16), ("s_xm", F32)):
            dbg_t[nm] = nc.dram_tensor(nm, [128, NKC, DEC_SEQ], dt_, kind="ExternalOutput").ap()

    Kc = nc.dram_tensor("Kc", [DEPTH, NH, 128, SEQ], BF16, kind="Internal").ap()
    Vc = nc.dram_tensor("Vc", [DEPTH, 4, SEQ, 512], BF16, kind="Internal").ap()
    Sc = nc.dram_tensor("Sc", [DEPTH, 128, NH * 128], F32, kind="Internal").ap()
    Kcs = nc.dram_tensor("Kcs", [DEPTH, NH, 128, PAST], BF16, kind="Internal").ap()
    Vcs = nc.dram_tensor("Vcs", [DEPTH, 4, PAST + 128, 512], BF16, kind="Internal").ap()
    r_Kc = [Res(f"Kc{l}") for l in range(DEPTH)]
    r_Vc = [Res(f"Vc{l}") for l in range(DEPTH)]
    r_Sc = [Res(f"Sc{l}") for l in range(DEPTH)]
    r_Kcs = [Res(f"Kcs{l}") for l in range(DEPTH)]
    r_Vcs = [Res(f"Vcs{l}") for l in range(DEPTH)]

    SB_BASE, SB_END = 16512, 229376
    cur = [SB_BASE]

    def sb(name, shape, dt, at=None):
        nbytes = int(np.prod(shape[1:])) * (4 if dt == F32 else 2)
        nbytes = (nbytes + 63) // 64 * 64
        if at is None:
            off = cur[0]
            cur[0] += nbytes
            assert cur[0] <= SB_END, f"SBUF overflow at {name}: {cur[0]}"
        else:
            off = at
        t = nc.alloc_sbuf_tensor_at(name, list(shape), dt, offset=off)
        return Buf(t, Res(name)), off, nbytes

    def sbb(name, shape, dt):
        return sb(name, shape, dt)[0]

    class Ctx:
        pass

    pc = Ctx()
    pc.kind, pc.n, pc.blk, pc.nblk, pc.nch, pc.L = "p", TT, 128, 4, 2, 64
    sc_ = Ctx()
    sc_.kind, sc_.n, sc_.blk, sc_.nblk, sc_.nch, sc_.L = "s", DEC_SEQ, 32, 1, 1, 32

    pc.xT = sbb("xT", [128, NKC, TT], F32).split(NKC)
    pc.hT = sbb("hT", [128, NKC, TT], BF16).split(NKC)
    pc.oa, oa_off, _ = sb("oa", [128, NKC, TT], BF16)
    pc.oh = sbb("oh", [128, NKC, TT], BF16)
    vbuf = pc.oh
    wring, wring_alt = [], []
    for i in range(nslot):
        b_, off_, _ = sb(f"w{i}", [128, NKC, 512], BF16)
        wring.append(b_)
        wring_alt.append(Buf(nc.alloc_sbuf_tensor_at(f"w{i}a", [128, 4, 2048], BF16, offset=off_), b_.r))
    kpeT = [sbb(f"kpeT{l}", [64, SEQ], BF16) for l in range(DEPTH)]
    gains = sbb("gains", [128, DEPTH, NG], F32)
    ident = sbb("ident", [128, 128], BF16)
    ones_bf = sbb("ones_bf", [128, 128], BF16)
    zeros_bf = sbb("zeros_bf", [128, 512], BF16)
    pc.m1 = sbb("m1", [128, 132], F32)
    pc.m2 = sbb("m2", [128, 128], F32)
    pc.mask = sbb("maskbd", [128, 128], F32)
    pc.ropec = sbb("ropec", [64, TT], F32)
    pc.ropes = sbb("ropes", [64, TT], F32)
    lbt = sbb("lbt", [128, 512], F32)
    omlt = sbb("omlt", [128, 512], F32)
    pc.sq = [sbb(f"sq{i}", [128, TT], BF16) for i in range(2)]
    pc.rstd = sbb("rstd", [128, TT], F32)
    pc.ntmp = sbb("ntmp", [128, TT], F32)
    kb0, kb_off, kb_bytes = sb("Kbuf0", [128, SEQ], BF16)
    kb1, kb1_off, _ = sb("Kbuf1", [128, SEQ], BF16)
    Sbuf = sb("Sbuf", [128, NH, 128], F32, at=kb_off)[0]
    Sres = [Res(f"S{h}") for h in range(NH)]
    for r_ in Sres:
        r_.aliases = (kb0.r, kb1.r)
    kb0.r.aliases = tuple(Sres)
    kb1.r.aliases = tuple(Sres)
    Kbuf = [kb0, kb1]
    pc.S_ap = lambda h: Sbuf.t[:, h, :]
    pc.S_res = lambda h: Sres[h]
    pc.mixT, mix_off, mix_bytes = sb("mixT", [128, NKC, TT], F32)
    sc_cur = [mix_off]
    scratch_all = []

    def scr(name, shape, dt):
        b, off, nb = sb(name, shape, dt, at=sc_cur[0])
        sc_cur[0] += nb
        assert sc_cur[0] <= mix_off + mix_bytes, f"scratch overflow {name}"
        scratch_all.append((b, off, nb))
        return b

    def alloc_scratch(c, alloc, n, blk):
        c.qlatn = alloc("qlatn", [128, 4, n], BF16)
        c.ckvbf = alloc("ckvbf", [128, 4, n], BF16)
        c.tmp4 = alloc("tmp4", [128, 4, n], F32)
        return c

    alloc_scratch(pc, scr, TT, 128)
    NSTG = 6
    stage = [scr(f"stage{i}", [128, TT], BF16) for i in range(NSTG)]
    pc.qn = [scr(f"qn{i}", [128, TT], BF16) for i in range(2)]
    pc.qr = [scr(f"qr{i}", [64, TT], BF16) for i in range(2)]
    pc.pT = [scr(f"pT{i}", [128, TT], BF16) for i in range(3)]
    pc.recip = scr("recip", [128, TT], F32)

    def alloc_hgrn(c, alloc, n, blk, nch, batched=False):
        c.hqT = alloc("hqT", [128, 4, n], BF16)
        c.hgT = alloc("hgT", [128, 4, n], BF16)
        c.logf = alloc("logf", [blk, n // blk, 512], F32)
        c.kkb = alloc("kkb", [blk, n // blk, 512], BF16)
        c.vtb = alloc("vtb", [blk, n // blk, 512], BF16)
        if batched:
            c.bufA = alloc("bufA", [128, 512], BF16)
            c.bufB = alloc("bufB", [128, 512], BF16)
            c.qtil4 = alloc("qtil4", [128, 4, 128], BF16)
            c.khat4 = alloc("khat4", [128, 512], BF16)
            c.eBs = alloc("eBs", [128, 16], F32)
            c.eL = alloc("eL", [128, 4, 2], F32)
            c.Smb4 = alloc("Smb4", [128, 4, 128], BF16)
            c.Sd4 = alloc("Sd4", [128, 4, 128], F32)
            return
        c.expB = [alloc(f"expB{i}", [128, blk + 2 * nch], F32) for i in range(2)]
        c.expE = [alloc(f"expE{i}", [blk, 128], F32) for i in range(2)]
        c.qtil = [alloc(f"qtil{i}", [128, blk], BF16) for i in range(2)]
        c.khat = [alloc(f"khat{i}", [blk, 128], BF16) for i in range(2)]
        c.ktil = [alloc(f"ktil{i}", [128, blk], BF16) for i in range(2)]
        c.ATm = [alloc(f"ATm{i}", [blk, blk], BF16) for i in range(2)]
        c.Smb = [alloc(f"Smb{i}", [128, 128], BF16) for i in range(2)]
        c.Sd = [alloc(f"Sd{i}", [128, 128], F32) for i in range(2)]

    sc_cur[0] = mix_off
    alloc_hgrn(pc, scr, TT, 128, 2, batched=True)
    sc_cur[0] = mix_off
    pc.sga = scr("sga", [128, NKC, TT], BF16)
    pc.sgb = scr("sgb", [128, NKC, TT], BF16)
    pc.mixT.r.aliases = tuple(b.r for b, _, _ in scratch_all)
    for b, off, nb in scratch_all:
        b.r.aliases = (pc.mixT.r,) + tuple(o.r for o, o_off, o_nb in scratch_all
                                           if o is not b and o_off < off + nb and off < o_off + o_nb)

    samp_start = cur[0]
    samp_bufs = []
    if with_sample:
        s = sc_
        n_s = DEC_SEQ

        def ssb(name, shape, dt):
            b_ = sbb("s_" + name, shape, dt)
            samp_bufs.append(b_)
            return b_
        s.xT = ssb("xT", [128, NKC, n_s], F32).split(NKC)
        s.hT = ssb("hT", [128, NKC, n_s], BF16).split(NKC)
        s.oa = ssb("oa", [128, NKC, n_s], BF16)
        s.oh = ssb("oh", [128, NKC, n_s], BF16)
        s.mixT, smix_off, _ = sb("s_mixT", [128, NKC, n_s], F32)
        alloc_scratch(s, ssb, n_s, 32)
        s.qn4 = ssb("qn4", [128, 4, n_s], BF16)
        s.qr4 = ssb("qr4", [64, 4, n_s], BF16)
        s.pT = [ssb(f"pT{i}", [128, n_s], BF16) for i in range(3)]
        s.recip = ssb("recip", [128, 128], F32)
        s.knew = ssb("knew", [128, NH, n_s], BF16)
        s.vnew = ssb("vnew", [32, 512], BF16)
        s.kpeS = ssb("kpeS", [64, 2048 + n_s], BF16)
        alloc_hgrn(s, ssb, n_s, 32, 1)
        s.sga = sb("s_sga", [128, NKC, n_s], BF16, at=smix_off)[0]
        s.sgb = sb("s_sgb", [128, NKC, n_s], BF16, at=smix_off + NKC * n_s * 2)[0]
        s.sga.r.aliases = (s.mixT.r,)
        s.sgb.r.aliases = (s.mixT.r,)
        s.mixT.r.aliases = (s.sga.r, s.sgb.r)
        s.Sg = ssb("Sg", [128, 4, 128], F32)
        s.Sg_res = [Res(f"sS{i}") for i in range(4)]
        s.S_ap = lambda h: s.Sg.t[:, h % 4, :]
        s.S_res = lambda h: s.Sg_res[h % 4]
        s.sq = [ssb(f"sq{i}", [128, 128], BF16) for i in range(2)]
        s.rstd = ssb("rstd", [128, 128], F32)
        s.ntmp = ssb("ntmp", [128, 128], F32)
        s.ropec = ssb("ropec", [64, n_s], F32)
        s.ropes = ssb("ropes", [64, n_s], F32)
        s.m1 = ssb("m1", [32, 34], F32)
        s.m2 = ssb("m2", [32, 32], F32)
        s.mask = ssb("mask", [32, 32], F32)
        slab = sb("s_slab", [128, 4, 512], BF16, at=oa_off)[0]
        slab.r.aliases = (pc.oa.r,)
        pc.oa.r.aliases = (slab.r,)
    extra_slot = with_sample and (cur[0] - samp_start) >= NKC * 512 * 2 and n_tiles > 1
    if extra_slot:
        b_ = sb("w2", [128, NKC, 512], BF16, at=samp_start)[0]
        samp_res = []
        for x_ in samp_bufs:
            samp_res.extend(x_.all())
        samp_res.extend([sc_.mixT.r, sc_.sga.r, sc_.sgb.r] + list(sc_.Sg_res))
        b_.r.aliases = tuple(samp_res)
        wring.append(b_)
        wring_alt.append(Buf(nc.alloc_sbuf_tensor_at("w2a", [128, 4, 2048], BF16, offset=samp_start), b_.r))
    sbuf_used = cur[0] - SB_BASE

    psum = []
    for i in range(8):
        t = nc.alloc_psum_tensor(f"ps{i}", [128, 512], F32)
        psum.append(Buf(t, Res(f"ps{i}")))

    st = contextlib.ExitStack()
    esems = {e: st.enter_context(nc.semaphore(f"s_{e}")) for e in ENGINES}

    def newsem(name):
        return DmaSem(st.enter_context(nc.semaphore(name)))

    wsem = [newsem(f"wsem{i}") for i in range(nslot + 1)]
    sem_const = newsem("const")
    sem_x = newsem("xload")
    sem_out_y = newsem("out_y")
    sem_out_ckv = newsem("out_ckv")
    sem_out_kpe = newsem("out_kpe")
    sem_out_st = newsem("out_st")
    sem_kv = [newsem("kvld0"), newsem("kvld1")]
    sem_vld = newsem("vld")
    sem_kvst = [newsem(f"kvst{i}") for i in range(NSTG)]
    sem_Sld = newsem("sld")
    sem_Sst = [newsem("sst0"), newsem("sst1")]
    sem_rc = newsem("ropec")
    sem_rs = newsem("ropes")
    sem_lb = newsem("lb")
    sem_oml = newsem("oml")
    sem_dbg = newsem("dbg")
    sem_sx = newsem("s_x")
    sem_sout = [newsem(f"s_out{i}") for i in range(4)]
    sem_slab = newsem("s_slab")
    sem_skpe = newsem("s_kpe")
    sem_sS = newsem("s_Sld")
    sem_svn = newsem("s_vnew")
    out_toks = []

    def dbg(nm, buf):
        if debug:
            out_toks.append(P.dma(SP, dbg_t[nm], buf.t[:], sem_dbg, reads=buf.all()))

    evac_flip = [0]

    def ew_engine():
        evac_flip[0] ^= 1
        return ACT if evac_flip[0] else DVE

    def act(out, in_, func, reads, writes, bias=None, scale=None):
        kw = {}
        if bias is not None:
            kw["bias"] = bias
        if scale is not None:
            kw["scale"] = scale
        return P.op(ACT, lambda e: e.activation(out=out, in_=in_, func=func, **kw), reads, writes)

    def copy_any(out, in_, reads, writes, eng=None):
        eng = eng or ew_engine()
        if eng == ACT:
            return P.op(ACT, lambda e: e.activation(out=out, in_=in_, func=AF.Copy), reads, writes)
        return P.op(DVE, lambda e: e.tensor_copy(out=out, in_=in_), reads, writes)

    def tt(out, in0, in1, op, reads, writes):
        return P.op(DVE, lambda e: e.tensor_tensor(out=out, in0=in0, in1=in1, op=op), reads, writes)

    def ts(out, in0, s1, s2, op0, op1, reads, writes):
        if s2 is None:
            return P.op(DVE, lambda e: e.tensor_scalar(out=out, in0=in0, scalar1=s1, scalar2=None, op0=op0), reads, writes)
        return P.op(DVE, lambda e: e.tensor_scalar(out=out, in0=in0, scalar1=s1, scalar2=s2, op0=op0, op1=op1), reads, writes)

    def stt(out, in0, scalar, in1, op0, op1, reads, writes):
        return P.op(DVE, lambda e: e.scalar_tensor_tensor(out=out, in0=in0, scalar=scalar, in1=in1, op0=op0, op1=op1), reads, writes)

    def mm(out, lhsT, rhs, start, stop, reads, writes):
        return P.op(PE, lambda e: e.matmul(out, lhsT, rhs, start=start, stop=stop, skip_group_check=True), reads, writes)

    class WStream:
        def __init__(self):
            self.blocks = []
            self.next_load = 0
            self.next_use = 0
            self.slot_last = {}
            self.cnt = {}

        def add(self, w2d, r0, nrows, c0, ncols, ring):
            nkc = nrows // 128
            v = w2d[r0:r0 + nrows, c0:c0 + ncols].rearrange("(kc p) n -> p kc n", p=128)
            k = self.cnt.get(ring, 0)
            self.cnt[ring] = k + 1
            self.blocks.append((v, nkc, ncols, k % ring))

        def _view(self, k):
            v, nkc, ncols, sl = self.blocks[k]
            return (wring_alt if ncols > 512 else wring)[sl]

        def _issue(self, k):
            v, nkc, ncols, sl = self.blocks[k]
            P.dma(POOL, self._view(k).t[:, 0:nkc, 0:ncols], v, wsem[sl], writes=[wring[sl].r])

        def prefetch(self):
            while self.next_load < len(self.blocks):
                sl = self.blocks[self.next_load][3]
                if self.slot_last.get(sl, -1) >= self.next_use:
                    break
                self._issue(self.next_load)
                self.slot_last[sl] = self.next_load
                self.next_load += 1

        def get(self):
            self.prefetch()
            k = self.next_use
            assert k < self.next_load
            self.next_use += 1
            return self._view(k)

    ws = WStream()

    def layer_blocks(l, ring):
        wi = w_in[l]
        n0 = len(ws.blocks)
        _add = ws.add
        ws_add = lambda *a: _add(*a, ring)
        ws_add(wi, 0, 2048, C_Q, 512)
        ws_add(wi, 0, 2048, C_KV, 512)
        ws_add(wi, 0, 2048, C_KPE, 64)
        ws_add(wi, 0, 2048, C_KPESW, 64)
        ws_add(w_uk[l], 0, 512, 0, 2048)
        ws_add(w_uv[l], 0, 512, 0, 2048)
        for g in range(4):
            ws_add(w_uq[l], 0, 512, g * 1024, 1024)
        for g in range(4):
            ws_add(wi, 0, 2048, C_HQ + g * 512, 512)
            ws_add(wi, 0, 2048, C_HF + g * 512, 512)
            ws_add(wi, 0, 2048, C_HI + g * 512, 512)
            ws_add(wi, 0, 2048, C_HG + g * 512, 512)
        for g in range(4):
            ws_add(wi, 0, 2048, C_GA + g * 512, 512)
        for g in range(4):
            ws_add(wi, 0, 2048, C_GB + g * 512, 512)
        for g in range(4):
            ws_add(w_oa[l], 0, 2048, g * 512, 512)
            ws_add(w_ob[l], 0, 2048, g * 512, 512)
        for g in range(4):
            ws_add(w_out[l], 0, 2048, g * 512, 512)
        for q in range(4):
            for g in range(4):
                ws_add(w_up[l], 0, 2048, q * 2048 + g * 512, 512)
            for g in range(4):
                ws_add(w_down[l], q * 2048, 2048, g * 512, 512)
        return len(ws.blocks) - n0

    nblocks_layer = 0
    for it in range(n_tiles):
        for l in range(n_layers):
            nblocks_layer = layer_blocks(l, nslot + 1 if (extra_slot and it > 0) else nslot)

    c_toks = []
    ident_f = pc.ntmp
    c_toks.append(P.dma(SP, gains.t[:], gains_d.rearrange("l p g -> p l g"), sem_const, writes=[gains.r]))
    c_toks.append(P.dma(SP, pc.m1.t[:], cm1, sem_const, writes=[pc.m1.r]))
    c_toks.append(P.dma(SP, pc.m2.t[:], cm2, sem_const, writes=[pc.m2.r]))
    c_toks.append(P.dma(SP, pc.mask.t[:], cmask, sem_const, writes=[pc.mask.r]))
    c_toks.append(P.dma(SP, ident_f.t[:, 0:128], cident, sem_const, writes=[ident_f.r]))
    if with_sample:
        c_toks.append(P.dma(SP, sc_.m1.t[:], cm1s, sem_const, writes=[sc_.m1.r]))
        c_toks.append(P.dma(SP, sc_.m2.t[:], cm2s, sem_const, writes=[sc_.m2.r]))
        c_toks.append(P.dma(SP, sc_.mask.t[:], cmasks, sem_const, writes=[sc_.mask.r]))
        c_toks.append(P.dma(SP, sc_.ropec.t[:], rope_s[0], sem_const, writes=[sc_.ropec.r]))
        c_toks.append(P.dma(SP, sc_.ropes.t[:], rope_s[1], sem_const, writes=[sc_.ropes.r]))
    for e_ in (DVE, ACT, PE):
        P.wait_all(e_, c_toks)
    P.op(DVE, lambda e: e.tensor_copy(out=ident.t[:], in_=ident_f.t[:, 0:128]), [ident_f.r], [ident.r])
    P.op(DVE, lambda e: e.memset(ones_bf.t[:], 1.0), [], [ones_bf.r])
    P.op(DVE, lambda e: e.memset(zeros_bf.t[:], 0.0), [], [zeros_bf.r])

    def rms_rstd(c, src_chunks, n, nfeat, reads):
        bank = psum[7]
        nch_ = len(src_chunks)
        for i, a in enumerate(src_chunks):
            s_ = c.sq[i % 2]
            rd = [reads[i]] if len(reads) == nch_ and nch_ > 1 else reads
            act(s_.t[:, 0:n], a, AF.Square, rd, [s_.r])
            mm(bank.t[:, 0:n], ones_bf.t[:, :], s_.t[:, 0:n], i == 0, i == nch_ - 1, [s_.r, ones_bf.r], [bank.r])
        act(c.ntmp.t[:, 0:n], bank.t[:, 0:n], AF.Sqrt, [bank.r], [c.ntmp.r], bias=float(EPS), scale=1.0 / nfeat)
        P.op(DVE, lambda e: e.reciprocal(out=c.rstd.t[:, 0:n], in_=c.ntmp.t[:, 0:n]), [c.ntmp.r], [c.rstd.r])

    def gemm_fm(slot, nchunks, rhs_buf, nk, n, consume, col0=0, banks=(0, 1, 2, 3)):
        for j in range(nchunks):
            bank = psum[banks[j % len(banks)]]
            for kc in range(nk):
                mm(bank.t[:, 0:n], slot.t[:, kc, col0 + j * 128: col0 + (j + 1) * 128], rhs_buf.t[:, kc, 0:n],
                   kc == 0, kc == nk - 1, [slot.r, rhs_buf.ck(kc)], [bank.r])
            consume(j, bank)

    def gemm_tm(slot, c, consume, banks=(0, 1, 2, 3)):
        for tb in range(c.nblk):
            bank = psum[banks[tb % len(banks)]]
            for kc in range(NKC):
                mm(bank.t[0:c.blk, 0:512], c.hT.t[:, kc, tb * c.blk:(tb + 1) * c.blk], slot.t[:, kc, 0:512],
                   kc == 0, kc == NKC - 1, [slot.r, c.hT.ck(kc)], [bank.r])
            consume(tb, bank)

    def layer_gen(c, l, it):
        n = c.n
        G = gains.t
        isp = c.kind == "p"
        t0 = it * TT

        rms_rstd(c, [c.xT.t[:, k, :] for k in range(NKC)], n, D_MODEL, c.xT.all())
        for k in range(NKC):
            stt(c.hT.t[:, k, :], c.xT.t[:, k, :], G[:, l, G_PRE + k:G_PRE + k + 1], c.rstd.t[:, 0:n], ALU.mult, ALU.mult,
                [c.xT.ck(k), c.rstd.r, gains.r], [c.hT.ck(k)])

        def lat_block(slot, dst_bf, gcol, out_dram=None, osem=None):
            def cons(j, bank):
                copy_any(c.tmp4.t[:, j, :], bank.t[:, 0:n], [bank.r], [c.tmp4.r])
            gemm_fm(slot, 4, c.hT, NKC, n, cons)
            rms_rstd(c, [c.tmp4.t[:, k, :] for k in range(4)], n, 512, [c.tmp4.r])
            for k in range(4):
                stt(c.tmp4.t[:, k, :], c.tmp4.t[:, k, :], G[:, l, gcol + k:gcol + k + 1], c.rstd.t[:, 0:n], ALU.mult, ALU.mult,
                    [c.tmp4.r, c.rstd.r, gains.r], [c.tmp4.r])
                copy_any(dst_bf.t[:, k, :], c.tmp4.t[:, k, :], [c.tmp4.r], [dst_bf.r])
            if out_dram is not None:
                out_toks.append(P.dma(SP, out_dram, c.tmp4.t[:], osem, reads=[c.tmp4.r]))

        slot = yield
        lat_block(slot, c.qlatn, G_QN)
        slot = yield
        if isp:
            lat_block(slot, c.ckvbf, G_KVN, ckvT_p[l, :, t0:t0 + TT].rearrange("(k p) t -> p k t", p=128), sem_out_ckv)
        else:
            lat_block(slot, c.ckvbf, G_KVN, ckvT_s[l].rearrange("(k p) t -> p k t", p=128), sem_sout[1])

        if isp and l == 0:
            P.dma(SP, c.ropec.t[:], rope_p[0, :, t0:t0 + TT], sem_rc, writes=[c.ropec.r])
            P.dma(SP, c.ropes.t[:], rope_p[1, :, t0:t0 + TT], sem_rs, writes=[c.ropes.r])
        s1 = yield
        b1 = psum[0]
        for kc in range(NKC):
            mm(b1.t[0:64, 0:n], s1.t[:, kc, 0:64], c.hT.t[:, kc, :], kc == 0, kc == NKC - 1, [s1.r, c.hT.ck(kc)], [b1.r])
        tt(c.tmp4.t[0:64, 0, :], b1.t[0:64, 0:n], c.ropec.t[:, :], ALU.mult, [b1.r, c.ropec.r], [c.tmp4.r])
        s2 = yield
        b2 = psum[1]
        for kc in range(NKC):
            mm(b2.t[0:64, 0:n], s2.t[:, kc, 0:64], c.hT.t[:, kc, :], kc == 0, kc == NKC - 1, [s2.r, c.hT.ck(kc)], [b2.r])
        tt(c.tmp4.t[0:64, 1, :], b2.t[0:64, 0:n], c.ropes.t[:, :], ALU.mult, [b2.r, c.ropes.r], [c.tmp4.r])
        tt(c.tmp4.t[0:64, 2, :], c.tmp4.t[0:64, 0, :], c.tmp4.t[0:64, 1, :], ALU.add, [c.tmp4.r], [c.tmp4.r])
        if isp:
            copy_any(kpeT[l].t[:, t0:t0 + TT], c.tmp4.t[0:64, 2, :], [c.tmp4.r], [kpeT[l].r], eng=ACT)
            out_toks.append(P.dma(SP, kpeT_p[l, :, t0:t0 + TT], c.tmp4.t[0:64, 2, :], sem_out_kpe, reads=[c.tmp4.r]))
        else:
            copy_any(c.kpeS.t[:, 2048:2048 + n], c.tmp4.t[0:64, 2, :], [c.tmp4.r], [c.kpeS.r], eng=ACT)
            out_toks.append(P.dma(SP, kpeT_s[l], c.tmp4.t[0:64, 2, :], sem_sout[2], reads=[c.tmp4.r]))

        slot = yield
        kv_st = []
        if isp:
            for h in range(NH):
                bank = psum[h % 8]
                for kc in range(4):
                    mm(bank.t[:, 0:n], slot.t[:, kc, h * 128:(h + 1) * 128], c.ckvbf.t[:, kc, :], kc == 0, kc == 3,
                       [slot.r, c.ckvbf.r], [bank.r])
                sg = stage[h % NSTG]
                copy_any(sg.t[:, :], bank.t[:, 0:n], [bank.r], [sg.r])
                kv_st.append(P.dma(SP, Kc[l, h, :, t0:t0 + TT], sg.t[:, :], sem_kvst[h % NSTG], reads=[sg.r], writes=[r_Kc[l]]))
        else:
            for h in range(NH):
                bank = psum[h % 8]
                for kc in range(4):
                    mm(bank.t[:, 0:n], slot.t[:, kc, h * 128:(h + 1) * 128], c.ckvbf.t[:, kc, :], kc == 0, kc == 3,
                       [slot.r, c.ckvbf.r], [bank.r])
                copy_any(c.knew.t[:, h, :], bank.t[:, 0:n], [bank.r], [c.knew.r])
            cnt = 0
            for sl in range(PAST // 512):
                P.dma(POOL, slab.t[:], cache_ckvT[l, :, sl * 512:(sl + 1) * 512].rearrange("(k p) t -> p k t", p=128),
                      sem_slab, writes=[slab.r])
                for h in range(NH):
                    bank = psum[cnt % 8]
                    for kc in range(4):
                        mm(bank.t[:, 0:512], slot.t[:, kc, h * 128:(h + 1) * 128], slab.t[:, kc, :], kc == 0, kc == 3,
                           [slot.r, slab.r], [bank.r])
                    sg = stage[cnt % NSTG]
                    copy_any(sg.t[:, :], bank.t[:, 0:512], [bank.r], [sg.r])
                    kv_st.append(P.dma(SP, Kcs[l, h, :, sl * 512:(sl + 1) * 512], sg.t[:, :], sem_kvst[cnt % NSTG],
                                       reads=[sg.r], writes=[r_Kcs[l]]))
                    cnt += 1
        slot = yield
        cnt = 0
        if isp:
            for tb in range(TT // 128):
                for g in range(4):
                    bank = psum[cnt % 8]
                    for kc in range(4):
                        mm(bank.t[:, 0:512], c.ckvbf.t[:, kc, tb * 128:(tb + 1) * 128], slot.t[:, kc, g * 512:(g + 1) * 512],
                           kc == 0, kc == 3, [slot.r, c.ckvbf.r], [bank.r])
                    sg = stage[cnt % NSTG]
                    copy_any(sg.t[:, :], bank.t[:, 0:512], [bank.r], [sg.r])
                    kv_st.append(P.dma(SP, Vc[l, g, t0 + tb * 128:t0 + (tb + 1) * 128, :], sg.t[:, :], sem_kvst[cnt % NSTG],
                                       reads=[sg.r], writes=[r_Vc[l]]))
                    cnt += 1
        else:
            for g in range(4):
                bank = psum[cnt % 8]
                for kc in range(4):
                    mm(bank.t[0:n, 0:512], c.ckvbf.t[:, kc, 0:n], slot.t[:, kc, g * 512:(g + 1) * 512],
                       kc == 0, kc == 3, [slot.r, c.ckvbf.r], [bank.r])
                sg = stage[cnt % NSTG]
                copy_any(sg.t[0:n, :], bank.t[0:n, 0:512], [bank.r], [sg.r])
                kv_st.append(P.dma(SP, Vcs[l, g, PAST:PAST + n, :], sg.t[0:n, :], sem_kvst[cnt % NSTG],
                                   reads=[sg.r], writes=[r_Vcs[l]]))
                cnt += 1
            for sl in range(PAST // 512):
                P.dma(POOL, slab.t[:], cache_ckvT[l, :, sl * 512:(sl + 1) * 512].rearrange("(k p) t -> p k t", p=128),
                      sem_slab, writes=[slab.r])
                for tb in range(4):
                    for g in range(4):
                        bank = psum[cnt % 8]
                        for kc in range(4):
                            mm(bank.t[:, 0:512], slab.t[:, kc, tb * 128:(tb + 1) * 128], slot.t[:, kc, g * 512:(g + 1) * 512],
                               kc == 0, kc == 3, [slot.r, slab.r], [bank.r])
                        sg = stage[cnt % NSTG]
                        copy_any(sg.t[:, :], bank.t[:, 0:512], [bank.r], [sg.r])
                        r0 = sl * 512 + tb * 128
                        kv_st.append(P.dma(SP, Vcs[l, g, r0:r0 + 128, :], sg.t[:, :], sem_kvst[cnt % NSTG],
                                           reads=[sg.r], writes=[r_Vcs[l]]))
                        cnt += 1
        last = {}
        for tk in kv_st:
            last[id(tk.dsem)] = tk
        P.wait_all(SP, list(last.values()))

        def q_proj(wq, hh, qn_ap, qr_ap):
            c0 = hh * 256
            bq, bp, bs = psum[6], psum[7], psum[0]
            for kc in range(4):
                mm(bq.t[:, 0:n], wq.t[:, kc, c0:c0 + 128], c.qlatn.t[:, kc, :], kc == 0, kc == 3, [wq.r, c.qlatn.r], [bq.r])
            for kc in range(4):
                mm(bp.t[0:64, 0:n], wq.t[:, kc, c0 + 128:c0 + 192], c.qlatn.t[:, kc, :], kc == 0, kc == 3, [wq.r, c.qlatn.r], [bp.r])
            for kc in range(4):
                mm(bs.t[0:64, 0:n], wq.t[:, kc, c0 + 192:c0 + 256], c.qlatn.t[:, kc, :], kc == 0, kc == 3, [wq.r, c.qlatn.r], [bs.r])
            return bq, bp, bs

        hcnt = 0
        for g in range(4):
            wq = yield
            if isp:
                Lk = t0 + TT
                nkb = Lk // 128
                P.dma(SP, vbuf.t[:, 0:nkb, :], Vc[l, g, 0:Lk, :].rearrange("(kb p) c -> p kb c", p=128), sem_vld,
                      reads=[r_Vc[l]], writes=[vbuf.r])

                def kload(hh_, hc_):
                    kbuf_ = Kbuf[hc_ % 2]
                    P.dma(SP, kbuf_.t[:, 0:Lk], Kc[l, g * 4 + hh_, :, 0:Lk], sem_kv[hc_ % 2], reads=[r_Kc[l]], writes=[kbuf_.r])

                def qprep(hh_, hc_):
                    qnb_, qrb_ = c.qn[hc_ % 2], c.qr[hc_ % 2]
                    c0 = hh_ * 256
                    bq, bp = psum[6], psum[7]
                    bs = psum[2] if hc_ % 2 == 0 else psum[4]
                    for kc in range(4):
                        mm(bq.t[:, 0:n], wq.t[:, kc, c0:c0 + 128], c.qlatn.t[:, kc, :], kc == 0, kc == 3, [wq.r, c.qlatn.r], [bq.r])
                    for kc in range(4):
                        mm(bp.t[0:64, 0:n], wq.t[:, kc, c0 + 128:c0 + 192], c.qlatn.t[:, kc, :], kc == 0, kc == 3, [wq.r, c.qlatn.r], [bp.r])
                    for kc in range(4):
                        mm(bs.t[0:64, 0:n], wq.t[:, kc, c0 + 192:c0 + 256], c.qlatn.t[:, kc, :], kc == 0, kc == 3, [wq.r, c.qlatn.r], [bs.r])
                    copy_any(qnb_.t[:, :], bq.t[:, 0:n], [bq.r], [qnb_.r], eng=ACT)
                    tt(c.tmp4.t[0:64, 0, :], bp.t[0:64, 0:n], c.ropec.t[:, :], ALU.mult, [bp.r, c.ropec.r], [c.tmp4.r])
                    tt(c.tmp4.t[0:64, 1, :], bs.t[0:64, 0:n], c.ropes.t[:, :], ALU.mult, [bs.r, c.ropes.r], [c.tmp4.r])
                    tt(qrb_.t[:, :], c.tmp4.t[0:64, 0, :], c.tmp4.t[0:64, 1, :], ALU.add, [c.tmp4.r], [qrb_.r])

                kload(0, hcnt)
                qprep(0, hcnt)
                for hh in range(4):
                    h = g * 4 + hh
                    kbuf = Kbuf[hcnt % 2]
                    qnb, qrb = c.qn[hcnt % 2], c.qr[hcnt % 2]
                    bo, bsum = (psum[2], psum[3]) if hcnt % 2 == 0 else (psum[4], psum[5])
                    if hh + 1 < 4:
                        kload(hh + 1, hcnt + 1)

                    def scores(kb):
                        d = kb - it * 4
                        qs = max(d, 0) * 128
                        nn = n - qs
                        bsc = psum[kb % 2]
                        mm(bsc.t[:, 0:nn], kbuf.t[:, kb * 128:(kb + 1) * 128], qnb.t[:, qs:n], True, False,
                           [kbuf.r, qnb.r], [bsc.r])
                        mm(bsc.t[:, 0:nn], kpeT[l].t[0:64, kb * 128:(kb + 1) * 128], qrb.t[0:64, qs:n], False, True,
                           [kpeT[l].r, qrb.r], [bsc.r])

                    scores(0)
                    for kb in range(nkb):
                        if kb + 1 < nkb:
                            scores(kb + 1)
                        if kb == min(1, nkb - 1) and hh + 1 < 4:
                            qprep(hh + 1, hcnt + 1)
                        d = kb - it * 4
                        qs = max(d, 0) * 128
                        nn = n - qs
                        bsc = psum[kb % 2]
                        pb = c.pT[kb % 3]
                        act(pb.t[:, 0:nn], bsc.t[:, 0:nn], AF.Exp, [bsc.r], [pb.r], scale=ATTN_SCALE)
                        if d >= 0:
                            P.op(DVE, lambda e, pb=pb: e.memset(pb.t[64:128, 0:64], 0.0), [], [pb.r])
                        mm(bo.t[:, qs:n], vbuf.t[:, kb, hh * 128:(hh + 1) * 128], pb.t[:, 0:nn], kb == 0, kb == nkb - 1,
                           [vbuf.r, pb.r], [bo.r])
                        mm(bsum.t[:, qs:n], ones_bf.t[:, :], pb.t[:, 0:nn], kb == 0, kb == nkb - 1,
                           [ones_bf.r, pb.r], [bsum.r])
                    P.op(DVE, lambda e, bsum=bsum: e.reciprocal(out=c.recip.t[:, 0:n], in_=bsum.t[:, 0:n]), [bsum.r], [c.recip.r])
                    tt(c.oa.t[:, h, :], bo.t[:, 0:n], c.recip.t[:, 0:n], ALU.mult, [bo.r, c.recip.r], [c.oa.r])
                    hcnt += 1
            else:
                for hh in range(4):
                    bq, bp, bs = q_proj(wq, hh, None, None)
                    copy_any(c.qn4.t[:, hh, :], bq.t[:, 0:n], [bq.r], [c.qn4.r], eng=ACT)
                    tt(c.tmp4.t[0:64, 0, :], bp.t[0:64, 0:n], c.ropec.t[:, :], ALU.mult, [bp.r, c.ropec.r], [c.tmp4.r])
                    tt(c.tmp4.t[0:64, 1, :], bs.t[0:64, 0:n], c.ropes.t[:, :], ALU.mult, [bs.r, c.ropes.r], [c.tmp4.r])
                    tt(c.qr4.t[:, hh, :], c.tmp4.t[0:64, 0, :], c.tmp4.t[0:64, 1, :], ALU.add, [c.tmp4.r], [c.qr4.r])
                bo, bsum = psum[2], psum[3]
                mm(bo.t[:, 0:128], zeros_bf.t[:, 0:128], zeros_bf.t[:, 0:128], True, False, [zeros_bf.r], [bo.r])
                mm(bsum.t[:, 0:128], zeros_bf.t[:, 0:128], zeros_bf.t[:, 0:128], True, False, [zeros_bf.r], [bsum.r])
                pcnt = 0
                for half in range(2):
                    P.dma(POOL, c.kpeS.t[:, 0:2048], cache_kpeT[l, :, half * 2048:(half + 1) * 2048], sem_skpe,
                          writes=[c.kpeS.r])
                    P.dma(SP, vbuf.t[:, 0:16, :], Vcs[l, g, half * 2048:(half + 1) * 2048, :].rearrange("(kb p) c -> p kb c", p=128),
                          sem_vld, reads=[r_Vcs[l]], writes=[vbuf.r])
                    for hh in range(4):
                        h = g * 4 + hh
                        kbuf = Kbuf[hcnt % 2]
                        P.dma(SP, kbuf.t[:, 0:2048], Kcs[l, h, :, half * 2048:(half + 1) * 2048], sem_kv[hcnt % 2],
                              reads=[r_Kcs[l]], writes=[kbuf.r])
                        hcnt += 1

                        def sscores(kb_, pc_):
                            bsc_ = psum[pc_ % 2]
                            mm(bsc_.t[:, 0:n], kbuf.t[:, kb_ * 128:(kb_ + 1) * 128], c.qn4.t[:, hh, :], True, False,
                               [kbuf.r, c.qn4.r], [bsc_.r])
                            mm(bsc_.t[:, 0:n], c.kpeS.t[0:64, kb_ * 128:(kb_ + 1) * 128], c.qr4.t[0:64, hh, :], False, True,
                               [c.kpeS.r, c.qr4.r], [bsc_.r])

                        sscores(0, pcnt)
                        for kb in range(16):
                            if kb + 1 < 16:
                                sscores(kb + 1, pcnt + 1)
                            bsc = psum[pcnt % 2]
                            pb = c.pT[pcnt % 3]
                            act(pb.t[:, 0:n], bsc.t[:, 0:n], AF.Exp, [bsc.r], [pb.r], scale=ATTN_SCALE)
                            mm(bo.t[:, hh * n:(hh + 1) * n], vbuf.t[:, kb, hh * 128:(hh + 1) * 128], pb.t[:, 0:n], False, False,
                               [vbuf.r, pb.r], [bo.r])
                            mm(bsum.t[:, hh * n:(hh + 1) * n], ones_bf.t[:, :], pb.t[:, 0:n], False, False,
                               [ones_bf.r, pb.r], [bsum.r])
                            pcnt += 1
                P.dma(SP, c.vnew.t[:, :], Vcs[l, g, PAST:PAST + n, :], sem_svn, reads=[r_Vcs[l]], writes=[c.vnew.r])
                for hh in range(4):
                    h = g * 4 + hh
                    bsc = psum[pcnt % 2]
                    mm(bsc.t[0:n, 0:n], c.knew.t[:, h, :], c.qn4.t[:, hh, :], True, False, [c.knew.r, c.qn4.r], [bsc.r])
                    mm(bsc.t[0:n, 0:n], c.kpeS.t[0:64, 2048:2048 + n], c.qr4.t[0:64, hh, :], False, True,
                       [c.kpeS.r, c.qr4.r], [bsc.r])
                    pb = c.pT[pcnt % 3]
                    act(pb.t[0:n, 0:n], bsc.t[0:n, 0:n], AF.Exp, [bsc.r], [pb.r], scale=ATTN_SCALE)
                    mm(bo.t[:, hh * n:(hh + 1) * n], c.vnew.t[0:n, hh * 128:(hh + 1) * 128], pb.t[0:n, 0:n], False, False,
                       [c.vnew.r, pb.r], [bo.r])
                    mm(bsum.t[:, hh * n:(hh + 1) * n], ones_bf.t[0:n, :], pb.t[0:n, 0:n], False, False,
                       [ones_bf.r, pb.r], [bsum.r])
                    pcnt += 1
                P.op(DVE, lambda e, bsum=bsum: e.reciprocal(out=c.recip.t[:, 0:128], in_=bsum.t[:, 0:128]), [bsum.r], [c.recip.r])
                tt(c.oa.t[:, g * 4:(g + 1) * 4, :], bo.t[:, 0:128].rearrange("p (h t) -> p h t", h=4),
                   c.recip.t[:, 0:128].rearrange("p (h t) -> p h t", h=4), ALU.mult, [bo.r, c.recip.r], [c.oa.r])
        dbg("d_oa" if isp else "s_oa", c.oa)

        blk, nch, L = c.blk, c.nch, c.L
        for g in range(4):
            if not isp:
                P.dma(SP, c.Sg.t[:], state_s[l, g * 4:(g + 1) * 4].rearrange("h k v -> k h v"), sem_sS, writes=c.Sg_res)
            slot = yield
            if isp and g == 0:
                if it == 0:
                    P.op(DVE, lambda e: e.memset(Sbuf.t[:], 0.0), [], Sres)
                else:
                    P.dma(SP, Sbuf.t[:], Sc[l].rearrange("p (h v) -> p h v", h=NH), sem_Sld, reads=[r_Sc[l]], writes=Sres)
            if isp and l > 0:
                cs_ = slice(g * 512, (g + 1) * 512)
                P.dma(SP, lbt.t[:], hg_lb[l, cs_].partition_broadcast(128), sem_lb, writes=[lbt.r])
                P.dma(SP, omlt.t[:], hg_lb[0, cs_].partition_broadcast(128), sem_oml, writes=[omlt.r])
                tt(lbt.t[:], lbt.t[:], omlt.t[:], ALU.subtract, [lbt.r, omlt.r], [lbt.r])
                act(lbt.t[:], lbt.t[:], AF.Sigmoid, [lbt.r], [lbt.r])
                ts(omlt.t[:], lbt.t[:], -1.0, 1.0, ALU.mult, ALU.add, [lbt.r], [omlt.r])

            def cons_q(j, bank):
                act(c.hqT.t[:, j, :], bank.t[:, 0:n], AF.Silu, [bank.r], [c.hqT.r])
            gemm_fm(slot, 4, c.hT, NKC, n, cons_q)
            slot = yield

            def cons_f(tb, bank):
                lf = c.logf.t[:, tb, :]
                act(lf, bank.t[0:blk, 0:512], AF.Sigmoid, [bank.r], [c.logf.r])
                if l > 0:
                    tt(lf, lf, omlt.t[0:blk, :], ALU.mult, [c.logf.r, omlt.r], [c.logf.r])
                    tt(lf, lf, lbt.t[0:blk, :], ALU.add, [c.logf.r, lbt.r], [c.logf.r])
                ts(c.kkb.t[:, tb, :], lf, -1.0, 1.0, ALU.mult, ALU.add, [c.logf.r], [c.kkb.r])
                act(lf, lf, AF.Ln, [c.logf.r], [c.logf.r])
            gemm_tm(slot, c, cons_f)
            slot = yield

            def cons_v(tb, bank):
                copy_any(c.vtb.t[:, tb, :], bank.t[0:blk, 0:512], [bank.r], [c.vtb.r])
            gemm_tm(slot, c, cons_v)
            slot = yield

            def cons_g(j, bank):
                act(c.hgT.t[:, j, :], bank.t[:, 0:n], AF.Silu, [bank.r], [c.hgT.r])
            gemm_fm(slot, 4, c.hT, NKC, n, cons_g)
            bcnt = 0
            W4 = 4 * blk
            for tb in range(c.nblk):
                if isp:
                    bO = psum[4 + (tb % 2)]
                    mm(bO.t[:, 0:512], zeros_bf.t[:, 0:128], zeros_bf.t[:, 0:512], True, False, [zeros_bf.r], [bO.r])
                    tsl = slice(tb * 128, (tb + 1) * 128)
                    lf = c.logf.t[:, tb, :]
                    bBm, bEk, bSm, bD0, bD1 = psum[0], psum[1], psum[2], psum[3], psum[6]
                    Sg = Sbuf.t[:, g * 4:(g + 1) * 4, :]
                    Sg_res = [Sres[g * 4 + i] for i in range(4)]
                    eBs4 = c.eBs.t[:, :].rearrange("p (h c t) -> p h c t", h=4, c=2)
                    HS = [slice(i * 128, (i + 1) * 128) for i in range(4)]
                    for hh in range(4):
                        mm(bBm.t[:, HS[hh]], lf[:, HS[hh]], c.m1.t[:, 0:128], True, True, [c.logf.r, c.m1.r], [bBm.r])
                    for hh in range(4):
                        mm(bSm.t[:, hh * 4:(hh + 1) * 4], lf[:, HS[hh]], c.m1.t[:, 128:132], True, True, [c.logf.r, c.m1.r], [bSm.r])
                    mm(bEk.t[:, 0:512], c.m2.t[:, :], lf[:, 0:512], True, True, [c.logf.r, c.m2.r], [bEk.r])
                    act(c.bufA.t[:, :], bBm.t[:, 0:512], AF.Exp, [bBm.r], [c.bufA.r])
                    act(c.eBs.t[:, :], bSm.t[:, 0:16], AF.Exp, [bSm.r], [c.eBs.r])
                    act(c.bufB.t[:, :], bEk.t[:, 0:512], AF.Exp, [bEk.r], [c.bufB.r])
                    tt(c.eL.t[:, :, :], eBs4[:, :, :, 0], eBs4[:, :, :, 1], ALU.mult, [c.eBs.r], [c.eL.r])
                    tt(c.qtil4.t[:, :, :], c.hqT.t[:, :, tsl], c.bufA.t[:, :].rearrange("p (h t) -> p h t", h=4), ALU.mult,
                       [c.hqT.r, c.bufA.r], [c.qtil4.r])
                    tt(c.khat4.t[:, :], c.kkb.t[:, tb, :], c.bufB.t[:, :], ALU.mult, [c.kkb.r, c.bufB.r], [c.khat4.r])
                    for hh in range(4):
                        mm(bEk.t[:, HS[hh]], c.khat4.t[:, HS[hh]], ident.t[:, :], True, True, [c.khat4.r, ident.r], [bEk.r])
                    copy_any(c.bufA.t[:, :], bEk.t[:, 0:512], [bEk.r], [c.bufA.r], eng=ACT)
                    for hh in range(4):
                        mm(bBm.t[:, HS[hh]], c.bufA.t[:, HS[hh]], c.qtil4.t[:, hh, :], True, True, [c.bufA.r, c.qtil4.r], [bBm.r])
                    tt(c.bufB.t[:, :].rearrange("p (h t) -> p h t", h=4), bBm.t[:, 0:512].rearrange("p (h t) -> p h t", h=4),
                       c.mask.t[:, :].unsqueeze(1).broadcast_to([128, 4, 128]), ALU.mult, [bBm.r, c.mask.r], [c.bufB.r])
                    for hh in range(4):
                        mm(bO.t[:, HS[hh]], c.vtb.t[:, tb, HS[hh]], c.bufB.t[:, HS[hh]], False, False, [c.vtb.r, c.bufB.r], [bO.r])
                    for hh in range(4):
                        mm(bD0.t[:, HS[hh]], c.khat4.t[0:64, HS[hh]], c.vtb.t[0:64, tb, HS[hh]], True, True,
                           [c.khat4.r, c.vtb.r], [bD0.r])
                        mm(bD1.t[:, HS[hh]], c.khat4.t[64:128, HS[hh]], c.vtb.t[64:128, tb, HS[hh]], True, True,
                           [c.khat4.r, c.vtb.r], [bD1.r])
                    for ch in range(2):
                        bD = bD0 if ch == 0 else bD1
                        em = eBs4[:, :, ch, 0:1].broadcast_to([128, 4, 128])
                        elm = eBs4[:, :, ch, 1:2].broadcast_to([128, 4, 128])
                        el = c.eL.t[:, :, ch:ch + 1].broadcast_to([128, 4, 128])
                        tt(c.Smb4.t[:, :, :], Sg, em, ALU.mult, Sg_res + [c.eBs.r], [c.Smb4.r])
                        tt(c.Sd4.t[:, :, :], Sg, el, ALU.mult, Sg_res + [c.eL.r], [c.Sd4.r])
                        for hh in range(4):
                            mm(bO.t[:, hh * 128 + ch * 64: hh * 128 + (ch + 1) * 64], c.Smb4.t[:, hh, :],
                               c.qtil4.t[:, hh, ch * 64:(ch + 1) * 64], False, False, [c.Smb4.r, c.qtil4.r], [bO.r])
                        tt(Sg, bD.t[:, 0:512].rearrange("p (h v) -> p h v", h=4), elm, ALU.mult, [bD.r, c.eBs.r], Sg_res)
                        tt(Sg, Sg, c.Sd4.t[:, :, :], ALU.add, Sg_res + [c.Sd4.r], Sg_res)
                else:
                    bO = psum[4 + (tb % 2)]
                    mm(bO.t[:, 0:W4], zeros_bf.t[:, 0:128], zeros_bf.t[:, 0:W4], True, False, [zeros_bf.r], [bO.r])
                    for hh in range(4):
                        h = g * 4 + hh
                        i2 = bcnt % 2
                        cs = slice(hh * 128, (hh + 1) * 128)
                        tsl = slice(tb * blk, (tb + 1) * blk)
                        ocs = slice(hh * blk, (hh + 1) * blk)
                        bB, bE, bK, bA = psum[0], psum[1], psum[2], psum[3]
                        bDs = (psum[6], psum[7])
                        eBb, eEb, qtb, khb, ktb, atb = c.expB[i2], c.expE[i2], c.qtil[i2], c.khat[i2], c.ktil[i2], c.ATm[i2]
                        nb2 = blk + 2 * nch
                        mm(bB.t[:, 0:nb2], c.logf.t[:, tb, cs], c.m1.t[:, :], True, True, [c.logf.r, c.m1.r], [bB.r])
                        mm(bE.t[0:blk, 0:128], c.m2.t[:, :], c.logf.t[:, tb, cs], True, True, [c.logf.r, c.m2.r], [bE.r])
                        act(eBb.t[:, :], bB.t[:, 0:nb2], AF.Exp, [bB.r], [eBb.r])
                        act(eEb.t[:, :], bE.t[0:blk, 0:128], AF.Exp, [bE.r], [eEb.r])
                        tt(qtb.t[:, :], c.hqT.t[:, hh, tsl], eBb.t[:, 0:blk], ALU.mult, [c.hqT.r, eBb.r], [qtb.r])
                        tt(khb.t[:, :], c.kkb.t[:, tb, cs], eEb.t[:, :], ALU.mult, [c.kkb.r, eEb.r], [khb.r])
                        mm(bK.t[:, 0:blk], khb.t[:, :], ident.t[0:blk, 0:blk], True, True, [khb.r, ident.r], [bK.r])
                        copy_any(ktb.t[:, :], bK.t[:, 0:blk], [bK.r], [ktb.r], eng=ACT)
                        mm(bA.t[0:blk, 0:blk], ktb.t[:, :], qtb.t[:, :], True, True, [ktb.r, qtb.r], [bA.r])
                        tt(atb.t[:, :], bA.t[0:blk, 0:blk], c.mask.t[:, :], ALU.mult, [bA.r, c.mask.r], [atb.r])
                        mm(bO.t[:, ocs], c.vtb.t[:, tb, cs], atb.t[:, :], False, False, [c.vtb.r, atb.r], [bO.r])
                        for ch in range(nch):
                            mm(bDs[ch].t[:, 0:128], khb.t[ch * L:(ch + 1) * L, :], c.vtb.t[ch * L:(ch + 1) * L, tb, cs], True, True,
                               [khb.r, c.vtb.r], [bDs[ch].r])
                        for ch in range(nch):
                            cm, cl = blk + 2 * ch, blk + 2 * ch + 1
                            eB = eBb.t
                            Sap, Sr = c.S_ap(h), c.S_res(h)
                            ts(c.Smb[ch].t[:, :], Sap, eB[:, cm:cm + 1], None, ALU.mult, None, [Sr, eBb.r], [c.Smb[ch].r])
                            ts(c.Sd[ch].t[:, :], Sap, eB[:, cm:cm + 1], eB[:, cl:cl + 1], ALU.mult, ALU.mult, [Sr, eBb.r], [c.Sd[ch].r])
                            mm(bO.t[:, hh * blk + ch * L: hh * blk + (ch + 1) * L], c.Smb[ch].t[:, :],
                               qtb.t[:, ch * L:(ch + 1) * L], False, False, [c.Smb[ch].r, qtb.r], [bO.r])
                            stt(Sap, bDs[ch].t[:, 0:128], eB[:, cl:cl + 1], c.Sd[ch].t[:, :], ALU.mult, ALU.add,
                                [bDs[ch].r, eBb.r, c.Sd[ch].r], [Sr])
                        bcnt += 1
                s0 = c.sq[0]
                bN = psum[2]
                act(s0.t[:, 0:W4], bO.t[:, 0:W4], AF.Square, [bO.r], [s0.r])
                mm(bN.t[:, 0:W4], ones_bf.t[:, :], s0.t[:, 0:W4], True, True, [s0.r, ones_bf.r], [bN.r])
                act(c.ntmp.t[:, 0:W4], bN.t[:, 0:W4], AF.Sqrt, [bN.r], [c.ntmp.r], bias=float(EPS), scale=1.0 / 128)
                P.op(DVE, lambda e: e.reciprocal(out=c.rstd.t[:, 0:W4], in_=c.ntmp.t[:, 0:W4]), [c.ntmp.r], [c.rstd.r])
                stt(c.ntmp.t[:, 0:W4], bO.t[:, 0:W4], G[:, l, G_HGN:G_HGN + 1], c.rstd.t[:, 0:W4], ALU.mult, ALU.mult,
                    [bO.r, c.rstd.r, gains.r], [c.ntmp.r])
                tt(c.oh.t[:, g * 4:(g + 1) * 4, tb * blk:(tb + 1) * blk],
                   c.ntmp.t[:, 0:W4].rearrange("p (h t) -> p h t", h=4),
                   c.hgT.t[:, :, tb * blk:(tb + 1) * blk], ALU.mult, [c.ntmp.r, c.hgT.r], [c.oh.r])
            if not isp:
                out_toks.append(P.dma(SP, st_s[l, g * 4:(g + 1) * 4].rearrange("h k v -> k h v"), c.Sg.t[:], sem_sout[3],
                                      reads=c.Sg_res))
        if isp:
            if it == n_tiles - 1:
                out_toks.append(P.dma(SP, st_p[l].rearrange("h k v -> k h v"), Sbuf.t[:], sem_out_st, reads=Sres))
            else:
                P.dma(SP, Sc[l].rearrange("p (h v) -> p h v", h=NH), Sbuf.t[:], sem_Sst[l], reads=Sres, writes=[r_Sc[l]])
        dbg("d_oh" if isp else "s_oh", c.oh)

        for g in range(4):
            slot = yield

            def cons_a(j, bank, g=g):
                act(c.sga.t[:, g * 4 + j, :], bank.t[:, 0:n], AF.Sigmoid, [bank.r], [c.sga.r])
            gemm_fm(slot, 4, c.hT, NKC, n, cons_a)
        for g in range(4):
            slot = yield

            def cons_b(j, bank, g=g):
                act(c.sgb.t[:, g * 4 + j, :], bank.t[:, 0:n], AF.Sigmoid, [bank.r], [c.sgb.r])
            gemm_fm(slot, 4, c.hT, NKC, n, cons_b)
        for g in range(4):
            slot = yield

            def cons_oa(j, bank, g=g):
                m = g * 4 + j
                tt(c.sga.t[:, m, :], bank.t[:, 0:n], c.sga.t[:, m, :], ALU.mult, [bank.r, c.sga.r], [c.sga.r])
            gemm_fm(slot, 4, c.oa, NKC, n, cons_oa)
            slot = yield

            def cons_ob(j, bank, g=g):
                m = g * 4 + j
                tt(c.sgb.t[:, m, :], bank.t[:, 0:n], c.sgb.t[:, m, :], ALU.mult, [bank.r, c.sgb.r], [c.sgb.r])
                tt(c.hT.t[:, m, :], c.sga.t[:, m, :], c.sgb.t[:, m, :], ALU.add, [c.sga.r, c.sgb.r], [c.hT.ck(m)])
            gemm_fm(slot, 4, c.oh, NKC, n, cons_ob, banks=(4, 5, 6, 7))
        dbg("d_gated" if isp else "s_gated", c.hT)

        for g in range(4):
            slot = yield

            def cons_o(j, bank, g=g):
                copy_any(c.mixT.t[:, g * 4 + j, :], bank.t[:, 0:n], [bank.r], [c.mixT.r])
            gemm_fm(slot, 4, c.hT, NKC, n, cons_o)

        def post_norm_residual(gcol):
            rms_rstd(c, [c.mixT.t[:, k, :] for k in range(NKC)], n, D_MODEL, [c.mixT.r])
            for k in range(NKC):
                stt(c.mixT.t[:, k, :], c.mixT.t[:, k, :], G[:, l, gcol + k:gcol + k + 1], c.rstd.t[:, 0:n], ALU.mult, ALU.mult,
                    [c.mixT.r, c.rstd.r, gains.r], [c.mixT.r])
                tt(c.xT.t[:, k, :], c.xT.t[:, k, :], c.mixT.t[:, k, :], ALU.add, [c.xT.ck(k), c.mixT.r], [c.xT.ck(k)])
        if isp:
            dbg("d_mix", c.mixT)
        post_norm_residual(G_POST)
        dbg("d_xm" if isp else "s_xm", c.xT)

        rms_rstd(c, [c.xT.t[:, k, :] for k in range(NKC)], n, D_MODEL, c.xT.all())
        for k in range(NKC):
            stt(c.hT.t[:, k, :], c.xT.t[:, k, :], G[:, l, G_PREM + k:G_PREM + k + 1], c.rstd.t[:, 0:n], ALU.mult, ALU.mult,
                [c.xT.ck(k), c.rstd.r, gains.r], [c.hT.ck(k)])
        for q in range(4):
            for g in range(4):
                slot = yield

                def cons_u(j, bank, g=g):
                    k = g * 4 + j
                    act(c.ntmp.t[:, 0:n], bank.t[:, 0:n], AF.Relu, [bank.r], [c.ntmp.r])
                    tt(c.oa.t[:, k, :], c.ntmp.t[:, 0:n], c.ntmp.t[:, 0:n], ALU.mult, [c.ntmp.r], [c.oa.r])
                gemm_fm(slot, 4, c.hT, NKC, n, cons_u)
            for g in range(4):
                slot = yield

                def cons_d(j, bank, g=g, q=q):
                    m = g * 4 + j
                    if q == 0:
                        copy_any(c.mixT.t[:, m, :], bank.t[:, 0:n], [bank.r], [c.mixT.r])
                    else:
                        tt(c.mixT.t[:, m, :], bank.t[:, 0:n], c.mixT.t[:, m, :], ALU.add, [bank.r, c.mixT.r], [c.mixT.r])
                gemm_fm(slot, 4, c.oa, NKC, n, cons_d, banks=(4, 5, 6, 7))
        post_norm_residual(G_POSTM)

    def run_layer(l, it, ctxs):
        gens = [layer_gen(c, l, it) for c in ctxs]
        for g_ in gens:
            next(g_)
        nb = 0
        while True:
            slot = ws.get()
            nb += 1
            done = 0
            for g_ in gens:
                try:
                    g_.send(slot)
                except StopIteration:
                    done += 1
            if done:
                assert done == len(gens)
                break
        assert nb == nblocks_layer, (nb, nblocks_layer)

    for it in range(n_tiles):
        t0 = it * TT
        P.dma(SP, pc.xT.t[:], xT_p[:, t0:t0 + TT].rearrange("(k p) t -> p k t", p=128), sem_x, writes=pc.xT.all())
        ctxs = [pc]
        if with_sample and it == 0:
            P.dma(SP, sc_.xT.t[:], xT_s.rearrange("(k p) t -> p k t", p=128), sem_sx, writes=sc_.xT.all())
            ctxs.append(sc_)
        for l in range(n_layers):
            run_layer(l, it, ctxs)
        out_toks.append(P.dma(SP, yT_p[:, t0:t0 + TT].rearrange("(k p) t -> p k t", p=128), pc.xT.t[:], sem_out_y, reads=pc.xT.all()))
        if with_sample and it == 0:
            out_toks.append(P.dma(SP, yT_s.rearrange("(k p) t -> p k t", p=128), sc_.xT.t[:], sem_sout[0], reads=sc_.xT.all()))

    last = {}
    for tk in out_toks:
        last[id(tk.dsem)] = tk
    P.wait_all(SP, list(last.values()))
    stats = P.emit(esems)
    st.close()
    stats["sbuf_used"] = sbuf_used
    return nc, stats


def _rope_tables(pos):
    half = QK_ROPE // 2
    inv = (np.float32(10000.0) ** (-np.arange(half, dtype=np.float32) / np.float32(half))).astype(np.float32)
    ang = pos.astype(np.float32)[:, None] * inv[None, :]
    cos = np.cos(ang).astype(np.float32).T
    sin = np.sin(ang).astype(np.float32).T
    cos2 = np.concatenate([cos, cos], axis=0)
    sin2 = np.concatenate([-sin, sin], axis=0)
    return np.ascontiguousarray(np.stack([cos2, sin2]))


def _hgrn_consts():
    idx = np.arange(128)
    ch = idx // 64
    loc = idx % 64
    same = ch[:, None] == ch[None, :]
    m1 = np.zeros((128, 132), np.float32)
    le = (loc[:, None] <= loc[None, :]) & same
    mid = (loc[:, None] <= 31) & same
    m1[:, :128] = le.astype(np.float32) - mid.astype(np.float32)
    for c in range(2):
        m1[:, 128 + 2 * c] = ((ch == c) & (loc <= 31)).astype(np.float32)
        m1[:, 129 + 2 * c] = ((ch == c) & (loc > 31)).astype(np.float32)
    m2 = (mid.astype(np.float32) - le.astype(np.float32)).astype(np.float32)
    mask = ((loc[:, None] <= loc[None, :]) & same).astype(np.float32)
    return m1, m2, mask, np.eye(128, dtype=np.float32)


def _hgrn_consts_sample():
    i = np.arange(32)
    le = (i[:, None] <= i[None, :])
    mid = (i[:, None] <= 15) & np.ones((32, 32), bool)
    m1 = np.zeros((32, 34), np.float32)
    m1[:, :32] = le.astype(np.float32) - mid.astype(np.float32)
    m1[:, 32] = (i <= 15).astype(np.float32)
    m1[:, 33] = (i > 15).astype(np.float32)
    m2 = (mid.astype(np.float32) - le.astype(np.float32)).astype(np.float32)
    mask = le.astype(np.float32)
    return m1, m2, mask


_PROG_CACHE = {}


def _get_program(n_tiles, n_layers):
    key = (n_tiles, n_layers)
    if key not in _PROG_CACHE:
        _PROG_CACHE[key] = build_program(n_tiles, n_layers)
    return _PROG_CACHE[key]


def _col(v, n):
    return np.ascontiguousarray(v.reshape(n, 128).T)


def prepare_inputs(inp):
    f = lambda a: np.ascontiguousarray(np.asarray(a, dtype=np.float32))
    w_in = f(inp["w_in"])
    kpe = w_in[:, :, C_KPE:C_KPE + 64]
    w_in_ext = np.concatenate([w_in, kpe[:, :, 32:64], kpe[:, :, 0:32]], axis=2)
    wuq = f(inp["w_uq"]).reshape(DEPTH, Q_LORA, NH, 192)
    wuq_ext = np.concatenate([wuq, wuq[..., 160:192], wuq[..., 128:160]], axis=3).reshape(DEPTH, Q_LORA, NH * 256)
    gains = np.zeros((DEPTH, 128, NG), np.float32)
    for l in range(DEPTH):
        gains[l, :, G_PRE:G_PRE + 16] = _col(f(inp["pre_mix_g"])[l], 16)
        gains[l, :, G_POST:G_POST + 16] = _col(f(inp["post_mix_g"])[l], 16)
        gains[l, :, G_PREM:G_PREM + 16] = _col(f(inp["pre_mlp_g"])[l], 16)
        gains[l, :, G_POSTM:G_POSTM + 16] = _col(f(inp["post_mlp_g"])[l], 16)
        gains[l, :, G_QN:G_QN + 4] = _col(f(inp["q_norm_g"])[l], 4)
        gains[l, :, G_KVN:G_KVN + 4] = _col(f(inp["kv_norm_g"])[l], 4)
        gains[l, :, G_HGN] = f(inp["hg_norm_g"])[l]
    m1, m2, mask, ident = _hgrn_consts()
    m1s, m2s, masks = _hgrn_consts_sample()
    shared = {
        "rope_p": _rope_tables(np.arange(SEQ)),
        "rope_s": _rope_tables(PAST + np.arange(DEC_SEQ)),
        "cm1": m1, "cm2": m2, "cmask": mask, "cident": ident,
        "cm1s": m1s, "cm2s": m2s, "cmasks": masks,
        "gains": gains,
        "hg_lb": f(inp["hg_lb"]),
        "w_in": np.ascontiguousarray(w_in_ext),
        "w_uq": np.ascontiguousarray(wuq_ext),
        "w_uk": f(inp["w_uk"]).reshape(DEPTH, KV_LORA, 2048),
        "w_uv": f(inp["w_uv"]).reshape(DEPTH, KV_LORA, 2048),
        "w_oa": f(inp["w_oa"]), "w_ob": f(inp["w_ob"]), "w_out": f(inp["w_out"]),
        "w_up": f(inp["w_up"]), "w_down": f(inp["w_down"]),
    }
    xp = f(inp["x_prompt"])
    xs = f(inp["x_sample"])
    cckv = f(inp["cache_ckv"])
    ckpe = f(inp["cache_kpe"])
    sth = f(inp["state_hgrn"])
    in_maps = []
    for c in range(8):
        m = dict(shared)
        m["xT_p"] = np.ascontiguousarray(xp[c % 4].T)
        m["xT_s"] = np.ascontiguousarray(xs[c].T)
        m["cache_ckvT"] = np.ascontiguousarray(cckv[:, c].transpose(0, 2, 1))
        m["cache_kpeT"] = np.ascontiguousarray(ckpe[:, c].transpose(0, 2, 1))
        m["state_s"] = np.ascontiguousarray(sth[:, c])
        in_maps.append(m)
    return in_maps


def kernel(**inp):
    nc, _ = _get_program(4, DEPTH)
    in_maps = prepare_inputs(inp)
    res = run_bass_kernel_spmd(nc, in_maps, core_ids=list(range(8)))
    R = res.results
    B = 4
    y_p = np.stack([R[b]["yT_p"].T for b in range(B)])
    ckv_p = np.stack([np.stack([R[b]["ckvT_p"][l].T for b in range(B)]) for l in range(DEPTH)])
    kpe_p = np.stack([np.stack([R[b]["kpeT_p"][l].T for b in range(B)]) for l in range(DEPTH)])
    st_p = np.stack([np.stack([R[b]["st_p"][l] for b in range(B)]) for l in range(DEPTH)])
    y_s = np.stack([R[c]["yT_s"].T for c in range(8)])
    ckv_s = np.stack([np.stack([R[c]["ckvT_s"][l].T for c in range(8)]) for l in range(DEPTH)])
    kpe_s = np.stack([np.stack([R[c]["kpeT_s"][l].T for c in range(8)]) for l in range(DEPTH)])
    st_s = np.stack([np.stack([R[c]["st_s"][l] for c in range(8)]) for l in range(DEPTH)])
    return (np.ascontiguousarray(y_p), np.ascontiguousarray(y_s), np.ascontiguousarray(ckv_p),
            np.ascontiguousarray(kpe_p), np.ascontiguousarray(st_p), np.ascontiguousarray(ckv_s),
            np.ascontiguousarray(kpe_s), np.ascontiguousarray(st_s))
```

```python
import contextlib
import numpy as np
import concourse.bass as bass
import concourse.mybir as mybir
from concourse.bass_utils import run_bass_kernel_spmd

F32 = mybir.dt.float32
BF16 = mybir.dt.bfloat16
ALU = mybir.AluOpType
AF = mybir.ActivationFunctionType

PE, ACT, DVE, POOL, SP = "tensor", "scalar", "vector", "gpsimd", "sync"
ENGINES = (PE, ACT, DVE, POOL, SP)
STRICT_SAME_ENGINE = True

D_MODEL = 2048
SEQ = 2048
DEPTH = 2
DEC_SEQ = 32
PAST = 4096
NH = 16
QK_NOPE = 128
QK_ROPE = 64
KV_LORA = 512
Q_LORA = 512
D_FF = 8192
EPS = 1e-6
ATTN_SCALE = float((QK_NOPE + QK_ROPE) ** -0.5)
D_IN = 13376
C_Q, C_KV, C_KPE, C_HQ, C_HF, C_HI, C_HG, C_GA, C_GB = 0, 512, 1024, 1088, 3136, 5184, 7232, 9280, 11328
C_KPESW = D_IN
D_IN_EXT = D_IN + 64
TT = 512
NKC = D_MODEL // 128
G_PRE, G_POST, G_PREM, G_POSTM, G_QN, G_KVN, G_HGN, NG = 0, 16, 32, 48, 64, 68, 72, 73


class Res:
    __slots__ = ("name", "last_w", "readers", "aliases")

    def __init__(self, name):
        self.name = name
        self.last_w = None
        self.readers = []
        self.aliases = ()


class DmaSem:
    def __init__(self, handle):
        self.handle = handle
        self.total = 0


class Instr:
    __slots__ = ("eng", "fn", "deps", "signal", "count", "dsem", "dval", "is_dma")

    def __init__(self, eng, fn):
        self.eng = eng
        self.fn = fn
        self.deps = []
        self.signal = False
        self.count = 0
        self.dsem = None
        self.dval = 0
        self.is_dma = False


class Prog:
    def __init__(self, nc):
        self.nc = nc
        self.q = {e: [] for e in ENGINES}
        self.n_instr = 0

    def _track(self, ins, reads, writes):
        eng = ins.eng
        deps = ins.deps
        for r in reads:
            w = r.last_w
            if w is not None and not (w.eng == PE and eng == PE and not w.is_dma and not ins.is_dma):
                deps.append(w)
            r.readers.append(ins)
        same_ok = STRICT_SAME_ENGINE and eng != PE
        for r0 in writes:
            for r in (r0,) + tuple(r0.aliases):
                w = r.last_w
                if w is not None and w is not ins and (w.is_dma or ins.is_dma or w.eng != eng or same_ok):
                    deps.append(w)
                for rd in r.readers:
                    if rd is ins:
                        continue
                    if rd.is_dma or ins.is_dma or rd.eng != eng or same_ok:
                        deps.append(rd)
            r0.last_w = ins
            r0.readers = []

    def op(self, eng, fn, reads=(), writes=()):
        ins = Instr(eng, fn)
        self._track(ins, reads, writes)
        self.q[eng].append(ins)
        self.n_instr += 1
        return ins

    def dma(self, eng, out, in_, dsem, reads=(), writes=(), **kw):
        def fn(e, out=out, in_=in_, kw=kw):
            return e.dma_start(out=out, in_=in_, **kw)
        ins = Instr(eng, fn)
        ins.is_dma = True
        dsem.total += 16
        ins.dsem = dsem
        ins.dval = dsem.total
        self._track(ins, reads, writes)
        self.q[eng].append(ins)
        self.n_instr += 1
        return ins

    def wait_all(self, eng, toks):
        ins = Instr(eng, None)
        ins.deps = list(toks)
        self.q[eng].append(ins)
        return ins

    def emit(self, esems):
        for e in ENGINES:
            for ins in self.q[e]:
                for d in ins.deps:
                    if not d.is_dma:
                        d.signal = True
        for e in ENGINES:
            c = 0
            for ins in self.q[e]:
                if ins.signal and not ins.is_dma:
                    c += 1
                    ins.count = c
        stats = {}
        nc = self.nc
        with nc.Block() as block:
            for e in ENGINES:
                lst = self.q[e]
                if not lst:
                    continue

                def body(engine, lst=lst, e=e):
                    seen = {}
                    nw = 0
                    for ins in lst:
                        need = {}
                        for d in ins.deps:
                            if d.is_dma:
                                key, val, h = id(d.dsem), d.dval, d.dsem.handle
                            else:
                                key, val, h = d.eng, d.count, esems[d.eng]
                            if seen.get(key, 0) >= val:
                                continue
                            if key not in need or need[key][1] < val:
                                need[key] = (h, val)
                        for key, (h, val) in need.items():
                            engine.wait_ge(h, val)
                            seen[key] = val
                            nw += 1
                        if ins.fn is None:
                            continue
                        bi = ins.fn(engine)
                        if ins.is_dma:
                            bi.then_inc(ins.dsem.handle, 16)
                        elif ins.signal:
                            bi.then_inc(esems[e], 1)
                    stats[e] = (len(lst), nw)

                getattr(block, e)(body)
        return stats


class Buf:
    __slots__ = ("t", "r", "rc")

    def __init__(self, t, r):
        self.t = t
        self.r = r
        self.rc = None

    def split(self, n):
        self.rc = [Res(f"{self.r.name}.{i}") for i in range(n)]
        return self

    def ck(self, k):
        return self.rc[k] if self.rc is not None else self.r

    def all(self):
        return list(self.rc) if self.rc is not None else [self.r]


def build_program(n_tiles=4, n_layers=DEPTH, nslot=2, with_sample=True, debug=False):
    nc = bass.Bass("TRN2", target_bir_lowering=False)
    P = Prog(nc)

    def din(name, shape, dt=F32):
        return nc.dram_tensor(name, list(shape), dt, kind="ExternalInput").ap()

    def dout(name, shape, dt=F32):
        return nc.dram_tensor(name, list(shape), dt, kind="ExternalOutput").ap()

    xT_p = din("xT_p", [D_MODEL, SEQ])
    rope_p = din("rope_p", [2, 64, SEQ])
    cm1 = din("cm1", [128, 132])
    cm2 = din("cm2", [128, 128])
    cmask = din("cmask", [128, 128])
    cident = din("cident", [128, 128])
    gains_d = din("gains", [DEPTH, 128, NG])
    hg_lb = din("hg_lb", [DEPTH, D_MODEL])
    w_in = din("w_in", [DEPTH, D_MODEL, D_IN_EXT])
    w_uq = din("w_uq", [DEPTH, Q_LORA, 4096])
    w_uk = din("w_uk", [DEPTH, KV_LORA, 2048])
    w_uv = din("w_uv", [DEPTH, KV_LORA, 2048])
    w_oa = din("w_oa", [DEPTH, D_MODEL, D_MODEL])
    w_ob = din("w_ob", [DEPTH, D_MODEL, D_MODEL])
    w_out = din("w_out", [DEPTH, D_MODEL, D_MODEL])
    w_up = din("w_up", [DEPTH, D_MODEL, D_FF])
    w_down = din("w_down", [DEPTH, D_FF, D_MODEL])
    xT_s = din("xT_s", [D_MODEL, DEC_SEQ])
    rope_s = din("rope_s", [2, 64, DEC_SEQ])
    cache_ckvT = din("cache_ckvT", [DEPTH, KV_LORA, PAST])
    cache_kpeT = din("cache_kpeT", [DEPTH, 64, PAST])
    state_s = din("state_s", [DEPTH, NH, 128, 128])
    cm1s = din("cm1s", [32, 34])
    cm2s = din("cm2s", [32, 32])
    cmasks = din("cmasks", [32, 32])

    yT_p = dout("yT_p", [D_MODEL, SEQ])
    ckvT_p = dout("ckvT_p", [DEPTH, KV_LORA, SEQ])
    kpeT_p = dout("kpeT_p", [DEPTH, 64, SEQ])
    st_p = dout("st_p", [DEPTH, NH, 128, 128])
    yT_s = dout("yT_s", [D_MODEL, DEC_SEQ])
    ckvT_s = dout("ckvT_s", [DEPTH, KV_LORA, DEC_SEQ])
    kpeT_s = dout("kpeT_s", [DEPTH, 64, DEC_SEQ])
    st_s = dout("st_s", [DEPTH, NH, 128, 128])
    dbg_t = {}
    if debug:
        for nm, dt_ in (("d_oa", BF16), ("d_oh", BF16), ("d_gated", BF16), ("d_xm", F32), ("d_mix", F32)):
            dbg_t[nm] = nc.dram_tensor(nm, [128, NKC, TT], dt_, kind="ExternalOutput").ap()
        for nm, dt_ in (("s_oa", BF16), ("s_oh", BF16), ("s_gated", BF16), ("s_xm", F32)):
            dbg_t[nm] = nc.dram_tensor(nm, [128, NKC, DEC_SEQ], dt_, kind="ExternalOutput").ap()

    Kc = nc.dram_tensor("Kc", [DEPTH, NH, 128, SEQ], BF16, kind="Internal").ap()
    Vc = nc.dram_tensor("Vc", [DEPTH, 4, SEQ, 512], BF16, kind="Internal").ap()
    Sc = nc.dram_tensor("Sc", [DEPTH, 128, NH * 128], F32, kind="Internal").ap()
    Kcs = nc.dram_tensor("Kcs", [DEPTH, NH, 128, PAST], BF16, kind="Internal").ap()
    Vcs = nc.dram_tensor("Vcs", [DEPTH, 4, PAST + 128, 512], BF16, kind="Internal").ap()
    r_Kc = [Res(f"Kc{l}") for l in range(DEPTH)]
    r_Vc = [Res(f"Vc{l}") for l in range(DEPTH)]
    r_Sc = [Res(f"Sc{l}") for l in range(DEPTH)]
    r_Kcs = [Res(f"Kcs{l}") for l in range(DEPTH)]
    r_Vcs = [Res(f"Vcs{l}") for l in range(DEPTH)]

    SB_BASE, SB_END = 16512, 229376
    cur = [SB_BASE]

    def sb(name, shape, dt, at=None):
        nbytes = int(np.prod(shape[1:])) * (4 if dt == F32 else 2)
        nbytes = (nbytes + 63) // 64 * 64
        if at is None:
            off = cur[0]
            cur[0] += nbytes
            assert cur[0] <= SB_END, f"SBUF overflow at {name}: {cur[0]}"
        else:
            off = at
        t = nc.alloc_sbuf_tensor_at(name, list(shape), dt, offset=off)
        return Buf(t, Res(name)), off, nbytes

    def sbb(name, shape, dt):
        return sb(name, shape, dt)[0]

    class Ctx:
        pass

    pc = Ctx()
    pc.kind, pc.n, pc.blk, pc.nblk, pc.nch, pc.L = "p", TT, 128, 4, 2, 64
    sc_ = Ctx()
    sc_.kind, sc_.n, sc_.blk, sc_.nblk, sc_.nch, sc_.L = "s", DEC_SEQ, 32, 1, 1, 32

    pc.xT = sbb("xT", [128, NKC, TT], F32).split(NKC)
    pc.hT = sbb("hT", [128, NKC, TT], BF16).split(NKC)
    pc.oa, oa_off, _ = sb("oa", [128, NKC, TT], BF16)
    pc.oh = sbb("oh", [128, NKC, TT], BF16)
    vbuf = pc.oh
    wring, wring_alt = [], []
    for i in range(nslot):
        b_, off_, _ = sb(f"w{i}", [128, NKC, 512], BF16)
        wring.append(b_)
        wring_alt.append(Buf(nc.alloc_sbuf_tensor_at(f"w{i}a", [128, 4, 2048], BF16, offset=off_), b_.r))
    kpeT = [sbb(f"kpeT{l}", [64, SEQ], BF16) for l in range(DEPTH)]
    gains = sbb("gains", [128, DEPTH, NG], F32)
    ident = sbb("ident", [128, 128], BF16)
    ones_bf = sbb("ones_bf", [128, 128], BF16)
    zeros_bf = sbb("zeros_bf", [128, 512], BF16)
    pc.m1 = sbb("m1", [128, 132], F32)
    pc.m2 = sbb("m2", [128, 128], F32)
    pc.mask = sbb("maskbd", [128, 128], F32)
    pc.ropec = sbb("ropec", [64, TT], F32)
    pc.ropes = sbb("ropes", [64, TT], F32)
    lbt = sbb("lbt", [128, 512], F32)
    omlt = sbb("omlt", [128, 512], F32)
    pc.sq = [sbb(f"sq{i}", [128, TT], BF16) for i in range(2)]
    pc.rstd = sbb("rstd", [128, TT], F32)
    pc.ntmp = sbb("ntmp", [128, TT], F32)
    kb0, kb_off, kb_bytes = sb("Kbuf0", [128, SEQ], BF16)
    kb1, kb1_off, _ = sb("Kbuf1", [128, SEQ], BF16)
    Sbuf = sb("Sbuf", [128, NH, 128], F32, at=kb_off)[0]
    Sres = [Res(f"S{h}") for h in range(NH)]
    for r_ in Sres:
        r_.aliases = (kb0.r, kb1.r)
    kb0.r.aliases = tuple(Sres)
    kb1.r.aliases = tuple(Sres)
    Kbuf = [kb0, kb1]
    pc.S_ap = lambda h: Sbuf.t[:, h, :]
    pc.S_res = lambda h: Sres[h]
    pc.mixT, mix_off, mix_bytes = sb("mixT", [128, NKC, TT], F32)
    sc_cur = [mix_off]
    scratch_all = []

    def scr(name, shape, dt):
        b, off, nb = sb(name, shape, dt, at=sc_cur[0])
        sc_cur[0] += nb
        assert sc_cur[0] <= mix_off + mix_bytes, f"scratch overflow {name}"
        scratch_all.append((b, off, nb))
        return b

    def alloc_scratch(c, alloc, n, blk):
        c.qlatn = alloc("qlatn", [128, 4, n], BF16)
        c.ckvbf = alloc("ckvbf", [128, 4, n], BF16)
        c.tmp4 = alloc("tmp4", [128, 4, n], F32)
        return c

    alloc_scratch(pc, scr, TT, 128)
    NSTG = 6
    stage = [scr(f"stage{i}", [128, TT], BF16) for i in range(NSTG)]
    pc.qn = [scr(f"qn{i}", [128, TT], BF16) for i in range(2)]
    pc.qr = [scr(f"qr{i}", [64, TT], BF16) for i in range(2)]
    pc.pT = [scr(f"pT{i}", [128, TT], BF16) for i in range(3)]
    pc.recip = scr("recip", [128, TT], F32)

    def alloc_hgrn(c, alloc, n, blk, nch, batched=False):
        c.hqT = alloc("hqT", [128, 4, n], BF16)
        c.hgT = alloc("hgT", [128, 4, n], BF16)
        c.logf = alloc("logf", [blk, n // blk, 512], F32)
        c.kkb = alloc("kkb", [blk, n // blk, 512], BF16)
        c.vtb = alloc("vtb", [blk, n // blk, 512], BF16)
        if batched:
            c.bufA = alloc("bufA", [128, 512], BF16)
            c.bufB = alloc("bufB", [128, 512], BF16)
            c.qtil4 = alloc("qtil4", [128, 4, 128], BF16)
            c.khat4 = alloc("khat4", [128, 512], BF16)
            c.eBs = alloc("eBs", [128, 16], F32)
            c.eL = alloc("eL", [128, 4, 2], F32)
            c.Smb4 = alloc("Smb4", [128, 4, 128], BF16)
            c.Sd4 = alloc("Sd4", [128, 4, 128], F32)
            return
        c.expB = [alloc(f"expB{i}", [128, blk + 2 * nch], F32) for i in range(2)]
        c.expE = [alloc(f"expE{i}", [blk, 128], F32) for i in range(2)]
        c.qtil = [alloc(f"qtil{i}", [128, blk], BF16) for i in range(2)]
        c.khat = [alloc(f"khat{i}", [blk, 128], BF16) for i in range(2)]
        c.ktil = [alloc(f"ktil{i}", [128, blk], BF16) for i in range(2)]
        c.ATm = [alloc(f"ATm{i}", [blk, blk], BF16) for i in range(2)]
        c.Smb = [alloc(f"Smb{i}", [128, 128], BF16) for i in range(2)]
        c.Sd = [alloc(f"Sd{i}", [128, 128], F32) for i in range(2)]

    sc_cur[0] = mix_off
    alloc_hgrn(pc, scr, TT, 128, 2, batched=True)
    sc_cur[0] = mix_off
    pc.sga = scr("sga", [128, NKC, TT], BF16)
    pc.sgb = scr("sgb", [128, NKC, TT], BF16)
    pc.mixT.r.aliases = tuple(b.r for b, _, _ in scratch_all)
    for b, off, nb in scratch_all:
        b.r.aliases = (pc.mixT.r,) + tuple(o.r for o, o_off, o_nb in scratch_all
                                           if o is not b and o_off < off + nb and off < o_off + o_nb)

    samp_start = cur[0]
    samp_bufs = []
    if with_sample:
        s = sc_
        n_s = DEC_SEQ

        def ssb(name, shape, dt):
            b_ = sbb("s_" + name, shape, dt)
            samp_bufs.append(b_)
            return b_
        s.xT = ssb("xT", [128, NKC, n_s], F32).split(NKC)
        s.hT = ssb("hT", [128, NKC, n_s], BF16).split(NKC)
        s.oa = ssb("oa", [128, NKC, n_s], BF16)
        s.oh = ssb("oh", [128, NKC, n_s], BF16)
        s.mixT, smix_off, _ = sb("s_mixT", [128, NKC, n_s], F32)
        alloc_scratch(s, ssb, n_s, 32)
        s.qn4 = ssb("qn4", [128, 4, n_s], BF16)
        s.qr4 = ssb("qr4", [64, 4, n_s], BF16)
        s.pT = [ssb(f"pT{i}", [128, n_s], BF16) for i in range(3)]
        s.recip = ssb("recip", [128, 128], F32)
        s.knew = ssb("knew", [128, NH, n_s], BF16)
        s.vnew = ssb("vnew", [32, 512], BF16)
        s.kpeS = ssb("kpeS", [64, 2048 + n_s], BF16)
        alloc_hgrn(s, ssb, n_s, 32, 1)
        s.sga = sb("s_sga", [128, NKC, n_s], BF16, at=smix_off)[0]
        s.sgb = sb("s_sgb", [128, NKC, n_s], BF16, at=smix_off + NKC * n_s * 2)[0]
        s.sga.r.aliases = (s.mixT.r,)
        s.sgb.r.aliases = (s.mixT.r,)
        s.mixT.r.aliases = (s.sga.r, s.sgb.r)
        s.Sg = ssb("Sg", [128, 4, 128], F32)
        s.Sg_res = [Res(f"sS{i}") for i in range(4)]
        s.S_ap = lambda h: s.Sg.t[:, h % 4, :]
        s.S_res = lambda h: s.Sg_res[h % 4]
        s.sq = [ssb(f"sq{i}", [128, 128], BF16) for i in range(2)]
        s.rstd = ssb("rstd", [128, 128], F32)
        s.ntmp = ssb("ntmp", [128, 128], F32)
        s.ropec = ssb("ropec", [64, n_s], F32)
        s.ropes = ssb("ropes", [64, n_s], F32)
        s.m1 = ssb("m1", [32, 34], F32)
        s.m2 = ssb("m2", [32, 32], F32)
        s.mask = ssb("mask", [32, 32], F32)
        slab = sb("s_slab", [128, 4, 512], BF16, at=oa_off)[0]
        slab.r.aliases = (pc.oa.r,)
        xstage = [sb(f"xstage{i}", [128, TT], BF16, at=oa_off + 4096 + i * 1024)[0] for i in range(12)]
        for x_ in xstage:
            x_.r.aliases = (pc.oa.r,)
        stage.extend(xstage)
        pc.oa.r.aliases = (slab.r,) + tuple(x_.r for x_ in xstage)
    extra_slot = with_sample and (cur[0] - samp_start) >= NKC * 512 * 2 and n_tiles > 1
    if extra_slot:
        b_ = sb("w2", [128, NKC, 512], BF16, at=samp_start)[0]
        samp_res = []
        for x_ in samp_bufs:
            samp_res.extend(x_.all())
        samp_res.extend([sc_.mixT.r, sc_.sga.r, sc_.sgb.r] + list(sc_.Sg_res))
        b_.r.aliases = tuple(samp_res)
        wring.append(b_)
        wring_alt.append(Buf(nc.alloc_sbuf_tensor_at("w2a", [128, 4, 2048], BF16, offset=samp_start), b_.r))
    sbuf_used = cur[0] - SB_BASE

    psum = []
    for i in range(8):
        t = nc.alloc_psum_tensor(f"ps{i}", [128, 512], F32)
        psum.append(Buf(t, Res(f"ps{i}")))

    st = contextlib.ExitStack()
    esems = {e: st.enter_context(nc.semaphore(f"s_{e}")) for e in ENGINES}

    def newsem(name):
        return DmaSem(st.enter_context(nc.semaphore(name)))

    wsem = [newsem(f"wsem{i}") for i in range(nslot + 1)]
    sem_const = newsem("const")
    sem_x = newsem("xload")
    sem_out_y = newsem("out_y")
    sem_out_ckv = newsem("out_ckv")
    sem_out_kpe = newsem("out_kpe")
    sem_out_st = newsem("out_st")
    sem_kv = [newsem("kvld0"), newsem("kvld1")]
    sem_vld = newsem("vld")
    sem_kvst = [newsem(f"kvst{i}") for i in range(len(stage))]
    sem_Sld = newsem("sld")
    sem_Sst = [newsem("sst0"), newsem("sst1")]
    sem_rc = newsem("ropec")
    sem_rs = newsem("ropes")
    sem_lb = newsem("lb")
    sem_oml = newsem("oml")
    sem_dbg = newsem("dbg")
    sem_sx = newsem("s_x")
    sem_sout = [newsem(f"s_out{i}") for i in range(4)]
    sem_slab = newsem("s_slab")
    sem_skpe = newsem("s_kpe")
    sem_sS = newsem("s_Sld")
    sem_svn = newsem("s_vnew")
    out_toks = []

    def dbg(nm, buf):
        if debug:
            out_toks.append(P.dma(SP, dbg_t[nm], buf.t[:], sem_dbg, reads=buf.all()))

    evac_flip = [0]

    def ew_engine():
        evac_flip[0] ^= 1
        return ACT if evac_flip[0] else DVE

    def act(out, in_, func, reads, writes, bias=None, scale=None):
        kw = {}
        if bias is not None:
            kw["bias"] = bias
        if scale is not None:
            kw["scale"] = scale
        return P.op(ACT, lambda e: e.activation(out=out, in_=in_, func=func, **kw), reads, writes)

    def copy_any(out, in_, reads, writes, eng=None):
        eng = eng or ew_engine()
        if eng == ACT:
            return P.op(ACT, lambda e: e.activation(out=out, in_=in_, func=AF.Copy), reads, writes)
        return P.op(DVE, lambda e: e.tensor_copy(out=out, in_=in_), reads, writes)

    def tt(out, in0, in1, op, reads, writes):
        return P.op(DVE, lambda e: e.tensor_tensor(out=out, in0=in0, in1=in1, op=op), reads, writes)

    def ts(out, in0, s1, s2, op0, op1, reads, writes):
        if s2 is None:
            return P.op(DVE, lambda e: e.tensor_scalar(out=out, in0=in0, scalar1=s1, scalar2=None, op0=op0), reads, writes)
        return P.op(DVE, lambda e: e.tensor_scalar(out=out, in0=in0, scalar1=s1, scalar2=s2, op0=op0, op1=op1), reads, writes)

    def stt(out, in0, scalar, in1, op0, op1, reads, writes):
        return P.op(DVE, lambda e: e.scalar_tensor_tensor(out=out, in0=in0, scalar=scalar, in1=in1, op0=op0, op1=op1), reads, writes)

    def mm(out, lhsT, rhs, start, stop, reads, writes):
        return P.op(PE, lambda e: e.matmul(out, lhsT, rhs, start=start, stop=stop, skip_group_check=True), reads, writes)

    class WStream:
        def __init__(self):
            self.blocks = []
            self.next_load = 0
            self.next_use = 0
            self.slot_last = {}
            self.cnt = {}

        def add(self, w2d, r0, nrows, c0, ncols, ring):
            nkc = nrows // 128
            v = w2d[r0:r0 + nrows, c0:c0 + ncols].rearrange("(kc p) n -> p kc n", p=128)
            k = self.cnt.get(ring, 0)
            self.cnt[ring] = k + 1
            self.blocks.append((v, nkc, ncols, k % ring))

        def _view(self, k):
            v, nkc, ncols, sl = self.blocks[k]
            return (wring_alt if ncols > 512 else wring)[sl]

        def _issue(self, k):
            v, nkc, ncols, sl = self.blocks[k]
            P.dma(POOL, self._view(k).t[:, 0:nkc, 0:ncols], v, wsem[sl], writes=[wring[sl].r])

        def prefetch(self):
            while self.next_load < len(self.blocks):
                sl = self.blocks[self.next_load][3]
                if self.slot_last.get(sl, -1) >= self.next_use:
                    break
                self._issue(self.next_load)
                self.slot_last[sl] = self.next_load
                self.next_load += 1

        def get(self):
            self.prefetch()
            k = self.next_use
            assert k < self.next_load
            self.next_use += 1
            return self._view(k)

    ws = WStream()

    def layer_blocks(l, ring):
        wi = w_in[l]
        n0 = len(ws.blocks)
        _add = ws.add
        ws_add = lambda *a: _add(*a, ring)
        ws_add(wi, 0, 2048, C_Q, 512)
        ws_add(wi, 0, 2048, C_KV, 512)
        ws_add(wi, 0, 2048, C_KPE, 64)
        ws_add(wi, 0, 2048, C_KPESW, 64)
        ws_add(w_uk[l], 0, 512, 0, 2048)
        ws_add(w_uv[l], 0, 512, 0, 2048)
        for g in range(4):
            ws_add(w_uq[l], 0, 512, g * 1024, 1024)
        for g in range(4):
            ws_add(wi, 0, 2048, C_HQ + g * 512, 512)
            ws_add(wi, 0, 2048, C_HF + g * 512, 512)
            ws_add(wi, 0, 2048, C_HI + g * 512, 512)
            ws_add(wi, 0, 2048, C_HG + g * 512, 512)
        for g in range(4):
            ws_add(wi, 0, 2048, C_GA + g * 512, 512)
        for g in range(4):
            ws_add(wi, 0, 2048, C_GB + g * 512, 512)
        for g in range(4):
            ws_add(w_oa[l], 0, 2048, g * 512, 512)
            ws_add(w_ob[l], 0, 2048, g * 512, 512)
        for g in range(4):
            ws_add(w_out[l], 0, 2048, g * 512, 512)
        for q in range(4):
            for g in range(4):
                ws_add(w_up[l], 0, 2048, q * 2048 + g * 512, 512)
            for g in range(4):
                ws_add(w_down[l], q * 2048, 2048, g * 512, 512)
        return len(ws.blocks) - n0

    nblocks_layer = 0
    for it in range(n_tiles):
        for l in range(n_layers):
            nblocks_layer = layer_blocks(l, nslot + 1 if (extra_slot and it > 0) else nslot)

    c_toks = []
    ident_f = pc.ntmp
    c_toks.append(P.dma(SP, gains.t[:], gains_d.rearrange("l p g -> p l g"), sem_const, writes=[gains.r]))
    c_toks.append(P.dma(SP, pc.m1.t[:], cm1, sem_const, writes=[pc.m1.r]))
    c_toks.append(P.dma(SP, pc.m2.t[:], cm2, sem_const, writes=[pc.m2.r]))
    c_toks.append(P.dma(SP, pc.mask.t[:], cmask, sem_const, writes=[pc.mask.r]))
    c_toks.append(P.dma(SP, ident_f.t[:, 0:128], cident, sem_const, writes=[ident_f.r]))
    if with_sample:
        c_toks.append(P.dma(SP, sc_.m1.t[:], cm1s, sem_const, writes=[sc_.m1.r]))
        c_toks.append(P.dma(SP, sc_.m2.t[:], cm2s, sem_const, writes=[sc_.m2.r]))
        c_toks.append(P.dma(SP, sc_.mask.t[:], cmasks, sem_const, writes=[sc_.mask.r]))
        c_toks.append(P.dma(SP, sc_.ropec.t[:], rope_s[0], sem_const, writes=[sc_.ropec.r]))
        c_toks.append(P.dma(SP, sc_.ropes.t[:], rope_s[1], sem_const, writes=[sc_.ropes.r]))
    for e_ in (DVE, ACT, PE):
        P.wait_all(e_, c_toks)
    P.op(DVE, lambda e: e.tensor_copy(out=ident.t[:], in_=ident_f.t[:, 0:128]), [ident_f.r], [ident.r])
    P.op(DVE, lambda e: e.memset(ones_bf.t[:], 1.0), [], [ones_bf.r])
    P.op(DVE, lambda e: e.memset(zeros_bf.t[:], 0.0), [], [zeros_bf.r])

    def rms_rstd(c, src_chunks, n, nfeat, reads):
        bank = psum[7]
        nch_ = len(src_chunks)
        for i, a in enumerate(src_chunks):
            s_ = c.sq[i % 2]
            rd = [reads[i]] if len(reads) == nch_ and nch_ > 1 else reads
            act(s_.t[:, 0:n], a, AF.Square, rd, [s_.r])
            mm(bank.t[:, 0:n], ones_bf.t[:, :], s_.t[:, 0:n], i == 0, i == nch_ - 1, [s_.r, ones_bf.r], [bank.r])
        act(c.ntmp.t[:, 0:n], bank.t[:, 0:n], AF.Sqrt, [bank.r], [c.ntmp.r], bias=float(EPS), scale=1.0 / nfeat)
        P.op(DVE, lambda e: e.reciprocal(out=c.rstd.t[:, 0:n], in_=c.ntmp.t[:, 0:n]), [c.ntmp.r], [c.rstd.r])

    def gemm_fm(slot, nchunks, rhs_buf, nk, n, consume, col0=0, banks=(0, 1, 2, 3)):
        for j in range(nchunks):
            bank = psum[banks[j % len(banks)]]
            for kc in range(nk):
                mm(bank.t[:, 0:n], slot.t[:, kc, col0 + j * 128: col0 + (j + 1) * 128], rhs_buf.t[:, kc, 0:n],
                   kc == 0, kc == nk - 1, [slot.r, rhs_buf.ck(kc)], [bank.r])
            consume(j, bank)

    def gemm_tm(slot, c, consume, banks=(0, 1, 2, 3)):
        for tb in range(c.nblk):
            bank = psum[banks[tb % len(banks)]]
            for kc in range(NKC):
                mm(bank.t[0:c.blk, 0:512], c.hT.t[:, kc, tb * c.blk:(tb + 1) * c.blk], slot.t[:, kc, 0:512],
                   kc == 0, kc == NKC - 1, [slot.r, c.hT.ck(kc)], [bank.r])
            consume(tb, bank)

    def layer_gen(c, l, it):
        n = c.n
        G = gains.t
        isp = c.kind == "p"
        t0 = it * TT

        rms_rstd(c, [c.xT.t[:, k, :] for k in range(NKC)], n, D_MODEL, c.xT.all())
        for k in range(NKC):
            stt(c.hT.t[:, k, :], c.xT.t[:, k, :], G[:, l, G_PRE + k:G_PRE + k + 1], c.rstd.t[:, 0:n], ALU.mult, ALU.mult,
                [c.xT.ck(k), c.rstd.r, gains.r], [c.hT.ck(k)])

        def lat_block(slot, dst_bf, gcol, out_dram=None, osem=None):
            def cons(j, bank):
                copy_any(c.tmp4.t[:, j, :], bank.t[:, 0:n], [bank.r], [c.tmp4.r])
            gemm_fm(slot, 4, c.hT, NKC, n, cons)
            rms_rstd(c, [c.tmp4.t[:, k, :] for k in range(4)], n, 512, [c.tmp4.r])
            for k in range(4):
                stt(c.tmp4.t[:, k, :], c.tmp4.t[:, k, :], G[:, l, gcol + k:gcol + k + 1], c.rstd.t[:, 0:n], ALU.mult, ALU.mult,
                    [c.tmp4.r, c.rstd.r, gains.r], [c.tmp4.r])
                copy_any(dst_bf.t[:, k, :], c.tmp4.t[:, k, :], [c.tmp4.r], [dst_bf.r])
            if out_dram is not None:
                out_toks.append(P.dma(SP, out_dram, c.tmp4.t[:], osem, reads=[c.tmp4.r]))

        slot = yield
        lat_block(slot, c.qlatn, G_QN)
        slot = yield
        if isp:
            lat_block(slot, c.ckvbf, G_KVN, ckvT_p[l, :, t0:t0 + TT].rearrange("(k p) t -> p k t", p=128), sem_out_ckv)
        else:
            lat_block(slot, c.ckvbf, G_KVN, ckvT_s[l].rearrange("(k p) t -> p k t", p=128), sem_sout[1])

        if isp and l == 0:
            P.dma(SP, c.ropec.t[:], rope_p[0, :, t0:t0 + TT], sem_rc, writes=[c.ropec.r])
            P.dma(SP, c.ropes.t[:], rope_p[1, :, t0:t0 + TT], sem_rs, writes=[c.ropes.r])
        s1 = yield
        b1 = psum[0]
        for kc in range(NKC):
            mm(b1.t[0:64, 0:n], s1.t[:, kc, 0:64], c.hT.t[:, kc, :], kc == 0, kc == NKC - 1, [s1.r, c.hT.ck(kc)], [b1.r])
        tt(c.tmp4.t[0:64, 0, :], b1.t[0:64, 0:n], c.ropec.t[:, :], ALU.mult, [b1.r, c.ropec.r], [c.tmp4.r])
        s2 = yield
        b2 = psum[1]
        for kc in range(NKC):
            mm(b2.t[0:64, 0:n], s2.t[:, kc, 0:64], c.hT.t[:, kc, :], kc == 0, kc == NKC - 1, [s2.r, c.hT.ck(kc)], [b2.r])
        tt(c.tmp4.t[0:64, 1, :], b2.t[0:64, 0:n], c.ropes.t[:, :], ALU.mult, [b2.r, c.ropes.r], [c.tmp4.r])
        tt(c.tmp4.t[0:64, 2, :], c.tmp4.t[0:64, 0, :], c.tmp4.t[0:64, 1, :], ALU.add, [c.tmp4.r], [c.tmp4.r])
        if isp:
            copy_any(kpeT[l].t[:, t0:t0 + TT], c.tmp4.t[0:64, 2, :], [c.tmp4.r], [kpeT[l].r], eng=ACT)
            out_toks.append(P.dma(SP, kpeT_p[l, :, t0:t0 + TT], c.tmp4.t[0:64, 2, :], sem_out_kpe, reads=[c.tmp4.r]))
        else:
            copy_any(c.kpeS.t[:, 2048:2048 + n], c.tmp4.t[0:64, 2, :], [c.tmp4.r], [c.kpeS.r], eng=ACT)
            out_toks.append(P.dma(SP, kpeT_s[l], c.tmp4.t[0:64, 2, :], sem_sout[2], reads=[c.tmp4.r]))

        slot = yield
        kv_st = []
        if isp:
            for h in range(NH):
                bank = psum[h % 4]
                for kc in range(4):
                    mm(bank.t[:, 0:n], slot.t[:, kc, h * 128:(h + 1) * 128], c.ckvbf.t[:, kc, :], kc == 0, kc == 3,
                       [slot.r, c.ckvbf.r], [bank.r])
                sg = stage[h % len(stage)]
                copy_any(sg.t[:, :], bank.t[:, 0:n], [bank.r], [sg.r])
                kv_st.append(P.dma(SP, Kc[l, h, :, t0:t0 + TT], sg.t[:, :], sem_kvst[h % len(stage)], reads=[sg.r], writes=[r_Kc[l]]))
        else:
            for h in range(NH):
                bank = psum[h % 4]
                for kc in range(4):
                    mm(bank.t[:, 0:n], slot.t[:, kc, h * 128:(h + 1) * 128], c.ckvbf.t[:, kc, :], kc == 0, kc == 3,
                       [slot.r, c.ckvbf.r], [bank.r])
                copy_any(c.knew.t[:, h, :], bank.t[:, 0:n], [bank.r], [c.knew.r])
            cnt = 0
            for sl in range(PAST // 512):
                P.dma(POOL, slab.t[:], cache_ckvT[l, :, sl * 512:(sl + 1) * 512].rearrange("(k p) t -> p k t", p=128),
                      sem_slab, writes=[slab.r])
                for h in range(NH):
                    bank = psum[cnt % 4]
                    for kc in range(4):
                        mm(bank.t[:, 0:512], slot.t[:, kc, h * 128:(h + 1) * 128], slab.t[:, kc, :], kc == 0, kc == 3,
                           [slot.r, slab.r], [bank.r])
                    sg = stage[cnt % len(stage)]
                    copy_any(sg.t[:, :], bank.t[:, 0:512], [bank.r], [sg.r])
                    kv_st.append(P.dma(SP, Kcs[l, h, :, sl * 512:(sl + 1) * 512], sg.t[:, :], sem_kvst[cnt % len(stage)],
                                       reads=[sg.r], writes=[r_Kcs[l]]))
                    cnt += 1
        slot = yield
        cnt = 0
        if isp:
            for tb in range(TT // 128):
                for g in range(4):
                    bank = psum[cnt % 4]
                    for kc in range(4):
                        mm(bank.t[:, 0:512], c.ckvbf.t[:, kc, tb * 128:(tb + 1) * 128], slot.t[:, kc, g * 512:(g + 1) * 512],
                           kc == 0, kc == 3, [slot.r, c.ckvbf.r], [bank.r])
                    sg = stage[cnt % len(stage)]
                    copy_any(sg.t[:, :], bank.t[:, 0:512], [bank.r], [sg.r])
                    kv_st.append(P.dma(SP, Vc[l, g, t0 + tb * 128:t0 + (tb + 1) * 128, :], sg.t[:, :], sem_kvst[cnt % len(stage)],
                                       reads=[sg.r], writes=[r_Vc[l]]))
                    cnt += 1
        else:
            for g in range(4):
                bank = psum[cnt % 4]
                for kc in range(4):
                    mm(bank.t[0:n, 0:512], c.ckvbf.t[:, kc, 0:n], slot.t[:, kc, g * 512:(g + 1) * 512],
                       kc == 0, kc == 3, [slot.r, c.ckvbf.r], [bank.r])
                sg = stage[cnt % len(stage)]
                copy_any(sg.t[0:n, :], bank.t[0:n, 0:512], [bank.r], [sg.r])
                kv_st.append(P.dma(SP, Vcs[l, g, PAST:PAST + n, :], sg.t[0:n, :], sem_kvst[cnt % len(stage)],
                                   reads=[sg.r], writes=[r_Vcs[l]]))
                cnt += 1
            for sl in range(PAST // 512):
                P.dma(POOL, slab.t[:], cache_ckvT[l, :, sl * 512:(sl + 1) * 512].rearrange("(k p) t -> p k t", p=128),
                      sem_slab, writes=[slab.r])
                for tb in range(4):
                    for g in range(4):
                        bank = psum[cnt % 4]
                        for kc in range(4):
                            mm(bank.t[:, 0:512], slab.t[:, kc, tb * 128:(tb + 1) * 128], slot.t[:, kc, g * 512:(g + 1) * 512],
                               kc == 0, kc == 3, [slot.r, slab.r], [bank.r])
                        sg = stage[cnt % len(stage)]
                        copy_any(sg.t[:, :], bank.t[:, 0:512], [bank.r], [sg.r])
                        r0 = sl * 512 + tb * 128
                        kv_st.append(P.dma(SP, Vcs[l, g, r0:r0 + 128, :], sg.t[:, :], sem_kvst[cnt % len(stage)],
                                           reads=[sg.r], writes=[r_Vcs[l]]))
                        cnt += 1
        last = {}
        for tk in kv_st:
            last[id(tk.dsem)] = tk
        P.wait_all(SP, list(last.values()))

        def q_proj(wq, hh, qn_ap, qr_ap):
            c0 = hh * 256
            bq, bp, bs = psum[6], psum[7], psum[0]
            for kc in range(4):
                mm(bq.t[:, 0:n], wq.t[:, kc, c0:c0 + 128], c.qlatn.t[:, kc, :], kc == 0, kc == 3, [wq.r, c.qlatn.r], [bq.r])
            for kc in range(4):
                mm(bp.t[0:64, 0:n], wq.t[:, kc, c0 + 128:c0 + 192], c.qlatn.t[:, kc, :], kc == 0, kc == 3, [wq.r, c.qlatn.r], [bp.r])
            for kc in range(4):
                mm(bs.t[0:64, 0:n], wq.t[:, kc, c0 + 192:c0 + 256], c.qlatn.t[:, kc, :], kc == 0, kc == 3, [wq.r, c.qlatn.r], [bs.r])
            return bq, bp, bs

        hcnt = 0
        for g in range(4):
            wq = yield
            if isp:
                Lk = t0 + TT
                nkb = Lk // 128
                P.dma(SP, vbuf.t[:, 0:nkb, :], Vc[l, g, 0:Lk, :].rearrange("(kb p) c -> p kb c", p=128), sem_vld,
                      reads=[r_Vc[l]], writes=[vbuf.r])

                def kload(hh_, hc_):
                    kbuf_ = Kbuf[hc_ % 2]
                    P.dma(SP, kbuf_.t[:, 0:Lk], Kc[l, g * 4 + hh_, :, 0:Lk], sem_kv[hc_ % 2], reads=[r_Kc[l]], writes=[kbuf_.r])

                def qprep(hh_, hc_):
                    qnb_, qrb_ = c.qn[hc_ % 2], c.qr[hc_ % 2]
                    c0 = hh_ * 256
                    bq, bp = psum[6], psum[7]
                    bs = psum[2] if hc_ % 2 == 0 else psum[4]
                    for kc in range(4):
                        mm(bq.t[:, 0:n], wq.t[:, kc, c0:c0 + 128], c.qlatn.t[:, kc, :], kc == 0, kc == 3, [wq.r, c.qlatn.r], [bq.r])
                    for kc in range(4):
                        mm(bp.t[0:64, 0:n], wq.t[:, kc, c0 + 128:c0 + 192], c.qlatn.t[:, kc, :], kc == 0, kc == 3, [wq.r, c.qlatn.r], [bp.r])
                    for kc in range(4):
                        mm(bs.t[0:64, 0:n], wq.t[:, kc, c0 + 192:c0 + 256], c.qlatn.t[:, kc, :], kc == 0, kc == 3, [wq.r, c.qlatn.r], [bs.r])
                    copy_any(qnb_.t[:, :], bq.t[:, 0:n], [bq.r], [qnb_.r], eng=ACT)
                    tt(c.tmp4.t[0:64, 0, :], bp.t[0:64, 0:n], c.ropec.t[:, :], ALU.mult, [bp.r, c.ropec.r], [c.tmp4.r])
                    tt(c.tmp4.t[0:64, 1, :], bs.t[0:64, 0:n], c.ropes.t[:, :], ALU.mult, [bs.r, c.ropes.r], [c.tmp4.r])
                    tt(qrb_.t[:, :], c.tmp4.t[0:64, 0, :], c.tmp4.t[0:64, 1, :], ALU.add, [c.tmp4.r], [qrb_.r])

                kload(0, hcnt)
                qprep(0, hcnt)
                for hh in range(4):
                    h = g * 4 + hh
                    kbuf = Kbuf[hcnt % 2]
                    qnb, qrb = c.qn[hcnt % 2], c.qr[hcnt % 2]
                    bo, bsum = (psum[2], psum[3]) if hcnt % 2 == 0 else (psum[4], psum[5])
                    if hh + 1 < 4:
                        kload(hh + 1, hcnt + 1)

                    def scores(kb):
                        d = kb - it * 4
                        qs = max(d, 0) * 128
                        nn = n - qs
                        bsc = psum[kb % 2]
                        mm(bsc.t[:, 0:nn], kbuf.t[:, kb * 128:(kb + 1) * 128], qnb.t[:, qs:n], True, False,
                           [kbuf.r, qnb.r], [bsc.r])
                        mm(bsc.t[:, 0:nn], kpeT[l].t[0:64, kb * 128:(kb + 1) * 128], qrb.t[0:64, qs:n], False, True,
                           [kpeT[l].r, qrb.r], [bsc.r])

                    scores(0)
                    for kb in range(nkb):
                        if kb + 1 < nkb:
                            scores(kb + 1)
                        if kb == min(1, nkb - 1) and hh + 1 < 4:
                            qprep(hh + 1, hcnt + 1)
                        d = kb - it * 4
                        qs = max(d, 0) * 128
                        nn = n - qs
                        bsc = psum[kb % 2]
                        pb = c.pT[kb % 3]
                        act(pb.t[:, 0:nn], bsc.t[:, 0:nn], AF.Exp, [bsc.r], [pb.r], scale=ATTN_SCALE)
                        if d >= 0:
                            P.op(DVE, lambda e, pb=pb: e.memset(pb.t[64:128, 0:64], 0.0), [], [pb.r])
                        mm(bo.t[:, qs:n], vbuf.t[:, kb, hh * 128:(hh + 1) * 128], pb.t[:, 0:nn], kb == 0, kb == nkb - 1,
                           [vbuf.r, pb.r], [bo.r])
                        mm(bsum.t[:, qs:n], ones_bf.t[:, :], pb.t[:, 0:nn], kb == 0, kb == nkb - 1,
                           [ones_bf.r, pb.r], [bsum.r])
                    P.op(DVE, lambda e, bsum=bsum: e.reciprocal(out=c.recip.t[:, 0:n], in_=bsum.t[:, 0:n]), [bsum.r], [c.recip.r])
                    tt(c.oa.t[:, h, :], bo.t[:, 0:n], c.recip.t[:, 0:n], ALU.mult, [bo.r, c.recip.r], [c.oa.r])
                    hcnt += 1
            else:
                for hh in range(4):
                    bq, bp, bs = q_proj(wq, hh, None, None)
                    copy_any(c.qn4.t[:, hh, :], bq.t[:, 0:n], [bq.r], [c.qn4.r], eng=ACT)
                    tt(c.tmp4.t[0:64, 0, :], bp.t[0:64, 0:n], c.ropec.t[:, :], ALU.mult, [bp.r, c.ropec.r], [c.tmp4.r])
                    tt(c.tmp4.t[0:64, 1, :], bs.t[0:64, 0:n], c.ropes.t[:, :], ALU.mult, [bs.r, c.ropes.r], [c.tmp4.r])
                    tt(c.qr4.t[:, hh, :], c.tmp4.t[0:64, 0, :], c.tmp4.t[0:64, 1, :], ALU.add, [c.tmp4.r], [c.qr4.r])
                bo, bsum = psum[2], psum[3]
                mm(bo.t[:, 0:128], zeros_bf.t[:, 0:128], zeros_bf.t[:, 0:128], True, False, [zeros_bf.r], [bo.r])
                mm(bsum.t[:, 0:128], zeros_bf.t[:, 0:128], zeros_bf.t[:, 0:128], True, False, [zeros_bf.r], [bsum.r])
                pcnt = 0
                for half in range(2):
                    P.dma(POOL, c.kpeS.t[:, 0:2048], cache_kpeT[l, :, half * 2048:(half + 1) * 2048], sem_skpe,
                          writes=[c.kpeS.r])
                    P.dma(SP, vbuf.t[:, 0:16, :], Vcs[l, g, half * 2048:(half + 1) * 2048, :].rearrange("(kb p) c -> p kb c", p=128),
                          sem_vld, reads=[r_Vcs[l]], writes=[vbuf.r])
                    for hh in range(4):
                        h = g * 4 + hh
                        kbuf = Kbuf[hcnt % 2]
                        P.dma(SP, kbuf.t[:, 0:2048], Kcs[l, h, :, half * 2048:(half + 1) * 2048], sem_kv[hcnt % 2],
                              reads=[r_Kcs[l]], writes=[kbuf.r])
                        hcnt += 1

                        def sscores(kb_, pc_):
                            bsc_ = psum[pc_ % 2]
                            mm(bsc_.t[:, 0:n], kbuf.t[:, kb_ * 128:(kb_ + 1) * 128], c.qn4.t[:, hh, :], True, False,
                               [kbuf.r, c.qn4.r], [bsc_.r])
                            mm(bsc_.t[:, 0:n], c.kpeS.t[0:64, kb_ * 128:(kb_ + 1) * 128], c.qr4.t[0:64, hh, :], False, True,
                               [c.kpeS.r, c.qr4.r], [bsc_.r])

                        sscores(0, pcnt)
                        for kb in range(16):
                            if kb + 1 < 16:
                                sscores(kb + 1, pcnt + 1)
                            bsc = psum[pcnt % 2]
                            pb = c.pT[pcnt % 3]
                            act(pb.t[:, 0:n], bsc.t[:, 0:n], AF.Exp, [bsc.r], [pb.r], scale=ATTN_SCALE)
                            mm(bo.t[:, hh * n:(hh + 1) * n], vbuf.t[:, kb, hh * 128:(hh + 1) * 128], pb.t[:, 0:n], False, False,
                               [vbuf.r, pb.r], [bo.r])
                            mm(bsum.t[:, hh * n:(hh + 1) * n], ones_bf.t[:, :], pb.t[:, 0:n], False, False,
                               [ones_bf.r, pb.r], [bsum.r])
                            pcnt += 1
                P.dma(SP, c.vnew.t[:, :], Vcs[l, g, PAST:PAST + n, :], sem_svn, reads=[r_Vcs[l]], writes=[c.vnew.r])
                for hh in range(4):
                    h = g * 4 + hh
                    bsc = psum[pcnt % 2]
                    mm(bsc.t[0:n, 0:n], c.knew.t[:, h, :], c.qn4.t[:, hh, :], True, False, [c.knew.r, c.qn4.r], [bsc.r])
                    mm(bsc.t[0:n, 0:n], c.kpeS.t[0:64, 2048:2048 + n], c.qr4.t[0:64, hh, :], False, True,
                       [c.kpeS.r, c.qr4.r], [bsc.r])
                    pb = c.pT[pcnt % 3]
                    act(pb.t[0:n, 0:n], bsc.t[0:n, 0:n], AF.Exp, [bsc.r], [pb.r], scale=ATTN_SCALE)
                    mm(bo.t[:, hh * n:(hh + 1) * n], c.vnew.t[0:n, hh * 128:(hh + 1) * 128], pb.t[0:n, 0:n], False, False,
                       [c.vnew.r, pb.r], [bo.r])
                    mm(bsum.t[:, hh * n:(hh + 1) * n], ones_bf.t[0:n, :], pb.t[0:n, 0:n], False, False,
                       [ones_bf.r, pb.r], [bsum.r])
                    pcnt += 1
                P.op(DVE, lambda e, bsum=bsum: e.reciprocal(out=c.recip.t[:, 0:128], in_=bsum.t[:, 0:128]), [bsum.r], [c.recip.r])
                tt(c.oa.t[:, g * 4:(g + 1) * 4, :], bo.t[:, 0:128].rearrange("p (h t) -> p h t", h=4),
                   c.recip.t[:, 0:128].rearrange("p (h t) -> p h t", h=4), ALU.mult, [bo.r, c.recip.r], [c.oa.r])
        dbg("d_oa" if isp else "s_oa", c.oa)

        blk, nch, L = c.blk, c.nch, c.L
        for g in range(4):
            if not isp:
                P.dma(SP, c.Sg.t[:], state_s[l, g * 4:(g + 1) * 4].rearrange("h k v -> k h v"), sem_sS, writes=c.Sg_res)
            slot = yield
            if isp and g == 0:
                if it == 0:
                    P.op(DVE, lambda e: e.memset(Sbuf.t[:], 0.0), [], Sres)
                else:
                    P.dma(SP, Sbuf.t[:], Sc[l].rearrange("p (h v) -> p h v", h=NH), sem_Sld, reads=[r_Sc[l]], writes=Sres)
            if isp and l > 0:
                cs_ = slice(g * 512, (g + 1) * 512)
                P.dma(SP, lbt.t[:], hg_lb[l, cs_].partition_broadcast(128), sem_lb, writes=[lbt.r])
                P.dma(SP, omlt.t[:], hg_lb[0, cs_].partition_broadcast(128), sem_oml, writes=[omlt.r])
                tt(lbt.t[:], lbt.t[:], omlt.t[:], ALU.subtract, [lbt.r, omlt.r], [lbt.r])
                act(lbt.t[:], lbt.t[:], AF.Sigmoid, [lbt.r], [lbt.r])
                ts(omlt.t[:], lbt.t[:], -1.0, 1.0, ALU.mult, ALU.add, [lbt.r], [omlt.r])

            def cons_q(j, bank):
                act(c.hqT.t[:, j, :], bank.t[:, 0:n], AF.Silu, [bank.r], [c.hqT.r])
            gemm_fm(slot, 4, c.hT, NKC, n, cons_q)
            slot = yield

            def cons_f(tb, bank):
                lf = c.logf.t[:, tb, :]
                act(lf, bank.t[0:blk, 0:512], AF.Sigmoid, [bank.r], [c.logf.r])
                if l > 0:
                    tt(lf, lf, omlt.t[0:blk, :], ALU.mult, [c.logf.r, omlt.r], [c.logf.r])
                    tt(lf, lf, lbt.t[0:blk, :], ALU.add, [c.logf.r, lbt.r], [c.logf.r])
                ts(c.kkb.t[:, tb, :], lf, -1.0, 1.0, ALU.mult, ALU.add, [c.logf.r], [c.kkb.r])
                act(lf, lf, AF.Ln, [c.logf.r], [c.logf.r])
            gemm_tm(slot, c, cons_f)
            slot = yield

            def cons_v(tb, bank):
                copy_any(c.vtb.t[:, tb, :], bank.t[0:blk, 0:512], [bank.r], [c.vtb.r])
            gemm_tm(slot, c, cons_v)
            slot = yield

            def cons_g(j, bank):
                act(c.hgT.t[:, j, :], bank.t[:, 0:n], AF.Silu, [bank.r], [c.hgT.r])
            gemm_fm(slot, 4, c.hT, NKC, n, cons_g)
            bcnt = 0
            W4 = 4 * blk
            for tb in range(c.nblk):
                if isp:
                    bO = psum[4 + (tb % 2)]
                    mm(bO.t[:, 0:512], zeros_bf.t[:, 0:128], zeros_bf.t[:, 0:512], True, False, [zeros_bf.r], [bO.r])
                    tsl = slice(tb * 128, (tb + 1) * 128)
                    lf = c.logf.t[:, tb, :]
                    bBm, bEk, bSm, bD0, bD1 = psum[0], psum[1], psum[2], psum[3], psum[6]
                    Sg = Sbuf.t[:, g * 4:(g + 1) * 4, :]
                    Sg_res = [Sres[g * 4 + i] for i in range(4)]
                    eBs4 = c.eBs.t[:, :].rearrange("p (h c t) -> p h c t", h=4, c=2)
                    HS = [slice(i * 128, (i + 1) * 128) for i in range(4)]
                    for hh in range(4):
                        mm(bBm.t[:, HS[hh]], lf[:, HS[hh]], c.m1.t[:, 0:128], True, True, [c.logf.r, c.m1.r], [bBm.r])
                    for hh in range(4):
                        mm(bSm.t[:, hh * 4:(hh + 1) * 4], lf[:, HS[hh]], c.m1.t[:, 128:132], True, True, [c.logf.r, c.m1.r], [bSm.r])
                    mm(bEk.t[:, 0:512], c.m2.t[:, :], lf[:, 0:512], True, True, [c.logf.r, c.m2.r], [bEk.r])
                    act(c.bufA.t[:, :], bBm.t[:, 0:512], AF.Exp, [bBm.r], [c.bufA.r])
                    act(c.eBs.t[:, :], bSm.t[:, 0:16], AF.Exp, [bSm.r], [c.eBs.r])
                    act(c.bufB.t[:, :], bEk.t[:, 0:512], AF.Exp, [bEk.r], [c.bufB.r])
                    tt(c.eL.t[:, :, :], eBs4[:, :, :, 0], eBs4[:, :, :, 1], ALU.mult, [c.eBs.r], [c.eL.r])
                    tt(c.qtil4.t[:, :, :], c.hqT.t[:, :, tsl], c.bufA.t[:, :].rearrange("p (h t) -> p h t", h=4), ALU.mult,
                       [c.hqT.r, c.bufA.r], [c.qtil4.r])
                    tt(c.khat4.t[:, :], c.kkb.t[:, tb, :], c.bufB.t[:, :], ALU.mult, [c.kkb.r, c.bufB.r], [c.khat4.r])
                    for hh in range(4):
                        mm(bEk.t[:, HS[hh]], c.khat4.t[:, HS[hh]], ident.t[:, :], True, True, [c.khat4.r, ident.r], [bEk.r])
                    copy_any(c.bufA.t[:, :], bEk.t[:, 0:512], [bEk.r], [c.bufA.r], eng=ACT)
                    for hh in range(4):
                        mm(bBm.t[:, HS[hh]], c.bufA.t[:, HS[hh]], c.qtil4.t[:, hh, :], True, True, [c.bufA.r, c.qtil4.r], [bBm.r])
                    tt(c.bufB.t[:, :].rearrange("p (h t) -> p h t", h=4), bBm.t[:, 0:512].rearrange("p (h t) -> p h t", h=4),
                       c.mask.t[:, :].unsqueeze(1).broadcast_to([128, 4, 128]), ALU.mult, [bBm.r, c.mask.r], [c.bufB.r])
                    for hh in range(4):
                        mm(bO.t[:, HS[hh]], c.vtb.t[:, tb, HS[hh]], c.bufB.t[:, HS[hh]], False, False, [c.vtb.r, c.bufB.r], [bO.r])
                    for hh in range(4):
                        mm(bD0.t[:, HS[hh]], c.khat4.t[0:64, HS[hh]], c.vtb.t[0:64, tb, HS[hh]], True, True,
                           [c.khat4.r, c.vtb.r], [bD0.r])
                        mm(bD1.t[:, HS[hh]], c.khat4.t[64:128, HS[hh]], c.vtb.t[64:128, tb, HS[hh]], True, True,
                           [c.khat4.r, c.vtb.r], [bD1.r])
                    for ch in range(2):
                        bD = bD0 if ch == 0 else bD1
                        em = eBs4[:, :, ch, 0:1].broadcast_to([128, 4, 128])
                        elm = eBs4[:, :, ch, 1:2].broadcast_to([128, 4, 128])
                        el = c.eL.t[:, :, ch:ch + 1].broadcast_to([128, 4, 128])
                        tt(c.Smb4.t[:, :, :], Sg, em, ALU.mult, Sg_res + [c.eBs.r], [c.Smb4.r])
                        tt(c.Sd4.t[:, :, :], Sg, el, ALU.mult, Sg_res + [c.eL.r], [c.Sd4.r])
                        for hh in range(4):
                            mm(bO.t[:, hh * 128 + ch * 64: hh * 128 + (ch + 1) * 64], c.Smb4.t[:, hh, :],
                               c.qtil4.t[:, hh, ch * 64:(ch + 1) * 64], False, False, [c.Smb4.r, c.qtil4.r], [bO.r])
                        tt(Sg, bD.t[:, 0:512].rearrange("p (h v) -> p h v", h=4), elm, ALU.mult, [bD.r, c.eBs.r], Sg_res)
                        tt(Sg, Sg, c.Sd4.t[:, :, :], ALU.add, Sg_res + [c.Sd4.r], Sg_res)
                else:
                    bO = psum[4 + (tb % 2)]
                    mm(bO.t[:, 0:W4], zeros_bf.t[:, 0:128], zeros_bf.t[:, 0:W4], True, False, [zeros_bf.r], [bO.r])
                    for hh in range(4):
                        h = g * 4 + hh
                        i2 = bcnt % 2
                        cs = slice(hh * 128, (hh + 1) * 128)
                        tsl = slice(tb * blk, (tb + 1) * blk)
                        ocs = slice(hh * blk, (hh + 1) * blk)
                        bB, bE, bK, bA = psum[0], psum[1], psum[2], psum[3]
                        bDs = (psum[6], psum[7])
                        eBb, eEb, qtb, khb, ktb, atb = c.expB[i2], c.expE[i2], c.qtil[i2], c.khat[i2], c.ktil[i2], c.ATm[i2]
                        nb2 = blk + 2 * nch
                        mm(bB.t[:, 0:nb2], c.logf.t[:, tb, cs], c.m1.t[:, :], True, True, [c.logf.r, c.m1.r], [bB.r])
                        mm(bE.t[0:blk, 0:128], c.m2.t[:, :], c.logf.t[:, tb, cs], True, True, [c.logf.r, c.m2.r], [bE.r])
                        act(eBb.t[:, :], bB.t[:, 0:nb2], AF.Exp, [bB.r], [eBb.r])
                        act(eEb.t[:, :], bE.t[0:blk, 0:128], AF.Exp, [bE.r], [eEb.r])
                        tt(qtb.t[:, :], c.hqT.t[:, hh, tsl], eBb.t[:, 0:blk], ALU.mult, [c.hqT.r, eBb.r], [qtb.r])
                        tt(khb.t[:, :], c.kkb.t[:, tb, cs], eEb.t[:, :], ALU.mult, [c.kkb.r, eEb.r], [khb.r])
                        mm(bK.t[:, 0:blk], khb.t[:, :], ident.t[0:blk, 0:blk], True, True, [khb.r, ident.r], [bK.r])
                        copy_any(ktb.t[:, :], bK.t[:, 0:blk], [bK.r], [ktb.r], eng=ACT)
                        mm(bA.t[0:blk, 0:blk], ktb.t[:, :], qtb.t[:, :], True, True, [ktb.r, qtb.r], [bA.r])
                        tt(atb.t[:, :], bA.t[0:blk, 0:blk], c.mask.t[:, :], ALU.mult, [bA.r, c.mask.r], [atb.r])
                        mm(bO.t[:, ocs], c.vtb.t[:, tb, cs], atb.t[:, :], False, False, [c.vtb.r, atb.r], [bO.r])
                        for ch in range(nch):
                            mm(bDs[ch].t[:, 0:128], khb.t[ch * L:(ch + 1) * L, :], c.vtb.t[ch * L:(ch + 1) * L, tb, cs], True, True,
                               [khb.r, c.vtb.r], [bDs[ch].r])
                        for ch in range(nch):
                            cm, cl = blk + 2 * ch, blk + 2 * ch + 1
                            eB = eBb.t
                            Sap, Sr = c.S_ap(h), c.S_res(h)
                            ts(c.Smb[ch].t[:, :], Sap, eB[:, cm:cm + 1], None, ALU.mult, None, [Sr, eBb.r], [c.Smb[ch].r])
                            ts(c.Sd[ch].t[:, :], Sap, eB[:, cm:cm + 1], eB[:, cl:cl + 1], ALU.mult, ALU.mult, [Sr, eBb.r], [c.Sd[ch].r])
                            mm(bO.t[:, hh * blk + ch * L: hh * blk + (ch + 1) * L], c.Smb[ch].t[:, :],
                               qtb.t[:, ch * L:(ch + 1) * L], False, False, [c.Smb[ch].r, qtb.r], [bO.r])
                            stt(Sap, bDs[ch].t[:, 0:128], eB[:, cl:cl + 1], c.Sd[ch].t[:, :], ALU.mult, ALU.add,
                                [bDs[ch].r, eBb.r, c.Sd[ch].r], [Sr])
                        bcnt += 1
                s0 = c.sq[0]
                bN = psum[2]
                act(s0.t[:, 0:W4], bO.t[:, 0:W4], AF.Square, [bO.r], [s0.r])
                mm(bN.t[:, 0:W4], ones_bf.t[:, :], s0.t[:, 0:W4], True, True, [s0.r, ones_bf.r], [bN.r])
                act(c.ntmp.t[:, 0:W4], bN.t[:, 0:W4], AF.Sqrt, [bN.r], [c.ntmp.r], bias=float(EPS), scale=1.0 / 128)
                P.op(DVE, lambda e: e.reciprocal(out=c.rstd.t[:, 0:W4], in_=c.ntmp.t[:, 0:W4]), [c.ntmp.r], [c.rstd.r])
                stt(c.ntmp.t[:, 0:W4], bO.t[:, 0:W4], G[:, l, G_HGN:G_HGN + 1], c.rstd.t[:, 0:W4], ALU.mult, ALU.mult,
                    [bO.r, c.rstd.r, gains.r], [c.ntmp.r])
                tt(c.oh.t[:, g * 4:(g + 1) * 4, tb * blk:(tb + 1) * blk],
                   c.ntmp.t[:, 0:W4].rearrange("p (h t) -> p h t", h=4),
                   c.hgT.t[:, :, tb * blk:(tb + 1) * blk], ALU.mult, [c.ntmp.r, c.hgT.r], [c.oh.r])
            if not isp:
                out_toks.append(P.dma(SP, st_s[l, g * 4:(g + 1) * 4].rearrange("h k v -> k h v"), c.Sg.t[:], sem_sout[3],
                                      reads=c.Sg_res))
        if isp:
            if it == n_tiles - 1:
                out_toks.append(P.dma(SP, st_p[l].rearrange("h k v -> k h v"), Sbuf.t[:], sem_out_st, reads=Sres))
            else:
                P.dma(SP, Sc[l].rearrange("p (h v) -> p h v", h=NH), Sbuf.t[:], sem_Sst[l], reads=Sres, writes=[r_Sc[l]])
        dbg("d_oh" if isp else "s_oh", c.oh)

        for g in range(4):
            slot = yield

            def cons_a(j, bank, g=g):
                act(c.sga.t[:, g * 4 + j, :], bank.t[:, 0:n], AF.Sigmoid, [bank.r], [c.sga.r])
            gemm_fm(slot, 4, c.hT, NKC, n, cons_a)
        for g in range(4):
            slot = yield

            def cons_b(j, bank, g=g):
                act(c.sgb.t[:, g * 4 + j, :], bank.t[:, 0:n], AF.Sigmoid, [bank.r], [c.sgb.r])
            gemm_fm(slot, 4, c.hT, NKC, n, cons_b)
        for g in range(4):
            slot = yield

            def cons_oa(j, bank, g=g):
                m = g * 4 + j
                tt(c.sga.t[:, m, :], bank.t[:, 0:n], c.sga.t[:, m, :], ALU.mult, [bank.r, c.sga.r], [c.sga.r])
            gemm_fm(slot, 4, c.oa, NKC, n, cons_oa)
            slot = yield

            def cons_ob(j, bank, g=g):
                m = g * 4 + j
                tt(c.sgb.t[:, m, :], bank.t[:, 0:n], c.sgb.t[:, m, :], ALU.mult, [bank.r, c.sgb.r], [c.sgb.r])
                tt(c.hT.t[:, m, :], c.sga.t[:, m, :], c.sgb.t[:, m, :], ALU.add, [c.sga.r, c.sgb.r], [c.hT.ck(m)])
            gemm_fm(slot, 4, c.oh, NKC, n, cons_ob, banks=(4, 5, 6, 7))
        dbg("d_gated" if isp else "s_gated", c.hT)

        for g in range(4):
            slot = yield

            def cons_o(j, bank, g=g):
                copy_any(c.mixT.t[:, g * 4 + j, :], bank.t[:, 0:n], [bank.r], [c.mixT.r])
            gemm_fm(slot, 4, c.hT, NKC, n, cons_o)

        def post_norm_residual(gcol):
            rms_rstd(c, [c.mixT.t[:, k, :] for k in range(NKC)], n, D_MODEL, [c.mixT.r])
            for k in range(NKC):
                stt(c.mixT.t[:, k, :], c.mixT.t[:, k, :], G[:, l, gcol + k:gcol + k + 1], c.rstd.t[:, 0:n], ALU.mult, ALU.mult,
                    [c.mixT.r, c.rstd.r, gains.r], [c.mixT.r])
                tt(c.xT.t[:, k, :], c.xT.t[:, k, :], c.mixT.t[:, k, :], ALU.add, [c.xT.ck(k), c.mixT.r], [c.xT.ck(k)])
        if isp:
            dbg("d_mix", c.mixT)
        post_norm_residual(G_POST)
        dbg("d_xm" if isp else "s_xm", c.xT)

        rms_rstd(c, [c.xT.t[:, k, :] for k in range(NKC)], n, D_MODEL, c.xT.all())
        for k in range(NKC):
            stt(c.hT.t[:, k, :], c.xT.t[:, k, :], G[:, l, G_PREM + k:G_PREM + k + 1], c.rstd.t[:, 0:n], ALU.mult, ALU.mult,
                [c.xT.ck(k), c.rstd.r, gains.r], [c.hT.ck(k)])
        for q in range(4):
            for g in range(4):
                slot = yield

                def cons_u(j, bank, g=g):
                    k = g * 4 + j
                    act(c.ntmp.t[:, 0:n], bank.t[:, 0:n], AF.Relu, [bank.r], [c.ntmp.r])
                    tt(c.oa.t[:, k, :], c.ntmp.t[:, 0:n], c.ntmp.t[:, 0:n], ALU.mult, [c.ntmp.r], [c.oa.r])
                gemm_fm(slot, 4, c.hT, NKC, n, cons_u)
            for g in range(4):
                slot = yield

                def cons_d(j, bank, g=g, q=q):
                    m = g * 4 + j
                    if q == 0:
                        copy_any(c.mixT.t[:, m, :], bank.t[:, 0:n], [bank.r], [c.mixT.r])
                    else:
                        tt(c.mixT.t[:, m, :], bank.t[:, 0:n], c.mixT.t[:, m, :], ALU.add, [bank.r, c.mixT.r], [c.mixT.r])
                gemm_fm(slot, 4, c.oa, NKC, n, cons_d, banks=(4, 5, 6, 7))
        post_norm_residual(G_POSTM)

    def run_layer(l, it, ctxs):
        gens = [layer_gen(c, l, it) for c in ctxs]
        for g_ in gens:
            next(g_)
        nb = 0
        while True:
            slot = ws.get()
            nb += 1
            done = 0
            for g_ in gens:
                try:
                    g_.send(slot)
                except StopIteration:
                    done += 1
            if done:
                assert done == len(gens)
                break
        assert nb == nblocks_layer, (nb, nblocks_layer)

    for it in range(n_tiles):
        t0 = it * TT
        P.dma(SP, pc.xT.t[:], xT_p[:, t0:t0 + TT].rearrange("(k p) t -> p k t", p=128), sem_x, writes=pc.xT.all())
        ctxs = [pc]
        if with_sample and it == 0:
            P.dma(SP, sc_.xT.t[:], xT_s.rearrange("(k p) t -> p k t", p=128), sem_sx, writes=sc_.xT.all())
            ctxs.append(sc_)
        for l in range(n_layers):
            run_layer(l, it, ctxs)
        out_toks.append(P.dma(SP, yT_p[:, t0:t0 + TT].rearrange("(k p) t -> p k t", p=128), pc.xT.t[:], sem_out_y, reads=pc.xT.all()))
        if with_sample and it == 0:
            out_toks.append(P.dma(SP, yT_s.rearrange("(k p) t -> p k t", p=128), sc_.xT.t[:], sem_sout[0], reads=sc_.xT.all()))

    last = {}
    for tk in out_toks:
        last[id(tk.dsem)] = tk
    P.wait_all(SP, list(last.values()))
    stats = P.emit(esems)
    st.close()
    stats["sbuf_used"] = sbuf_used
    return nc, stats


def _rope_tables(pos):
    half = QK_ROPE // 2
    inv = (np.float32(10000.0) ** (-np.arange(half, dtype=np.float32) / np.float32(half))).astype(np.float32)
    ang = pos.astype(np.float32)[:, None] * inv[None, :]
    cos = np.cos(ang).astype(np.float32).T
    sin = np.sin(ang).astype(np.float32).T
    cos2 = np.concatenate([cos, cos], axis=0)
    sin2 = np.concatenate([-sin, sin], axis=0)
    return np.ascontiguousarray(np.stack([cos2, sin2]))


def _hgrn_consts():
    idx = np.arange(128)
    ch = idx // 64
    loc = idx % 64
    same = ch[:, None] == ch[None, :]
    m1 = np.zeros((128, 132), np.float32)
    le = (loc[:, None] <= loc[None, :]) & same
    mid = (loc[:, None] <= 31) & same
    m1[:, :128] = le.astype(np.float32) - mid.astype(np.float32)
    for c in range(2):
        m1[:, 128 + 2 * c] = ((ch == c) & (loc <= 31)).astype(np.float32)
        m1[:, 129 + 2 * c] = ((ch == c) & (loc > 31)).astype(np.float32)
    m2 = (mid.astype(np.float32) - le.astype(np.float32)).astype(np.float32)
    mask = ((loc[:, None] <= loc[None, :]) & same).astype(np.float32)
    return m1, m2, mask, np.eye(128, dtype=np.float32)


def _hgrn_consts_sample():
    i = np.arange(32)
    le = (i[:, None] <= i[None, :])
    mid = (i[:, None] <= 15) & np.ones((32, 32), bool)
    m1 = np.zeros((32, 34), np.float32)
    m1[:, :32] = le.astype(np.float32) - mid.astype(np.float32)
    m1[:, 32] = (i <= 15).astype(np.float32)
    m1[:, 33] = (i > 15).astype(np.float32)
    m2 = (mid.astype(np.float32) - le.astype(np.float32)).astype(np.float32)
    mask = le.astype(np.float32)
    return m1, m2, mask


_PROG_CACHE = {}


def _get_program(n_tiles, n_layers):
    key = (n_tiles, n_layers)
    if key not in _PROG_CACHE:
        _PROG_CACHE[key] = build_program(n_tiles, n_layers)
    return _PROG_CACHE[key]


def _col(v, n):
    return np.ascontiguousarray(v.reshape(n, 128).T)


def prepare_inputs(inp):
    f = lambda a: np.ascontiguousarray(np.asarray(a, dtype=np.float32))
    w_in = f(inp["w_in"])
    kpe = w_in[:, :, C_KPE:C_KPE + 64]
    w_in_ext = np.concatenate([w_in, kpe[:, :, 32:64], kpe[:, :, 0:32]], axis=2)
    wuq = f(inp["w_uq"]).reshape(DEPTH, Q_LORA, NH, 192)
    wuq_ext = np.concatenate([wuq, wuq[..., 160:192], wuq[..., 128:160]], axis=3).reshape(DEPTH, Q_LORA, NH * 256)
    gains = np.zeros((DEPTH, 128, NG), np.float32)
    for l in range(DEPTH):
        gains[l, :, G_PRE:G_PRE + 16] = _col(f(inp["pre_mix_g"])[l], 16)
        gains[l, :, G_POST:G_POST + 16] = _col(f(inp["post_mix_g"])[l], 16)
        gains[l, :, G_PREM:G_PREM + 16] = _col(f(inp["pre_mlp_g"])[l], 16)
        gains[l, :, G_POSTM:G_POSTM + 16] = _col(f(inp["post_mlp_g"])[l], 16)
        gains[l, :, G_QN:G_QN + 4] = _col(f(inp["q_norm_g"])[l], 4)
        gains[l, :, G_KVN:G_KVN + 4] = _col(f(inp["kv_norm_g"])[l], 4)
        gains[l, :, G_HGN] = f(inp["hg_norm_g"])[l]
    m1, m2, mask, ident = _hgrn_consts()
    m1s, m2s, masks = _hgrn_consts_sample()
    shared = {
        "rope_p": _rope_tables(np.arange(SEQ)),
        "rope_s": _rope_tables(PAST + np.arange(DEC_SEQ)),
        "cm1": m1, "cm2": m2, "cmask": mask, "cident": ident,
        "cm1s": m1s, "cm2s": m2s, "cmasks": masks,
        "gains": gains,
        "hg_lb": f(inp["hg_lb"]),
        "w_in": np.ascontiguousarray(w_in_ext),
        "w_uq": np.ascontiguousarray(wuq_ext),
        "w_uk": f(inp["w_uk"]).reshape(DEPTH, KV_LORA, 2048),
        "w_uv": f(inp["w_uv"]).reshape(DEPTH, KV_LORA, 2048),
        "w_oa": f(inp["w_oa"]), "w_ob": f(inp["w_ob"]), "w_out": f(inp["w_out"]),
        "w_up": f(inp["w_up"]), "w_down": f(inp["w_down"]),
    }
    xp = f(inp["x_prompt"])
    xs = f(inp["x_sample"])
    cckv = f(inp["cache_ckv"])
    ckpe = f(inp["cache_kpe"])
    sth = f(inp["state_hgrn"])
    in_maps = []
    for c in range(8):
        m = dict(shared)
        m["xT_p"] = np.ascontiguousarray(xp[c % 4].T)
        m["xT_s"] = np.ascontiguousarray(xs[c].T)
        m["cache_ckvT"] = np.ascontiguousarray(cckv[:, c].transpose(0, 2, 1))
        m["cache_kpeT"] = np.ascontiguousarray(ckpe[:, c].transpose(0, 2, 1))
        m["state_s"] = np.ascontiguousarray(sth[:, c])
        in_maps.append(m)
    return in_maps


def kernel(**inp):
    nc, _ = _get_program(4, DEPTH)
    in_maps = prepare_inputs(inp)
    res = run_bass_kernel_spmd(nc, in_maps, core_ids=list(range(8)))
    R = res.results
    B = 4
    y_p = np.stack([R[b]["yT_p"].T for b in range(B)])
    ckv_p = np.stack([np.stack([R[b]["ckvT_p"][l].T for b in range(B)]) for l in range(DEPTH)])
    kpe_p = np.stack([np.stack([R[b]["kpeT_p"][l].T for b in range(B)]) for l in range(DEPTH)])
    st_p = np.stack([np.stack([R[b]["st_p"][l] for b in range(B)]) for l in range(DEPTH)])
    y_s = np.stack([R[c]["yT_s"].T for c in range(8)])
    ckv_s = np.stack([np.stack([R[c]["ckvT_s"][l].T for c in range(8)]) for l in range(DEPTH)])
    kpe_s = np.stack([np.stack([R[c]["kpeT_s"][l].T for c in range(8)]) for l in range(DEPTH)])
    st_s = np.stack([np.stack([R[c]["st_s"][l] for c in range(8)]) for l in range(DEPTH)])
    return (np.ascontiguousarray(y_p), np.ascontiguousarray(y_s), np.ascontiguousarray(ckv_p),
            np.ascontiguousarray(kpe_p), np.ascontiguousarray(st_p), np.ascontiguousarray(ckv_s),
            np.ascontiguousarray(kpe_s), np.ascontiguousarray(st_s))
```
